# Optimizing a Trainium2 kernel written in Bass

```python
import jax, jax.numpy as jnp
from jax import lax
import numpy as np

D_MODEL = 1024
BATCH = 8
SEQ = 2048
DEPTH = 2
DEC_BATCH = 128
DEC_SEQ = 8
PAST_LEN = 16384
PAGE_SIZE = 128

MIX = D_MODEL
W_A = MIX // 4
W_B = MIX // 4
W_C = MIX // 4
W_D = MIX - W_A - W_B - W_C
K_A = 3
H_B = 4
DV = W_B // H_B
DK = DV // 2
GATE_RANK = 16
GATE_TAU = 16.0
GLA_CHUNK = 16
K_C = 31
H_D = 4
D_CHUNK = 128
D_FF = 4 * D_MODEL
ALPHA = (2 * DEPTH) ** 0.25
BETA = (8 * DEPTH) ** -0.25
LN_EPS = 1e-5
SEG = (3 * W_A, H_B * DK, H_B * DK, W_B, W_B, GATE_RANK, 2 * W_C, 2 * W_D)
IN_COLS = int(sum(SEG))
SPLIT_AT = tuple(int(s) for s in np.cumsum(SEG)[:-1])

kernel_name = 'hymba_style_conv_gla_conformer_gmlp_step'


def layer_norm(x, g, b):
    xf = x.astype(jnp.float32)
    mu = jnp.mean(xf, -1, keepdims=True)
    var = jnp.mean(jnp.square(xf - mu), -1, keepdims=True)
    return ((xf - mu) * lax.rsqrt(var + LN_EPS)).astype(x.dtype) * g + b


def causal_dwconv(xp, w):
    return lax.conv_general_dilated(xp, w[:, None, :], window_strides=(1,), padding='VALID',
                                    dimension_numbers=('NWC', 'WIO', 'NWC'),
                                    feature_group_count=xp.shape[-1])


def gla(q, k, v, log_a, s0):
    bsz, L, H, dk = q.shape
    dv = v.shape[-1]
    c = GLA_CHUNK if L % GLA_CHUNK == 0 else L
    n = L // c
    f32 = jnp.float32
    q = q.astype(f32).reshape(bsz, n, c, H, dk)
    k = k.astype(f32).reshape(bsz, n, c, H, dk)
    v = v.astype(f32).reshape(bsz, n, c, H, dv)
    b = jnp.cumsum(log_a.astype(f32).reshape(bsz, n, c, H, dk), axis=2)
    mask = jnp.tril(jnp.ones((c, c), bool))
    diff = b[:, :, :, None] - b[:, :, None, :]
    decay = jnp.exp(jnp.where(mask[None, None, :, :, None, None], diff, -jnp.inf))
    att = jnp.einsum('bnthd,bnshd,bntshd->bnhts', q, k, decay)
    o_intra = jnp.einsum('bnhts,bnshv->bnthv', att, v)
    b_last = b[:, :, -1]
    ds = jnp.einsum('bnshd,bnshv->bnhdv', k * jnp.exp(b_last[:, :, None] - b), v)

    def step(S, inp):
        dec, dS = inp
        return dec[..., None] * S + dS, S

    s_fin, s_starts = lax.scan(step, s0.astype(f32),
                               (jnp.moveaxis(jnp.exp(b_last), 1, 0), jnp.moveaxis(ds, 1, 0)))
    s_starts = jnp.moveaxis(s_starts, 0, 1)
    o_inter = jnp.einsum('bnthd,bnhdv->bnthv', q * jnp.exp(b), s_starts)
    o = (o_intra + o_inter).reshape(bsz, L, H, dv)
    return o, s_fin.astype(s0.dtype)


def token_mixers(x, hist_a, s0, hist_c, w_in, a_conv_w, b_gate_w2, b_gate_b, b_norm_g,
                 c_conv_w, c_conv_b, c_ln_g, c_ln_b, d_ln_g, d_ln_b, d_ws, d_bs, w_out):
    bsz, L, _ = x.shape
    z = x @ w_in
    z_a, z_q, z_k, z_v, z_g, z_lr, z_c, z_d = jnp.split(z, SPLIT_AT, axis=-1)
    a_b, a_c, a_h = jnp.split(z_a, 3, axis=-1)
    a_full = jnp.concatenate([hist_a.astype(x.dtype), a_c * a_h], axis=1)
    y_a = a_b * causal_dwconv(a_full, a_conv_w)
    new_a = a_full[:, -(K_A - 1):]
    q = z_q.reshape(bsz, L, H_B, DK) * (DK ** -0.5)
    k = z_k.reshape(bsz, L, H_B, DK)
    v = z_v.reshape(bsz, L, H_B, DV)
    log_a = jax.nn.log_sigmoid((z_lr @ b_gate_w2 + b_gate_b).astype(jnp.float32)) / GATE_TAU
    o, new_s = gla(q, k, v, log_a.reshape(bsz, L, H_B, DK), s0)
    o = o * lax.rsqrt(jnp.mean(jnp.square(o), -1, keepdims=True) + LN_EPS)
    y_b = o.reshape(bsz, L, W_B).astype(x.dtype) * b_norm_g * jax.nn.silu(z_g)
    c_a, c_g = jnp.split(z_c, 2, axis=-1)
    c_full = jnp.concatenate([hist_c.astype(x.dtype), c_a * jax.nn.sigmoid(c_g)], axis=1)
    y_c = jax.nn.silu(layer_norm(causal_dwconv(c_full, c_conv_w) + c_conv_b, c_ln_g, c_ln_b))
    new_c = c_full[:, -(K_C - 1):]
    d = jax.nn.gelu(z_d)
    u, vv = d[..., :W_D], d[..., W_D:]
    vv = layer_norm(vv, d_ln_g, d_ln_b)
    c = min(D_CHUNK, L)
    n = L // c
    ws = jnp.tril(d_ws[:, :c, :c])
    vg = vv.reshape(bsz, n, c, H_D, W_D // H_D)
    f = jnp.einsum('gts,bnsgc->bntgc', ws, vg) + d_bs[:, :c].T[None, None, :, :, None]
    y_d = u * f.reshape(bsz, L, W_D)
    y = jnp.concatenate([y_a, y_b, y_c, y_d], axis=-1) @ w_out
    return y, new_a, new_s, new_c, vv


def decoder_layer(x, hist_a, s0, hist_c, p):
    (w_in, a_conv_w, b_gate_w2, b_gate_b, b_norm_g, c_conv_w, c_conv_b, c_ln_g, c_ln_b,
     d_ln_g, d_ln_b, d_ws, d_bs, w_out, ln1_g, ln1_b, w_ff1, w_ff2, ln2_g, ln2_b) = p
    mix, new_a, new_s, new_c, v_rows = token_mixers(
        x, hist_a, s0, hist_c, w_in, a_conv_w, b_gate_w2, b_gate_b, b_norm_g,
        c_conv_w, c_conv_b, c_ln_g, c_ln_b, d_ln_g, d_ln_b, d_ws, d_bs, w_out)
    x = layer_norm(ALPHA * x + mix, ln1_g, ln1_b)
    h = jnp.square(jax.nn.relu(x @ w_ff1)) @ w_ff2
    x = layer_norm(ALPHA * x + h, ln2_g, ln2_b)
    return x, new_a, new_s, new_c, v_rows


def setup_inputs(seed: int = 0) -> dict:
    key = jax.random.key(seed)
    ks = jax.random.split(key, 32)

    def nrm(k, shape, s):
        return jax.random.normal(k, shape, jnp.float32) * s

    return {
        'x_prompt': nrm(ks[0], (BATCH, SEQ, D_MODEL), 1.0),
        'x_sample': nrm(ks[1], (DEC_BATCH, DEC_SEQ, D_MODEL), 1.0),
        'state_conv_a': nrm(ks[2], (DEPTH, DEC_BATCH, K_A - 1, W_A), 1.0),
        'state_gla': nrm(ks[3], (DEPTH, DEC_BATCH, H_B, DK, DV), 0.5),
        'state_conv_c': nrm(ks[4], (DEPTH, DEC_BATCH, K_C - 1, W_C), 0.5),
        'w_in': nrm(ks[5], (DEPTH, D_MODEL, IN_COLS), D_MODEL ** -0.5),
        'a_conv_w': nrm(ks[6], (DEPTH, K_A, W_A), K_A ** -0.5),
        'b_gate_w2': nrm(ks[7], (DEPTH, GATE_RANK, H_B * DK), GATE_RANK ** -0.5),
        'b_gate_b': nrm(ks[8], (DEPTH, H_B * DK), 0.01),
        'b_norm_g': 1.0 + nrm(ks[9], (DEPTH, W_B), 0.01),
        'c_conv_w': nrm(ks[10], (DEPTH, K_C, W_C), K_C ** -0.5),
        'c_conv_b': nrm(ks[11], (DEPTH, W_C), 0.01),
        'c_ln_g': 1.0 + nrm(ks[12], (DEPTH, W_C), 0.01),
        'c_ln_b': nrm(ks[13], (DEPTH, W_C), 0.01),
        'd_ln_g': 1.0 + nrm(ks[14], (DEPTH, W_D), 0.01),
        'd_ln_b': nrm(ks[15], (DEPTH, W_D), 0.01),
        'd_ws': nrm(ks[16], (DEPTH, H_D, D_CHUNK, D_CHUNK), D_CHUNK ** -0.5),
        'd_bs': 1.0 + nrm(ks[17], (DEPTH, H_D, D_CHUNK), 0.01),
        'w_out': nrm(ks[18], (DEPTH, MIX, D_MODEL), BETA * MIX ** -0.5),
        'ln1_g': 1.0 + nrm(ks[19], (DEPTH, D_MODEL), 0.01),
        'ln1_b': nrm(ks[20], (DEPTH, D_MODEL), 0.01),
        'w_ff1': nrm(ks[21], (DEPTH, D_MODEL, D_FF), D_MODEL ** -0.5),
        'w_ff2': nrm(ks[22], (DEPTH, D_FF, D_MODEL), BETA * D_FF ** -0.5),
        'ln2_g': 1.0 + nrm(ks[23], (DEPTH, D_MODEL), 0.01),
        'ln2_b': nrm(ks[24], (DEPTH, D_MODEL), 0.01),
    }


def reference(x_prompt, x_sample, state_conv_a, state_gla, state_conv_c, w_in, a_conv_w,
              b_gate_w2, b_gate_b, b_norm_g, c_conv_w, c_conv_b, c_ln_g, c_ln_b, d_ln_g, d_ln_b,
              d_ws, d_bs, w_out, ln1_g, ln1_b, w_ff1, w_ff2, ln2_g, ln2_b):
    weights = (w_in, a_conv_w, b_gate_w2, b_gate_b, b_norm_g, c_conv_w, c_conv_b, c_ln_g, c_ln_b,
               d_ln_g, d_ln_b, d_ws, d_bs, w_out, ln1_g, ln1_b, w_ff1, w_ff2, ln2_g, ln2_b)
    bp = x_prompt.shape[0]
    xp, xs = x_prompt, x_sample
    pa, ps, pc = [], [], []
    sa, ss, sc, sv = [], [], [], []
    for l in range(DEPTH):
        p = tuple(w[l] for w in weights)
        xp, na, ns, nc, _ = decoder_layer(
            xp, jnp.zeros((bp, K_A - 1, W_A), xp.dtype),
            jnp.zeros((bp, H_B, DK, DV), state_gla.dtype),
            jnp.zeros((bp, K_C - 1, W_C), xp.dtype), p)
        pa.append(na); ps.append(ns); pc.append(nc)
        xs, na, ns, nc, nv = decoder_layer(xs, state_conv_a[l], state_gla[l], state_conv_c[l], p)
        sa.append(na); ss.append(ns); sc.append(nc); sv.append(nv)
    return (xp, xs, jnp.stack(pa), jnp.stack(sa), jnp.stack(ps), jnp.stack(ss),
            jnp.stack(pc), jnp.stack(sc), jnp.stack(sv))
```

```python
import os
import numpy as np
import ml_dtypes
SKIP = set()
from contextlib import ExitStack
import concourse.bass as bass
import concourse.mybir as mybir
from concourse.bass_utils import run_bass_kernel_spmd

F32 = mybir.dt.float32
BF16 = mybir.dt.bfloat16
ALU = mybir.AluOpType
AF = mybir.ActivationFunctionType
AX = mybir.AxisListType

DEPTH = 2
ALPHA = float((2 * DEPTH) ** 0.25)
EPS = 1e-5
NCV = 109
GR = 64
ENGS = ("pe", "act", "dve", "pool", "sp")
STRICT = True


def I(name, *a, **k):
    return lambda e: getattr(e, name)(*a, **k)


class Op:
    __slots__ = ("eng", "fn", "reads", "writes", "deps", "signal", "cnt", "dkey", "didx", "raw")

    def __init__(self, eng, fn, reads, writes, dkey):
        self.eng = eng
        self.fn = fn
        self.reads = reads
        self.writes = writes
        self.deps = []
        self.signal = False
        self.cnt = 0
        self.dkey = dkey
        self.didx = 0
        self.raw = set()


class Prog:
    def __init__(self, nc):
        self.nc = nc
        self.ops = {e: [] for e in ENGS}
        self.lw = {}
        self.rd = {}
        self.dkeys = {}
        self.group_keys = set()

    def add(self, eng, fn, reads=(), writes=(), dkey=None, group=False):
        ps_r = [r for r in reads if r[0] == "ps"]
        if ps_r:
            reads = [r for r in reads if r[0] != "ps"]
            writes = list(writes) + [r for r in ps_r if r not in writes]
        op = Op(eng, fn, tuple(reads), tuple(writes), dkey)
        deps = []
        for r in ps_r:
            w = self.lw.get(r)
            if w is not None:
                op.raw.add(id(w))
        for r in op.reads:
            w = self.lw.get(r)
            if w is not None:
                deps.append(w)
                op.raw.add(id(w))
        for w in op.writes:
            p = self.lw.get(w)
            if p is not None:
                deps.append(p)
            q = self.rd.get(w)
            if q:
                deps.extend(q)
        for r in op.reads:
            self.rd.setdefault(r, []).append(op)
        for w in op.writes:
            self.lw[w] = op
            self.rd[w] = []
        seen = set()
        for d in deps:
            if id(d) in seen or d is op:
                continue
            seen.add(id(d))
            op.deps.append(d)
        if dkey is not None:
            lst = self.dkeys.setdefault(dkey, [])
            lst.append(op)
            op.didx = len(lst)
            if group:
                self.group_keys.add(dkey)
        self.ops[eng].append(op)
        return op

    def pe(self, fn, reads=(), writes=()):
        return self.add("pe", fn, reads, writes)

    def act(self, fn, reads=(), writes=()):
        return self.add("act", fn, reads, writes)

    def dve(self, fn, reads=(), writes=()):
        return self.add("dve", fn, reads, writes)

    def pool(self, fn, reads=(), writes=()):
        return self.add("pool", fn, reads, writes)

    def dma(self, q, out, in_, reads=(), writes=(), dkey=None, group=False, **kw):
        return self.add(q, I("dma_start", out=out, in_=in_, **kw), reads, writes, dkey=dkey, group=group)

    def _needs_edge(self, op, d):
        if d.dkey is not None or op.dkey is not None:
            return True
        if d.eng != op.eng:
            return True
        if op.eng == "pe":
            return False
        return STRICT or id(d) in op.raw

    def finalize(self, stack):
        nc = self.nc
        for e in ENGS:
            for op in self.ops[e]:
                op.deps = [d for d in op.deps if self._needs_edge(op, d)]
                for d in op.deps:
                    if d.dkey is None:
                        d.signal = True
        self.esem = {}
        for e in ENGS:
            c = 0
            for op in self.ops[e]:
                if op.dkey is None and op.signal:
                    c += 1
                    op.cnt = c
            self.esem[e] = stack.enter_context(nc.semaphore("sem_" + e))
        self.dsem = {}
        for i, k in enumerate(self.dkeys):
            self.dsem[k] = stack.enter_context(nc.semaphore("dsem%d" % i))

    def emit(self, block):
        emap = {"pe": "tensor", "act": "scalar", "dve": "vector", "pool": "gpsimd", "sp": "sync"}
        prog = self

        def mk(ename):
            def body(eng):
                waited = {}
                for op in prog.ops[ename]:
                    need = {}
                    for d in op.deps:
                        if d.dkey is not None:
                            s = prog.dsem[d.dkey]
                            if d.dkey in prog.group_keys:
                                v = 16 * len(prog.dkeys[d.dkey])
                            else:
                                v = 16 * d.didx
                            k = ("d", d.dkey)
                        else:
                            s = prog.esem[d.eng]
                            v = d.cnt
                            k = ("e", d.eng)
                        if need.get(k, (None, 0))[1] < v:
                            need[k] = (s, v)
                    for k, (s, v) in need.items():
                        if waited.get(k, 0) >= v:
                            continue
                        eng.wait_ge(s, v)
                        waited[k] = v
                    ins = op.fn(eng)
                    if op.dkey is not None:
                        ins.then_inc(prog.dsem[op.dkey], 16)
                    elif op.signal:
                        ins.then_inc(prog.esem[ename], 1)
                if ename == "sp":
                    for k, lst in prog.dkeys.items():
                        v = 16 * len(lst)
                        if waited.get(("d", k), 0) < v:
                            eng.wait_ge(prog.dsem[k], v)
            return body

        for ename in ENGS:
            if not prog.ops[ename] and ename != "sp":
                continue
            getattr(block, emap[ename])(mk(ename))


class Buf:
    def __init__(self, t, pstride, off, n, esz, ns, boff):
        self.t = t
        self.pstride = pstride
        self.off = off
        self.n = n
        self.esz = esz
        self.ns = ns
        self.boff = boff

    def k(self, lo=0, hi=None):
        if hi is None:
            hi = self.n
        b0 = self.boff + lo * self.esz
        b1 = self.boff + hi * self.esz
        return [(self.ns, g) for g in range(b0 // GR, (b1 - 1) // GR + 1)]

    def ap(self, off, dims, parts=128, p0=0):
        return bass.AP(self.t, p0 * self.pstride + self.off + off, [[self.pstride, parts]] + [list(d) for d in dims])

    def c(self, c0, n, parts=128, p0=0):
        return self.ap(c0, [[1, n]], parts, p0)

    def sub(self, lo, n):
        return Buf(self.t, self.pstride, self.off + lo, n, self.esz, self.ns, self.boff + lo * self.esz)


class _Stop(Exception):
    pass


STAGE_LIMIT = [99]


def stage(n):
    if n > STAGE_LIMIT[0]:
        raise _Stop()


def build_program():
    nc = bass.Bass("TRN2", target_bir_lowering=False)

    def din(name, shape, dt=F32):
        return nc.dram_tensor(name, shape, dt, kind="ExternalInput")

    def dout(name, shape):
        return nc.dram_tensor(name, shape, F32, kind="ExternalOutput")

    xp_d = din("xp", [2048, 1024])
    xs_d = din("xs", [128, 1024])
    sta_d = din("sta", [2, 32, 256])
    stg_d = din("stg", [2, 16, 8192])
    stc_d = din("stc", [2, 480, 256])
    win_d = din("w_in", [2, 1024, 2576])
    wout_d = din("w_out", [2, 1024, 1024])
    wff1_d = din("w_ff1", [2, 1024, 4096])
    wff2_d = din("w_ff2", [2, 4096, 1024])
    cvec_d = din("cvec", [2, 128, NCV])
    gw2_d = din("gw2", [2, 16, 128])
    dln_d = din("dln", [2, 2, 256])
    dws_d = din("dws", [2, 4, 128, 128])
    dbs_d = din("dbs", [2, 4, 128])
    ident_d = din("ident", [128, 128])
    tril_d = din("tril", [128, 128])
    hm4_d = din("hm4", [128, 4])
    bm_d = din("bm", [128, 256])
    rmask_d = din("rmask", [128, 1152], BF16)
    seqm_d = din("seqm", [128, 2048], BF16)
    tokm_d = din("tokm", [128, 16])
    dbt_d = din("dbt", [8, 128, 128])
    dwst_d = din("dwst", [16, 128, 128])

    yp_d = dout("yp", [2048, 1024])
    ys_d = dout("ys", [128, 1024])
    nap_d = dout("nap", [2, 2, 256])
    nas_d = dout("nas", [2, 32, 256])
    ngp_d = dout("ngp", [2, 8192])
    ngs_d = dout("ngs", [2, 16, 8192])
    ncp_d = dout("ncp", [2, 30, 256])
    ncs_d = dout("ncs", [2, 480, 256])
    nvs_d = dout("nvs", [2, 128, 256])

    def DAP(t, off, dims):
        return bass.AP(t, off, [list(d) for d in dims])

    NTMAX = 1152
    st = ExitStack()
    with st:
        P = Prog(nc)

        def mk(name, n, dt):
            t = st.enter_context(nc.sbuf_tensor("s_" + name, [128, n], dt))
            esz = 4 if dt == F32 else 2
            return Buf(t, n, 0, n, esz, name, 0)

        x32 = mk("x32", 8 * NTMAX, F32)
        xb = mk("xb", 8 * NTMAX, BF16)
        WSLOT = 4224
        wring = [mk("wr%d" % i, WSLOT, BF16) for i in range(3)]
        ident32 = mk("ident32", 128, F32)
        identb = mk("identb", 128, BF16)
        onesb = mk("onesb", 128, BF16)
        tril = mk("tril", 128, F32)
        hm4 = mk("hm4", 4, F32)
        bmk = mk("bmk", 256, F32)
        rmask = mk("rmask", NTMAX, BF16)
        seqm = mk("seqm", 2048, BF16)
        tokm = mk("tokm", 16, F32)
        trilb = mk("trilb", 128, BF16)
        sblkb = mk("sblkb", 128, BF16)
        cvec = mk("cvec", 2 * NCV, F32)
        ngb = mk("ngb", 2, F32)
        gw2 = mk("gw2", 2 * 128, F32)
        dlng = mk("dlng", 2 * 2 * 256, F32)
        dbias = mk("dbias", 2 * 2 * 2 * 128, F32)
        wsT = mk("wsT", 2 * 2 * 4 * 128, BF16)
        stA = mk("stA", 2 * 2 * 2, BF16)
        stC = mk("stC", 2 * 2 * 30, BF16)
        Sbd32 = mk("Sbd32", 2 * 256, F32)
        SCRB = 100 * 1024
        scr_t = st.enter_context(nc.sbuf_tensor("scr", [128, SCRB // 2], BF16))
        scr_h = {BF16: scr_t, F32: scr_t.bitcast(F32)}
        scr_top = [0]

        def carve(n, dt):
            esz = 4 if dt == F32 else 2
            b0 = (scr_top[0] + 63) // 64 * 64
            scr_top[0] = b0 + n * esz
            assert scr_top[0] <= SCRB, ("scratch overflow", scr_top[0])
            return Buf(scr_h[dt], SCRB // esz, b0 // esz, n, esz, "scr", b0)

        def alias(buf, n, dt, byte_off=0):
            esz = 4 if dt == F32 else 2
            b0 = buf.boff + byte_off
            assert b0 % esz == 0 and byte_off + n * esz <= buf.n * buf.esz
            return Buf(scr_h[dt], SCRB // esz, b0 // esz, n, esz, "scr", b0)

        pst = st.enter_context(nc.psum_tensor("pst", [128, 4096], F32))
        pst_bf = pst.bitcast(BF16)
        ps_ctr = [0]

        def psbank(i):
            return Buf(pst, 4096, 512 * i, 512, 4, "ps", 2048 * i)

        def psbank_bf(i):
            return Buf(pst_bf, 8192, 1024 * i, 1024, 2, "ps", 2048 * i)

        ps_reserved = set()

        def ps_next(bf=False):
            while True:
                i = ps_ctr[0] % 8
                ps_ctr[0] += 1
                if i not in ps_reserved:
                    break
            return psbank_bf(i) if bf else psbank(i)

        def ps_reserve():
            while True:
                i = ps_ctr[0] % 8
                ps_ctr[0] += 1
                if i not in ps_reserved:
                    break
            ps_reserved.add(i)
            return psbank(i), i

        def ps_next2():
            if ps_ctr[0] % 2:
                ps_ctr[0] += 1
            i = ps_ctr[0] % 8
            ps_ctr[0] += 2
            return Buf(pst, 4096, 512 * i, 1024, 4, "ps", 2048 * i)

        def kx32(t0, fc=None):
            if fc is None:
                return [("x32", f, t0 // 512) for f in range(8)]
            return [("x32", fc, t0 // 512)]

        def kxb(t0, fc=None):
            if fc is None:
                return [("xb", f, t0 // 512) for f in range(8)]
            return [("xb", fc, t0 // 512)]

        ev_ctr = [0]

        def evac_copy(out_ap, in_ap, reads, writes, eng=None):
            if eng is None:
                eng = "act" if ev_ctr[0] % 2 == 0 else "dve"
                ev_ctr[0] += 1
            if eng == "act":
                P.act(I("activation", out_ap, in_ap, AF.Copy), reads, writes)
            else:
                P.dve(I("tensor_copy", out_ap, in_ap), reads, writes)

        def transpose(out_ps_ap, in_ap, n_in_parts, reads, writes, bf=False):
            idn = identb if bf else ident32
            ida = idn.ap(0, [[1, n_in_parts]], parts=n_in_parts)
            P.pe(I("transpose", out_ps_ap, in_ap, ida), list(reads) + idn.k(), writes)

        wtiles = []
        for S in range(2):
            for l in range(2):
                for (c0, c1) in [(0, 512), (512, 1024), (1536, 2064), (1024, 1536), (2064, 2576)]:
                    wtiles.append(("in", l, c0, c1))
                for c0 in (0, 512):
                    wtiles.append(("out", l, c0, c0 + 512))
                for g in range(8):
                    wtiles.append(("ff1", l, 512 * g, 512 * g + 512))
                for oc in range(8):
                    wtiles.append(("ff2", l, 128 * oc, 128 * oc + 128))
        w_issued = [0]
        w_next = [0]

        def w_issue(gi):
            kind, l, c0, c1 = wtiles[gi]
            nco = c1 - c0
            slot = wring[gi % 3]
            if kind == "ff2":
                src = DAP(wff2_d, l * 4096 * 1024 + c0, [[1024, 128], [128 * 1024, 32], [1, nco]])
                dst = slot.ap(0, [[nco, 32], [1, nco]])
            else:
                dt_, ncols = {"in": (win_d, 2576), "out": (wout_d, 1024), "ff1": (wff1_d, 4096)}[kind]
                src = DAP(dt_, l * 1024 * ncols + c0, [[ncols, 128], [128 * ncols, 8], [1, nco]])
                dst = slot.ap(0, [[nco, 8], [1, nco]])
            P.dma("pool", dst, src, writes=slot.k(), dkey=("w", gi % 3))

        def w_acquire(kind, l, ahead=2):
            gi = w_next[0]
            while not (wtiles[gi][0] == kind and wtiles[gi][1] == l):
                gi += 1
            w_next[0] = gi + 1
            while w_issued[0] < min(len(wtiles), gi + 1 + ahead):
                w_issue(w_issued[0])
                w_issued[0] += 1
            nco = wtiles[gi][3] - wtiles[gi][2]
            return wring[gi % 3], nco

        def cload(buf, src, parts=128, dims=None):
            P.dma("sp", buf.ap(0, dims if dims else [[1, buf.n]], parts=parts), src, writes=buf.k(), dkey="c", group=True)

        cload(ident32, ident_d.ap())
        cload(tril, tril_d.ap())
        cload(hm4, hm4_d.ap())
        cload(bmk, bm_d.ap())
        cload(rmask, rmask_d.ap())
        cload(seqm, seqm_d.ap())
        cload(tokm, tokm_d.ap())
        if 'v' not in SKIP:
          cload(cvec, DAP(cvec_d, 0, [[NCV, 128], [128 * NCV, 2], [1, NCV]]), dims=[[NCV, 2], [1, NCV]])
        if 'g' not in SKIP:
          cload(gw2, DAP(gw2_d, 0, [[128, 16], [16 * 128, 2], [1, 128]]), parts=16, dims=[[128, 2], [1, 128]])
        if 'b' not in SKIP:
            cload(dlng, DAP(dln_d, 0, [[0, 128], [1, 1024]]))
        cload(dbias, DAP(dbt_d, 0, [[128, 128], [128 * 128, 8], [1, 128]]), dims=[[128, 8], [1, 128]])
        P.dve(I("tensor_copy", identb.c(0, 128), ident32.c(0, 128)), ident32.k(), identb.k())
        P.dve(I("memset", onesb.c(0, 128), 1.0), (), onesb.k())
        P.dve(I("tensor_copy", trilb.c(0, 128), tril.c(0, 128)), tril.k(), trilb.k())
        for l in range(2):
            P.dve(I("tensor_scalar", ngb.c(l, 1), cvec.c(l * NCV + 6, 1), -1.0, None, ALU.mult),
                  cvec.k(), ngb.k())
        P.dve(I("memset", Sbd32.c(0, 512), 0.0), (), Sbd32.k())
        P.dve(I("memset", stA.c(0, 8), 0.0), (), stA.k())
        P.dve(I("memset", stC.c(0, 120), 0.0), (), stC.k())

        mark = scr_top[0]
        wst = carve(16 * 128, F32)
        P.dma("sp", wst.ap(0, [[128, 16], [1, 128]]), DAP(dwst_d, 0, [[128, 128], [128 * 128, 16], [1, 128]]),
              writes=wst.k(), dkey="c", group=True)
        P.dve(I("tensor_tensor", wsT.ap(0, [[128, 16], [1, 128]]), wst.ap(0, [[128, 16], [1, 128]]),
                                        tril.ap(0, [[0, 16], [1, 128]]), ALU.mult), wst.k() + tril.k(), wsT.k())
        scr_top[0] = mark

        def layer(l, Sidx):
            nP = 1024
            has_s = Sidx == 1
            NT = nP + (128 if has_s else 0)
            TT = [(0, 512, "p"), (512, 512, "p")] + ([(1024, 128, "s")] if has_s else [])
            NCH = NT // 128
            cv = lambda col: cvec.c(l * NCV + col, 1)
            mark0 = scr_top[0]
            ymix = carve(8 * NT, BF16)

            def xb_t(kc, t0, n):
                return xb.c(kc * NTMAX + t0, n)

            def mm_fm(wb, nco, cc0, ccn, t0, n, pb):
                for kc in range(8):
                    P.pe(I("matmul", pb.c(0, n, parts=ccn), wb.ap(kc * nco + cc0, [[1, ccn]]), xb_t(kc, t0, n),
                                                    start=(kc == 0), stop=(kc == 7)),
                         wb.k() + kxb(t0, kc), pb.k())

            def cview(buf, j, L, H, t0, n, kind, shift=0):
                if kind == "p":
                    return buf.ap(j * L + t0 + shift, [[1, n]])
                return buf.ap(j * L + H + nP + shift, [[H + 8, 16], [1, 8]])

            def kcv(buf, j, L, H, t0, n, kind, lo, hi):
                if kind == "p":
                    return buf.k(j * L + t0 + lo, j * L + t0 + n + hi)
                return buf.k(j * L + H + nP, j * L + H + nP + 16 * (H + 8))

            def tview(buf, base, t0, n, kind):
                if kind == "p":
                    return buf.ap(base + t0, [[1, n]])
                return buf.ap(base + t0, [[8, 16], [1, 8]])

            stage(2)
            LA = 2 + nP + (160 if has_s else 0)
            mA = scr_top[0]
            ab32 = carve(2 * NT, F32)
            ac32 = carve(2 * NT, F32)
            gAb = carve(2 * LA, BF16)
            diagA = carve(6 * 128, BF16)
            gst = carve(2 * 34, F32)
            for j in range(2):
                P.dve(I("tensor_tensor", diagA.ap(j * 3 * 128, [[128, 3], [1, 128]]), ident32.ap(0, [[0, 3], [1, 128]]),
                        cvec.ap(l * NCV + j * 3, [[1, 3], [0, 128]]), ALU.mult), ident32.k() + cvec.k(), diagA.k())
                P.dve(I("tensor_copy", gAb.c(j * LA, 2), stA.c((l * 2 + j) * 2, 2)), stA.k(), gAb.k(j * LA, j * LA + 2))
            if has_s:
                sta_st = carve(256, F32)
                P.dma("sp", sta_st.ap(0, [[1, 256]], parts=32), DAP(sta_d, l * 32 * 256, [[256, 32], [1, 256]]),
                      writes=sta_st.k(), dkey="ld", group=False)
                for j in range(2):
                    pb = ps_next()
                    transpose(pb.c(0, 32), sta_st.ap(j * 128, [[1, 128]], parts=32), 32, sta_st.k(), pb.k())
                    P.act(I("activation", gAb.ap(j * LA + 2 + nP, [[10, 16], [1, 2]]),
                                                             pb.ap(0, [[2, 16], [1, 2]]), AF.Copy), pb.k(), gAb.k())
            wb, nco = w_acquire("in", l)
            for ci, dst in ((0, ab32), (1, ab32), (2, ac32), (3, ac32)):
                j = ci % 2
                for (t0, n, kind) in TT:
                    pb = ps_next()
                    mm_fm(wb, nco, ci * 128, 128, t0, n, pb)
                    evac_copy(dst.c(j * NT + t0, n), pb.c(0, n), pb.k(), dst.k(j * NT + t0, j * NT + t0 + n))
            wb2, nco2 = w_acquire("in", l)
            for j in range(2):
                for (t0, n, kind) in TT:
                    pb = ps_next()
                    mm_fm(wb2, nco2, j * 128, 128, t0, n, pb)
                    P.dve(I("tensor_tensor",
                        cview(gAb, j, LA, 2, t0, n, kind, shift=2 if kind == "p" else 2), tview(pb, 0, 0, n, kind),
                        tview(ac32, j * NT, t0, n, kind), ALU.mult), pb.k() + ac32.k(j * NT + t0, j * NT + t0 + n), kcv(gAb, j, LA, 2, t0, n, kind, 2, 2))
                    if kind == "p" and Sidx == 1 and t0 == 512:
                        P.dve(I("tensor_tensor", gst.c(j * 34, 2), pb.c(510, 2),
                                                                    ac32.c(j * NT + 1022, 2), ALU.mult),
                              pb.k() + ac32.k(), gst.k())
                    if kind == "s":
                        P.dve(I("tensor_tensor",
                            gst.ap(j * 34 + 2, [[2, 16], [1, 2]]), pb.ap(6, [[8, 16], [1, 2]]),
                            ac32.ap(j * NT + 1024 + 6, [[8, 16], [1, 2]]), ALU.mult), pb.k() + ac32.k(), gst.k())
            if Sidx == 0:
                for j in range(2):
                    P.dve(I("tensor_copy", stA.c((l * 2 + j) * 2, 2), gAb.c(j * LA + nP, 2)), gAb.k(j * LA + nP, j * LA + nP + 2), stA.k())
            for j in range(2):
                for (t0, n, kind) in TT:
                    pb = ps_next()
                    for k in range(3):
                        P.pe(I("matmul",
                            tview(pb, 0, 0, n, kind), diagA.c((j * 3 + k) * 128, 128), cview(gAb, j, LA, 2, t0, n, kind, shift=k),
                            start=(k == 0), stop=(k == 2)), diagA.k() + kcv(gAb, j, LA, 2, t0, n, kind, 0, 2), pb.k())
                    P.dve(I("tensor_tensor",
                        ymix.c((0 + j) * NT + t0, n), pb.c(0, n), ab32.c(j * NT + t0, n), ALU.mult),
                        pb.k() + ab32.k(j * NT + t0, j * NT + t0 + n), ymix.k(j * NT + t0, j * NT + t0 + n))
            if has_s:
                stg_a = carve(256, F32)
                for j in range(2):
                    pb = ps_next()
                    transpose(pb.c(0, 128, parts=34), gst.c(j * 34, 34), 128, gst.k(), pb.k())
                    P.act(I("activation", stg_a.c(j * 128, 128, parts=34), pb.c(0, 128, parts=34), AF.Copy),
                          pb.k(), stg_a.k())
                P.dma("sp", DAP(nap_d, l * 512, [[256, 2], [1, 256]]), stg_a.c(0, 256, parts=2), reads=stg_a.k(), dkey=("o", 1))
                P.dma("sp", DAP(nas_d, l * 32 * 256, [[256, 32], [1, 256]]), stg_a.c(0, 256, parts=32, p0=2),
                      reads=stg_a.k(), dkey=("o", 2))
            scr_top[0] = mA

            stage(3)
            mB = scr_top[0]
            qeT = carve(NT, BF16)
            keT = carve(NT, BF16)
            gsb = carve(2 * NT, BF16)
            vtm = carve(NCH * 256, BF16)
            ketm = carve(NCH * 128, BF16)
            elast = carve(32, F32)
            if has_s:
                S0c = carve(16 * 64, F32)
                S0bdb = carve(16 * 256, BF16)
                qeTs = carve(16 * 128, BF16)
                ketms = carve(16 * 128, BF16)
            mB2 = scr_top[0]
            B1 = carve(NT, F32)
            B2 = carve(NT, F32)
            B3 = carve(NT, F32)
            B4 = carve(NT, F32)
            zlr = carve(NT, F32)
            wb4, nco4 = w_acquire("in", l, ahead=1)
            for (t0, n, kind) in TT:
                pb = ps_next()
                mm_fm(wb4, nco4, 0, 16, t0, n, pb)
                evac_copy(zlr.c(t0, n, parts=16), pb.c(0, n, parts=16), pb.k(), zlr.k(t0, t0 + n))
            for (t0, n, kind) in TT:
                pb = ps_next()
                P.pe(I("matmul", pb.c(0, n), gw2.c(l * 128, 128, parts=16), zlr.c(t0, n, parts=16),
                       start=True, stop=True), gw2.k() + zlr.k(t0, t0 + n), pb.k())
                P.act(I("activation", B3.c(t0, n), pb.c(0, n), AF.Exp, bias=ngb.c(l, 1), scale=-1.0),
                      pb.k() + ngb.k(), B3.k(t0, t0 + n))
            P.act(I("activation", B3.c(0, NT), B3.c(0, NT), AF.Ln, bias=1.0, scale=1.0), B3.k(), B3.k())
            P.dve(I("tensor_tensor_scan", B4.c(0, NT), rmask.c(0, NT), B3.c(0, NT), 0.0, ALU.mult, ALU.add),
                  rmask.k() + B3.k(), B4.k())
            P.act(I("activation", B3.c(0, NT), B4.c(0, NT), AF.Exp, scale=-1.0 / 16.0), B4.k(), B3.k())
            P.act(I("activation", B4.c(0, NT), B4.c(0, NT), AF.Exp, scale=1.0 / 16.0), B4.k(), B4.k())
            P.dve(I("tensor_copy", elast.ap(0, [[1, 8]]), B3.ap(127, [[128, 8]])), B3.k(), elast.k())
            if has_s:
                P.dve(I("tensor_copy", elast.ap(8, [[1, 16]]), B3.ap(1024 + 7, [[8, 16]])), B3.k(), elast.k())
            for ci, dst in ((2, B1), (3, B2)):
                for (t0, n, kind) in TT:
                    pb = ps_next()
                    mm_fm(wb2, nco2, ci * 128, 128, t0, n, pb)
                    evac_copy(dst.c(t0, n), pb.c(0, n), pb.k(), dst.k(t0, t0 + n))
            P.dve(I("scalar_tensor_tensor", qeT.c(0, NT), B1.c(0, NT), float(32 ** -0.5), B3.c(0, NT), ALU.mult, ALU.mult),
                  B1.k() + B3.k(), qeT.k())
            P.dve(I("tensor_tensor", keT.c(0, NT), B2.c(0, NT), B4.c(0, NT), ALU.mult), B2.k() + B4.k(), keT.k())
            wb3, nco3 = w_acquire("in", l, ahead=1)
            for c in range(NCH):
                pb = ps_next()
                for kc in range(8):
                    P.pe(I("matmul", pb.c(0, 256), xb_t(kc, c * 128, 128), wb3.ap(kc * nco3, [[1, 256]]),
                           start=(kc == 0), stop=(kc == 7)), wb3.k() + kxb(c * 128, kc), pb.k())
                evac_copy(vtm.c(c * 256, 256), pb.c(0, 256), pb.k(), vtm.k(c * 256, c * 256 + 256))
            for j in range(2):
                for (t0, n, kind) in TT:
                    pb = ps_next()
                    mm_fm(wb3, nco3, 256 + j * 128, 128, t0, n, pb)
                    P.act(I("activation", gsb.c(j * NT + t0, n), pb.c(0, n), AF.Silu),
                          pb.k(), gsb.k(j * NT + t0, j * NT + t0 + n))
            scr_top[0] = mB2
            NPC = 8
            keTm_all = carve(4 * NT, BF16)
            attm_all = carve(NCH * 512, BF16)
            t1_all = carve(NCH * 256, F32)
            Sbdb_all = carve((NPC + 1) * 256, BF16)
            osb_all = carve(NCH * 256, F32)
            on_all = alias(attm_all, NCH * 256, BF16)
            ssa = carve(NCH * 8, F32)
            atmp = [carve(512, BF16) for _ in range(2)]
            ytmp = [carve(128, BF16) for _ in range(3)]
            if has_s:
                t1s = alias(keTm_all, 4 * 256, F32)
                reds = alias(keTm_all, 4 * 64, F32, byte_off=4096)
            for h in range(4):
                P.dve(I("tensor_scalar", keTm_all.c(h * NT, NT), keT.c(0, NT), hm4.c(h, 1), None, ALU.mult),
                      keT.k() + hm4.k(), keTm_all.k(h * NT, h * NT + NT))
            if has_s:
                P.dma("sp", S0c.ap(0, [[64, 16], [1, 64]]), DAP(stg_d, l * 16 * 8192, [[64, 128], [8192, 16], [1, 64]]),
                      writes=S0c.k(), dkey="ld2", group=False)
                P.dve(I("tensor_tensor", S0bdb.ap(0, [[256, 16], [64, 4], [1, 64]]), S0c.ap(0, [[64, 16], [0, 4], [1, 64]]),
                        hm4.ap(0, [[0, 16], [1, 4], [0, 64]]), ALU.mult), S0c.k() + hm4.k(), S0bdb.k())
            for c in range(NCH):
                c0 = c * 128
                pbt = ps_next(bf=True)
                transpose(pbt.c(0, 128), keT.c(c0, 128), 128, keT.k(c0, c0 + 128), pbt.k(), bf=True)
                evac_copy(ketm.c(c * 128, 128), pbt.c(0, 128), pbt.k(), ketm.k(c * 128, c * 128 + 128), eng="act")
            for c in range(NCH):
                kind = "p" if c < 8 else "s"
                c0 = c * 128
                pa = ps_next()
                for h in range(4):
                    P.pe(I("matmul", pa.c(h * 128, 128), keTm_all.c(h * NT + c0, 128), qeT.c(c0, 128), start=True, stop=True),
                         keTm_all.k(h * NT, h * NT + NT) + qeT.k(c0, c0 + 128), pa.k())
                mk_ = trilb if kind == "p" else sblkb
                ta = atmp[c % 2]
                P.act(I("activation", ta.c(0, 512), pa.c(0, 512), AF.Copy), pa.k(), ta.k())
                P.dve(I("tensor_tensor", attm_all.ap(c * 512, [[128, 4], [1, 128]]), ta.ap(0, [[128, 4], [1, 128]]),
                        mk_.ap(0, [[0, 4], [1, 128]]), ALU.mult), ta.k() + mk_.k(), attm_all.k(c * 512, c * 512 + 512))
                if kind == "p":
                    pd = ps_next()
                    P.pe(I("matmul", pd.c(0, 256), ketm.c(c * 128, 128), vtm.c(c * 256, 256), start=True, stop=True),
                         ketm.k(c * 128, c * 128 + 128) + vtm.k(c * 256, c * 256 + 256), pd.k())
                    P.dve(I("scalar_tensor_tensor", t1_all.c(c * 256, 256), pd.c(0, 256), elast.c(c, 1), bmk.c(0, 256),
                            ALU.mult, ALU.mult), pd.k() + elast.k() + bmk.k(), t1_all.k(c * 256, c * 256 + 256))
            P.dve(I("tensor_copy", Sbdb_all.c(0, 256), Sbd32.c(l * 256, 256)), Sbd32.k(), Sbdb_all.k(0, 256))
            for c in range(NPC):
                P.dve(I("scalar_tensor_tensor", Sbd32.c(l * 256, 256), Sbd32.c(l * 256, 256), elast.c(c, 1), t1_all.c(c * 256, 256),
                        ALU.mult, ALU.add), Sbd32.k() + elast.k() + t1_all.k(c * 256, c * 256 + 256), Sbd32.k())
                P.dve(I("tensor_copy", Sbdb_all.c((c + 1) * 256, 256), Sbd32.c(l * 256, 256)), Sbd32.k(),
                      Sbdb_all.k((c + 1) * 256, (c + 2) * 256))
            if Sidx == 1:
                for h in range(4):
                    P.dma("sp", DAP(ngp_d, l * 8192 + h * 2048, [[64, 32], [1, 64]]),
                          Sbd32.c(l * 256 + h * 64, 64, parts=32, p0=32 * h), reads=Sbd32.k(), dkey=("o", 3))
            for c in range(NCH):
                kind = "p" if c < 8 else "s"
                c0 = c * 128
                po = ps_next()
                if kind == "p":
                    P.pe(I("matmul", po.c(0, 256), qeT.c(c0, 128), Sbdb_all.c(c * 256, 256), start=True, stop=False, skip_group_check=True),
                         qeT.k(c0, c0 + 128) + Sbdb_all.k(c * 256, c * 256 + 256), po.k())
                else:
                    P.dve(I("tensor_tensor", qeTs.ap(0, [[128, 16], [1, 128]]), qeT.ap(c0, [[0, 16], [1, 128]]),
                            seqm.ap(0, [[128, 16], [1, 128]]), ALU.mult), qeT.k(c0, c0 + 128) + seqm.k(), qeTs.k())
                    for q in range(16):
                        P.pe(I("matmul", po.c(0, 256), qeTs.c(q * 128, 128), S0bdb.c(q * 256, 256),
                               start=(q == 0), stop=False, skip_group_check=True), qeTs.k() + S0bdb.k(), po.k())
                for h in range(4):
                    P.pe(I("matmul", po.c(h * 64, 64), attm_all.c(c * 512 + h * 128, 128), vtm.c(c * 256 + h * 64, 64),
                           start=False, stop=(h == 3), skip_group_check=True),
                         attm_all.k(c * 512, c * 512 + 512) + vtm.k(c * 256, c * 256 + 256), po.k())
                evac_copy(osb_all.c(c * 256, 256), po.c(0, 256), po.k(), osb_all.k(c * 256, c * 256 + 256), eng="act")
            NO = NCH * 256
            P.act(I("activation", t1_all.c(0, NO), osb_all.c(0, NO), AF.Square), osb_all.k(), t1_all.k())
            P.dve(I("tensor_reduce", ssa.c(0, NCH * 4), t1_all.ap(0, [[64, NCH * 4], [1, 64]]), AX.X, ALU.add), t1_all.k(), ssa.k())
            P.act(I("activation", ssa.c(NCH * 4, NCH * 4), ssa.c(0, NCH * 4), AF.Ln, bias=cEPS.c(0, 1), scale=1.0 / 64.0),
                  ssa.k() + cEPS.k(), ssa.k())
            P.act(I("activation", ssa.c(0, NCH * 4), ssa.c(NCH * 4, NCH * 4), AF.Exp, scale=-0.5), ssa.k(), ssa.k())
            P.dve(I("tensor_tensor", on_all.ap(0, [[64, NCH * 4], [1, 64]]), osb_all.ap(0, [[64, NCH * 4], [1, 64]]),
                    ssa.ap(0, [[1, NCH * 4], [0, 64]]), ALU.mult), osb_all.k() + ssa.k(), on_all.k())
            for c in range(NCH):
                c0 = c * 128
                for j in range(2):
                    pt = ps_next(bf=True)
                    transpose(pt.c(0, 128), on_all.c(c * 256 + j * 128, 128), 128, on_all.k(), pt.k(), bf=True)
                    yt = ytmp[(2 * c + j) % 3]
                    P.act(I("activation", yt.c(0, 128), pt.c(0, 128), AF.Copy, scale=cv(7 + j)), pt.k() + cvec.k(), yt.k())
                    P.dve(I("tensor_tensor", ymix.c((2 + j) * NT + c0, 128), yt.c(0, 128), gsb.c(j * NT + c0, 128), ALU.mult),
                          yt.k() + gsb.k(j * NT + c0, j * NT + c0 + 128), ymix.k((2 + j) * NT + c0, (2 + j) * NT + c0 + 128))
            if has_s:
                c = 8
                P.dve(I("tensor_tensor", ketms.ap(0, [[128, 16], [1, 128]]), ketm.ap(c * 128, [[0, 16], [1, 128]]),
                        tokm.ap(0, [[1, 16], [0, 128]]), ALU.mult), ketm.k(c * 128, c * 128 + 128) + tokm.k(), ketms.k())
                for rd in range(4):
                    pd2 = ps_next2()
                    for qq in range(4):
                        q = rd * 4 + qq
                        P.pe(I("matmul", pd2.c(qq * 256, 256), ketms.c(q * 128, 128), vtm.c(c * 256, 256), start=True, stop=True),
                             ketms.k() + vtm.k(c * 256, c * 256 + 256), pd2.k())
                    for h in range(4):
                        P.dve(I("tensor_tensor", reds.ap(0, [[64, 4], [1, 64]], parts=32, p0=32 * h),
                                pd2.ap(64 * h, [[256, 4], [1, 64]], parts=32, p0=32 * h),
                                S0c.ap(rd * 256, [[64, 4], [1, 64]], parts=32, p0=32 * h), ALU.add),
                              pd2.k() + S0c.k(), reds.k())
                    P.dve(I("tensor_tensor", S0c.ap(rd * 256, [[64, 4], [1, 64]]), reds.ap(0, [[64, 4], [1, 64]]),
                            elast.ap(8 + rd * 4, [[1, 4], [0, 64]]), ALU.mult), reds.k() + elast.k(), S0c.k())
                P.dma("sp", DAP(ngs_d, l * 16 * 8192, [[64, 128], [8192, 16], [1, 64]]), S0c.ap(0, [[64, 16], [1, 64]]),
                      reads=S0c.k(), dkey=("o", 4))
            scr_top[0] = mB

            stage(4)
            LC = 30 + nP + (16 * 38 if has_s else 0)
            mC = scr_top[0]
            ca32 = carve(2 * NT, F32)
            cv32 = alias(ca32, 2 * NT, F32)
            gCb = carve(2 * LC, BF16)
            diagC = carve(62 * 128, BF16)
            cst = carve(2 * 158, F32)
            sgt2 = [carve(512, F32) for _ in range(2)]
            sg_i = [0]
            cset = (carve(2 * 512, BF16), carve(2 * 512, BF16), carve(512, F32), carve(512, F32), carve(512, F32))
            ctmp = sgt2
            u16 = carve(2 * NT, BF16)
            vv32 = carve(NCH * 256, F32)
            vvm = carve(NCH * 512, BF16)
            stt = carve(NCH * 8, F32)
            mv = carve(NCH * 2, F32)
            rsd = carve(NCH, F32)
            nmr = carve(NCH, F32)
            ftm = [carve(128, F32) for _ in range(2)]
            for j in range(2):
                P.dve(I("tensor_tensor", diagC.ap(j * 31 * 128, [[128, 31], [1, 128]]), ident32.ap(0, [[0, 31], [1, 128]]),
                        cvec.ap(l * NCV + 9 + j * 31, [[1, 31], [0, 128]]), ALU.mult),
                      ident32.k() + cvec.k(), diagC.k(j * 31 * 128, (j + 1) * 31 * 128))
                P.dve(I("tensor_copy", gCb.c(j * LC, 30), stC.c((l * 2 + j) * 30, 30)), stC.k(), gCb.k(j * LC, j * LC + 30))
            P.dve(I("memset", vvm.c(0, NCH * 512), 0.0), (), vvm.k())
            if has_s:
                stc_st = carve(4 * 256, F32)
                P.dma("sp", stc_st.ap(0, [[256, 4], [1, 256]], parts=120), DAP(stc_d, l * 480 * 256, [[256, 120], [120 * 256, 4], [1, 256]]),
                      writes=stc_st.k(), dkey="ld3", group=False)
                for grp in range(4):
                    for j in range(2):
                        pb = ps_next()
                        transpose(pb.c(0, 120), stc_st.ap(grp * 256 + j * 128, [[1, 128]], parts=120), 120, stc_st.k(), pb.k())
                        evac_copy(gCb.ap(j * LC + 30 + nP + grp * 4 * 38, [[38, 4], [1, 30]]), pb.ap(0, [[30, 4], [1, 30]]), pb.k(), gCb.k(j * LC + 30 + nP, j * LC + 30 + nP + 16 * 38))
                P.dma("sp", DAP(ncs_d, l * 480 * 256, [[30 * 256, 16], [1, 22 * 256]]),
                      DAP(stc_d, l * 480 * 256 + 8 * 256, [[30 * 256, 16], [1, 22 * 256]]), dkey=("o", 5))
                stg_c = alias(stc_st, 256, F32)
                stg_c2 = alias(stc_st, 256, F32, byte_off=1024)

            def C1():
                for j in range(2):
                    for (t0, n, kind) in TT:
                        pb = ps_next()
                        mm_fm(wb4, nco4, 16 + j * 128, 128, t0, n, pb)
                        evac_copy(ca32.c(j * NT + t0, n), pb.c(0, n), pb.k(), ca32.k(j * NT + t0, j * NT + t0 + n), eng="dve")
                for j in range(2):
                    for (t0, n, kind) in TT:
                        pb = ps_next()
                        mm_fm(wb4, nco4, 272 + j * 128, 128, t0, n, pb)
                        sgt = sgt2[sg_i[0] % 2]
                        sg_i[0] += 1
                        P.act(I("activation", sgt.c(0, n), pb.c(0, n), AF.Sigmoid), pb.k(), sgt.k())
                        P.dve(I("tensor_tensor", cview(gCb, j, LC, 30, t0, n, kind, shift=30), tview(sgt, 0, 0, n, kind),
                                tview(ca32, j * NT, t0, n, kind), ALU.mult), sgt.k() + ca32.k(j * NT + t0, j * NT + t0 + n), kcv(gCb, j, LC, 30, t0, n, kind, 30, 30))
                        if kind == "p" and Sidx == 1 and t0 == 512:
                            P.dve(I("tensor_tensor", cst.c(j * 158, 30), sgt.c(482, 30), ca32.c(j * NT + 994, 30), ALU.mult),
                                  sgt.k() + ca32.k(j * NT + t0, j * NT + t0 + n), cst.k())
                        if kind == "s":
                            P.dve(I("tensor_tensor", cst.c(j * 158 + 30, 128), sgt.c(0, 128), ca32.c(j * NT + 1024, 128), ALU.mult),
                                  sgt.k() + ca32.k(j * NT + t0, j * NT + t0 + n), cst.k())
                if Sidx == 0:
                    for j in range(2):
                        P.dve(I("tensor_copy", stC.c((l * 2 + j) * 30, 30), gCb.c(j * LC + nP, 30)), gCb.k(j * LC + nP, j * LC + nP + 30), stC.k())

            def C2():
                for j in range(2):
                    for (t0, n, kind) in TT:
                        pb = ps_next()
                        for k in range(31):
                            P.pe(I("matmul", tview(pb, 0, 0, n, kind), diagC.c((j * 31 + k) * 128, 128),
                                   cview(gCb, j, LC, 30, t0, n, kind, shift=k), start=(k == 0), stop=(k == 30)),
                                 diagC.k((j * 31 + k) * 128, (j * 31 + k + 1) * 128) + kcv(gCb, j, LC, 30, t0, n, kind, 0, 30), pb.k())
                        P.dve(I("tensor_scalar", cv32.c(j * NT + t0, n), pb.c(0, n), cv(71 + j), None, ALU.add),
                              pb.k() + cvec.k(), cv32.k(j * NT + t0, j * NT + t0 + n))

            def C3():
                cb16, csq, mean, msq, rstd = cset
                for ti, (t0, n, kind) in enumerate(TT):
                    for j in range(2):
                        P.dve(I("tensor_copy", cb16.c(j * 512, n), cv32.c(j * NT + t0, n)), cv32.k(j * NT + t0, j * NT + t0 + n),
                              cb16.k(j * 512, j * 512 + n))
                        P.act(I("activation", csq.c(j * 512, n), cv32.c(j * NT + t0, n), AF.Square), cv32.k(j * NT + t0, j * NT + t0 + n),
                              csq.k(j * 512, j * 512 + n))
                    p1 = ps_next()
                    p2 = ps_next()
                    for j in range(2):
                        P.pe(I("matmul", p1.c(0, n), onesb.c(0, 128), cb16.c(j * 512, n), start=(j == 0), stop=(j == 1)),
                             onesb.k() + cb16.k(j * 512, j * 512 + n), p1.k())
                    for j in range(2):
                        P.pe(I("matmul", p2.c(0, n), onesb.c(0, 128), csq.c(j * 512, n), start=(j == 0), stop=(j == 1)),
                             onesb.k() + csq.k(j * 512, j * 512 + n), p2.k())
                    ln_stats(p1, p2, n, 1.0 / 256.0, mean, msq, rstd)
                    for j in range(2):
                        tmpn = ctmp[j]
                        P.dve(I("tensor_tensor", tmpn.c(0, n), cv32.c(j * NT + t0, n), mean.c(0, n), ALU.subtract),
                              cv32.k(j * NT + t0, j * NT + t0 + n) + mean.k(), tmpn.k())
                        P.dve(I("tensor_tensor", tmpn.c(0, n), tmpn.c(0, n), rstd.c(0, n), ALU.mult), tmpn.k() + rstd.k(), tmpn.k())
                        P.act(I("activation", ymix.c((4 + j) * NT + t0, n), tmpn.c(0, n), AF.Silu, bias=cv(75 + j), scale=cv(73 + j)),
                              tmpn.k() + cvec.k(), ymix.k((4 + j) * NT + t0, (4 + j) * NT + t0 + n))
                if has_s:
                    for j in range(2):
                        pb = ps_next()
                        transpose(pb.c(0, 128, parts=30), cst.c(j * 158, 30), 128, cst.k(), pb.k())
                        evac_copy(stg_c.c(j * 128, 128, parts=30), pb.c(0, 128, parts=30), pb.k(), stg_c.k())
                        pb = ps_next()
                        transpose(pb.c(0, 128), cst.c(j * 158 + 30, 128), 128, cst.k(), pb.k())
                        evac_copy(stg_c2.c(j * 128, 128), pb.c(0, 128), pb.k(), stg_c2.k())
                    P.dma("sp", DAP(ncp_d, l * 30 * 256, [[256, 30], [1, 256]]), stg_c.c(0, 256, parts=30), reads=stg_c.k(), dkey=("o", 6))
                    for q in range(16):
                        P.dma("sp", DAP(ncs_d, l * 480 * 256 + q * 30 * 256 + 22 * 256, [[256, 8], [1, 256]]),
                              stg_c2.c(0, 256, parts=8, p0=8 * q), reads=stg_c2.k(), dkey=("o", 7))

            def D1():
                for j in range(2):
                    for (t0, n, kind) in TT:
                        pb = ps_next()
                        mm_fm(wb5, nco5, j * 128, 128, t0, n, pb)
                        P.act(I("activation", u16.c(j * NT + t0, n), pb.c(0, n), AF.Gelu_apprx_tanh),
                              pb.k(), u16.k(j * NT + t0, j * NT + t0 + n))

            def D2():
                for c in range(NCH):
                    pb = ps_next()
                    for kc in range(8):
                        P.pe(I("matmul", pb.c(0, 256), xb_t(kc, c * 128, 128), wb5.ap(kc * nco5 + 256, [[1, 256]]),
                               start=(kc == 0), stop=(kc == 7)), wb5.k() + kxb(c * 128, kc), pb.k())
                    P.act(I("activation", vv32.c(c * 256, 256), pb.c(0, 256), AF.Gelu_apprx_tanh), pb.k(), vv32.k(c * 256, c * 256 + 256))
                    P.dve(I("bn_stats", stt.c(c * 8, 6), vv32.c(c * 256, 256)), vv32.k(c * 256, c * 256 + 256), stt.k())
                    P.dve(I("bn_aggr", mv.c(c * 2, 2), stt.c(c * 8, 6)), stt.k(), mv.k())

            def D3():
                P.act(I("activation", rsd.c(0, NCH), mv.ap(1, [[2, NCH]]), AF.Ln, bias=cEPS.c(0, 1), scale=1.0), mv.k() + cEPS.k(), rsd.k())
                P.act(I("activation", rsd.c(0, NCH), rsd.c(0, NCH), AF.Exp, scale=-0.5), rsd.k(), rsd.k())
                vall = vv32.ap(0, [[256, NCH], [1, 256]])
                P.dve(I("scalar_tensor_tensor", nmr.c(0, NCH), mv.ap(0, [[2, NCH]]), -1.0, rsd.c(0, NCH), ALU.mult, ALU.mult),
                      mv.k() + rsd.k(), nmr.k())
                for c in range(NCH):
                    P.act(I("activation", vv32.c(c * 256, 256), vv32.c(c * 256, 256), AF.Identity, bias=nmr.c(c, 1), scale=rsd.c(c, 1)),
                          vv32.k(c * 256, c * 256 + 256) + nmr.k() + rsd.k(), vv32.k(c * 256, c * 256 + 256))
                P.dve(I("tensor_tensor", vall, vall, dlng.ap((l * 2 + 0) * 256, [[0, NCH], [1, 256]]), ALU.mult), vv32.k() + dlng.k(), vv32.k())
                P.dve(I("tensor_tensor", vall, vall, dlng.ap((l * 2 + 1) * 256, [[0, NCH], [1, 256]]), ALU.add), vv32.k() + dlng.k(), vv32.k())
                for j in range(2):
                    evac_copy(vvm.ap(j * 256, [[512, NCH], [192, 2], [1, 64]]), vv32.ap(j * 128, [[256, NCH], [64, 2], [1, 64]]),
                              vv32.k(), vvm.k())
                if has_s:
                    P.dma("sp", DAP(nvs_d, l * 128 * 256, [[256, 128], [1, 256]]), vv32.c(8 * 256, 256), reads=vv32.k(), dkey=("o", 8))

            def D4():
                fi = 0
                for c in range(NCH):
                    kd = 0 if c < 8 else 1
                    for j in range(2):
                        pb = ps_next()
                        for gg in range(2):
                            g = 2 * j + gg
                            P.pe(I("matmul", pb.c(0, 128), vvm.c(c * 512 + g * 128, 128), wsT.c(((l * 2 + kd) * 4 + g) * 128, 128),
                                   start=(gg == 0), stop=(gg == 1)), vvm.k(c * 512 + g * 128, c * 512 + g * 128 + 128) + wsT.k(), pb.k())
                        ft = ftm[fi % 2]
                        fi += 1
                        P.dve(I("tensor_tensor", ft.c(0, 128), pb.c(0, 128), dbias.c(((l * 2 + kd) * 2 + j) * 128, 128), ALU.add),
                              pb.k() + dbias.k(), ft.k())
                        P.dve(I("tensor_tensor", ymix.c((6 + j) * NT + c * 128, 128), ft.c(0, 128),
                                u16.c(j * NT + c * 128, 128), ALU.mult), ft.k() + u16.k(j * NT + c * 128, j * NT + c * 128 + 128),
                              ymix.k((6 + j) * NT + c * 128, (6 + j) * NT + c * 128 + 128))

            C1()
            stage(5)
            wb5, nco5 = w_acquire("in", l)
            D2()
            D3()
            C2()
            C3()
            D1()
            D4()
            scr_top[0] = mC

            stage(6)
            class StatAcc:
                def __init__(self, lag, nfc=8):
                    self.lag = lag
                    self.nfc = nfc
                    self.banks = [(ps_reserve(), ps_reserve()) for _ in TT]
                    self.ring = [(carve(512, BF16), carve(512, BF16)) for _ in range(lag + 2)]
                    self.i = 0
                    self.pending = []

                def add(self, fc, ti, t0, n):
                    xbt, sqt_ = self.ring[self.i % len(self.ring)]
                    self.i += 1
                    P.act(I("activation", xbt.c(0, n), x32.c(fc * NTMAX + t0, n), AF.Copy), kx32(t0, fc), xbt.k())
                    P.act(I("activation", sqt_.c(0, n), x32.c(fc * NTMAX + t0, n), AF.Square), kx32(t0, fc), sqt_.k())
                    self.pending.append((fc, ti, n, xbt, sqt_))
                    while len(self.pending) > self.lag:
                        self._flush()

                def _flush(self):
                    fc, ti, n, xbt, sqt_ = self.pending.pop(0)
                    (p1, _), (p2, _) = self.banks[ti]
                    P.pe(I("matmul", p1.c(0, n), onesb.c(0, 128), xbt.c(0, n), start=(fc == 0), stop=(fc == self.nfc - 1)),
                         onesb.k() + xbt.k(), p1.k())
                    P.pe(I("matmul", p2.c(0, n), onesb.c(0, 128), sqt_.c(0, n), start=(fc == 0), stop=(fc == self.nfc - 1)),
                         onesb.k() + sqt_.k(), p2.k())

                def finish(self, mean_, msq_, rstd_):
                    while self.pending:
                        self._flush()
                    for ti, (t0, n, kind) in enumerate(TT):
                        (p1, i1_), (p2, i2_) = self.banks[ti]
                        ln_stats(p1, p2, n, 1.0 / 1024.0, mean_.sub(t0, n), msq_.sub(t0, n), rstd_.sub(t0, n))
                        ps_reserved.discard(i1_)
                        ps_reserved.discard(i2_)

            def resid_ln(acc, gcol, bcol):
                mean_, msq_, rstd_ = lnset
                acc.finish(mean_, msq_, rstd_)
                for fc in range(8):
                    kall32 = [("x32", fc, ti) for ti in range(len(TT))]
                    kallb = [("xb", fc, ti) for ti in range(len(TT))]
                    tn = tmpn3[fc % len(tmpn3)]
                    P.dve(I("tensor_tensor", tn.c(0, NT), x32.c(fc * NTMAX, NT), mean_.c(0, NT), ALU.subtract),
                          kall32 + mean_.k(), tn.k())
                    P.dve(I("tensor_tensor", tn.c(0, NT), tn.c(0, NT), rstd_.c(0, NT), ALU.mult), tn.k() + rstd_.k(), tn.k())
                    if True:
                        P.act(I("activation", x32.c(fc * NTMAX, NT), tn.c(0, NT), AF.Identity,
                                bias=cv(bcol + fc), scale=cv(gcol + fc)), tn.k() + cvec.k(), kall32)
                    else:
                        P.dve(I("tensor_scalar", x32.c(fc * NTMAX, NT), tn.c(0, NT), cv(gcol + fc), cv(bcol + fc), ALU.mult, ALU.add),
                              tn.k() + cvec.k(), kall32)
                    P.act(I("activation", xb.c(fc * NTMAX, NT), tn.c(0, NT), AF.Identity,
                            bias=cv(bcol + fc), scale=cv(gcol + fc)), tn.k() + cvec.k(), kallb)

            def carve_ln():
                return ((carve(NT, F32), carve(NT, F32), carve(NT, F32)), [carve(NT, F32) for _ in range(3)])

            lnscr = scr_top[0]
            lnset, tmpn3 = carve_ln()
            acc1 = StatAcc(lag=6)
            for half in range(2):
                wbo, ncoo = w_acquire("out", l)
                for cc in range(4):
                    fc = half * 4 + cc
                    for (t0, n, kind) in TT:
                        pb = ps_next()
                        for kc in range(8):
                            P.pe(I("matmul",
                                pb.c(0, n), wbo.ap(kc * ncoo + cc * 128, [[1, 128]]), ymix.c(kc * NT + t0, n),
                                start=(kc == 0), stop=(kc == 7)), wbo.k() + ymix.k(kc * NT + t0, kc * NT + t0 + n), pb.k())
                        if 'dbgmix' in SKIP:
                            P.dve(I("tensor_copy", x32.c(fc * NTMAX + t0, n), pb.c(0, n)), pb.k(), kx32(t0, fc))
                        else:
                            P.dve(I("scalar_tensor_tensor",
                                x32.c(fc * NTMAX + t0, n), x32.c(fc * NTMAX + t0, n), ALPHA, pb.c(0, n), ALU.mult, ALU.add),
                                kx32(t0, fc) + pb.k(), kx32(t0, fc))
                            acc1.add(fc, TT.index((t0, n, kind)), t0, n)
            if 'dbgmix' in SKIP:
                scr_top[0] = mark0
                return
            resid_ln(acc1, 77, 85)
            scr_top[0] = lnscr

            stage(7)
            if 'ffn' in SKIP:
                scr_top[0] = mark0
                return
            scr_top[0] = mark0
            hbuf = carve(32 * NT, BF16)
            sqt = [carve(512, F32) for _ in range(2)]
            si = 0
            for g in range(8):
                wbf, ncof = w_acquire("ff1", l)
                for cc in range(4):
                    hc = g * 4 + cc
                    for (t0, n, kind) in TT:
                        pb = ps_next()
                        mm_fm(wbf, ncof, cc * 128, 128, t0, n, pb)
                        s_ = sqt[si]
                        si = 1 - si
                        P.act(I("activation", s_.c(0, n), pb.c(0, n), AF.Square), pb.k(), s_.k())
                        P.dve(I("scalar_tensor_tensor",
                            hbuf.c(hc * NT + t0, n), pb.c(0, n), 0.0, s_.c(0, n), ALU.is_gt, ALU.mult),
                            pb.k() + s_.k(), hbuf.k(hc * NT + t0, hc * NT + t0 + n))
            acc2 = StatAcc(lag=2)
            for oc in range(8):
                wbf, ncof = w_acquire("ff2", l)
                for (t0, n, kind) in TT:
                    pb = ps_next()
                    for kc in range(32):
                        P.pe(I("matmul", pb.c(0, n), wbf.c(kc * 128, 128), hbuf.c(kc * NT + t0, n),
                                                                           start=(kc == 0), stop=(kc == 31)), wbf.k() + hbuf.k(kc * NT + t0, kc * NT + t0 + n), pb.k())
                    P.dve(I("scalar_tensor_tensor",
                        x32.c(oc * NTMAX + t0, n), x32.c(oc * NTMAX + t0, n), ALPHA, pb.c(0, n), ALU.mult, ALU.add),
                        kx32(t0, oc) + pb.k(), kx32(t0, oc))
                    acc2.add(oc, TT.index((t0, n, kind)), t0, n)

            scr_top[0] = mark0
            lnset, tmpn3 = carve_ln()
            resid_ln(acc2, 93, 101)
            scr_top[0] = mark0

        cEPS = mk("cEPS", 1, F32)
        P.dve(I("memset", cEPS.c(0, 1), EPS), (), cEPS.k())
        sblk = mk("sblk", 128, F32)
        sblk_d = din("sblk", [128, 128])
        P.dma("sp", sblk.c(0, 128), sblk_d.ap(), writes=sblk.k(), dkey="c", group=True)
        P.dve(I("tensor_copy", sblkb.c(0, 128), sblk.c(0, 128)), sblk.k(), sblkb.k())

        def ln_stats(p1, p2, n, inv, mean, msq, rstd):
            P.act(I("activation", mean.c(0, n), p1.c(0, n), AF.Copy, scale=inv), p1.k(), mean.k())
            P.act(I("activation", msq.c(0, n), p1.c(0, n), AF.Square, scale=inv), p1.k(), msq.k())
            P.dve(I("scalar_tensor_tensor", rstd.c(0, n), p2.c(0, n), inv, msq.c(0, n), ALU.mult, ALU.subtract),
                  p2.k() + msq.k(), rstd.k())
            P.act(I("activation", rstd.c(0, n), rstd.c(0, n), AF.Ln, bias=cEPS.c(0, 1), scale=1.0), rstd.k() + cEPS.k(), rstd.k())
            P.act(I("activation", rstd.c(0, n), rstd.c(0, n), AF.Exp, scale=-0.5), rstd.k(), rstd.k())

        def main_body():
            stage(1)
            for Sidx in range(2):
                nP = 1024
                NT = nP + (128 if Sidx == 1 else 0)
                NCH = NT // 128
                mark = scr_top[0]
                xst = [carve(1024, F32) for _ in range(2)]
                for c in range(2 if 'x2' in SKIP else NCH):
                    s_ = xst[c % 2]
                    if c < 8:
                        src = DAP(xp_d, (Sidx * 1024 + c * 128) * 1024, [[1024, 128], [1, 1024]])
                    else:
                        src = xs_d.ap()
                    P.dma("sp", s_.c(0, 1024), src, writes=s_.k(), dkey=("xin", c % 2))
                    for hf in range(2):
                        pb = ps_next()
                        for f4 in range(4):
                            fc = hf * 4 + f4
                            transpose(pb.c(f4 * 128, 128), s_.c(fc * 128, 128), 128, s_.k(), pb.k())
                        P.act(I("activation", x32.ap(hf * 4 * NTMAX + c * 128, [[NTMAX, 4], [1, 128]]),
                                                                        pb.ap(0, [[128, 4], [1, 128]]), AF.Copy), pb.k(), [k_ for f_ in range(hf * 4, hf * 4 + 4) for k_ in kx32(c * 128, f_)])
                        if 'x1' not in SKIP:
                            P.dve(I("tensor_copy", xb.ap(hf * 4 * NTMAX + c * 128, [[NTMAX, 4], [1, 128]]),
                                                                             pb.ap(0, [[128, 4], [1, 128]])), pb.k(), [k_ for f_ in range(hf * 4, hf * 4 + 4) for k_ in kxb(c * 128, f_)])
                scr_top[0] = mark
                for l in range(DEPTH):
                    layer(l, Sidx)
                mark = scr_top[0]
                yst = [carve(1024, F32) for _ in range(2)]
                for c in range(NCH):
                    s_ = yst[c % 2]
                    for hf in range(2):
                        pb = ps_next()
                        for f4 in range(4):
                            fc = hf * 4 + f4
                            transpose(pb.c(f4 * 128, 128), x32.c(fc * NTMAX + c * 128, 128), 128, kx32(c * 128, fc), pb.k())
                        evac_copy(s_.c(hf * 512, 512), pb.c(0, 512), pb.k(), s_.k())
                    if c < 8:
                        dst = DAP(yp_d, (Sidx * 1024 + c * 128) * 1024, [[1024, 128], [1, 1024]])
                    else:
                        dst = ys_d.ap()
                    P.dma("sp", dst, s_.c(0, 1024), reads=s_.k(), dkey=("yout", c % 2))
                scr_top[0] = mark

        try:
            main_body()
        except _Stop:
            pass

        P.finalize(st)
        with nc.Block() as block:
            P.emit(block)
    return nc


def _consts():
    p = np.arange(128)
    c = {}
    c["ident"] = np.eye(128, dtype=np.float32)
    c["tril"] = (p[:, None] <= p[None, :]).astype(np.float32)
    c["sblk"] = ((p[:, None] <= p[None, :]) & (p[:, None] // 8 == p[None, :] // 8)).astype(np.float32)
    c["hm4"] = (p[:, None] // 32 == np.arange(4)[None, :]).astype(np.float32)
    c["bm"] = (p[:, None] // 32 == (np.arange(256)[None, :] // 64)).astype(np.float32)
    rm = np.ones((128, 1152), np.float32)
    rm[:, 0:1024:128] = 0.0
    rm[:, 1024:1152:8] = 0.0
    c["rmask"] = rm.astype(ml_dtypes.bfloat16)
    sq = (np.arange(128)[None, :] // 8 == np.arange(16)[:, None]).astype(np.float32)
    c["seqm"] = np.broadcast_to(sq.reshape(1, 2048), (128, 2048)).astype(ml_dtypes.bfloat16)
    c["tokm"] = (p[:, None] // 8 == np.arange(16)[None, :]).astype(np.float32)
    return c


def _cvec(inp, l):
    def fm(v, nchunk):
        return np.asarray(v, np.float32).reshape(nchunk, 128).T
    cols = []
    aw = np.asarray(inp["a_conv_w"][l], np.float32)
    for j in range(2):
        for k in range(3):
            cols.append(aw[k, j * 128:(j + 1) * 128][:, None])
    cols.append(np.asarray(inp["b_gate_b"][l], np.float32)[:, None])
    cols.append(fm(inp["b_norm_g"][l], 2))
    cw = np.asarray(inp["c_conv_w"][l], np.float32)
    for j in range(2):
        for k in range(31):
            cols.append(cw[k, j * 128:(j + 1) * 128][:, None])
    cols.append(fm(inp["c_conv_b"][l], 2))
    cols.append(fm(inp["c_ln_g"][l], 2))
    cols.append(fm(inp["c_ln_b"][l], 2))
    cols.append(fm(inp["ln1_g"][l], 8))
    cols.append(fm(inp["ln1_b"][l], 8))
    cols.append(fm(inp["ln2_g"][l], 8))
    cols.append(fm(inp["ln2_b"][l], 8))
    out = np.concatenate(cols, axis=1)
    assert out.shape == (128, NCV), out.shape
    return out


def _dbt(inp):
    bs = np.asarray(inp["d_bs"], np.float32)
    out = np.zeros((2, 2, 2, 128, 128), np.float32)
    for l in range(2):
        for j in range(2):
            for gg in range(2):
                out[l, 0, j, 64 * gg:64 * gg + 64, :] = bs[l, 2 * j + gg][None, :]
                out[l, 1, j, 64 * gg:64 * gg + 64, :] = np.tile(bs[l, 2 * j + gg, :8], 16)[None, :]
    return np.ascontiguousarray(out.reshape(8, 128, 128))


def _dwst(inp):
    ws = np.asarray(inp["d_ws"], np.float32)
    out = np.zeros((2, 2, 4, 128, 128), np.float32)
    out[:, 0] = ws.transpose(0, 1, 3, 2)
    blk = ws[:, :, :8, :8].transpose(0, 1, 3, 2)
    for q in range(16):
        out[:, 1, :, 8 * q:8 * q + 8, 8 * q:8 * q + 8] = blk
    return np.ascontiguousarray(out.reshape(16, 128, 128))


_NC_CACHE = {}


def make_in_maps(inp):
    consts = _consts()
    f32 = lambda a: np.ascontiguousarray(a, dtype=np.float32)
    shared = {
        "w_in": f32(inp["w_in"]), "w_out": f32(inp["w_out"]), "w_ff1": f32(inp["w_ff1"]), "w_ff2": f32(inp["w_ff2"]),
        "cvec": f32(np.stack([_cvec(inp, l) for l in range(2)])),
        "gw2": f32(inp["b_gate_w2"]),
        "dln": f32(np.stack([inp["d_ln_g"], inp["d_ln_b"]], axis=1)),
        "dws": f32(inp["d_ws"]), "dbs": f32(inp["d_bs"]),
        "dbt": _dbt(inp), "dwst": _dwst(inp),
    }
    shared.update(consts)
    in_maps = []
    for i in range(8):
        m = dict(shared)
        m["xp"] = f32(inp["x_prompt"][i])
        m["xs"] = f32(inp["x_sample"][16 * i:16 * i + 16].reshape(128, 1024))
        m["sta"] = f32(inp["state_conv_a"][:, 16 * i:16 * i + 16].reshape(2, 32, 256))
        m["stg"] = f32(inp["state_gla"][:, 16 * i:16 * i + 16].reshape(2, 16, 8192))
        m["stc"] = f32(inp["state_conv_c"][:, 16 * i:16 * i + 16].reshape(2, 480, 256))
        in_maps.append(m)
    return in_maps


def kernel(**inp):
    inp = {k: np.asarray(v) for k, v in inp.items()}
    if "nc" not in _NC_CACHE:
        _NC_CACHE["nc"] = build_program()
    nc = _NC_CACHE["nc"]
    in_maps = make_in_maps(inp)
    res = run_bass_kernel_spmd(nc, in_maps, core_ids=list(range(8)))
    R = res.results
    y_prompt = np.stack([R[i]["yp"] for i in range(8)]).reshape(8, 2048, 1024)
    y_sample = np.concatenate([R[i]["ys"].reshape(16, 8, 1024) for i in range(8)], axis=0)
    na_p = np.stack([R[i]["nap"] for i in range(8)], axis=1).reshape(2, 8, 2, 256)
    na_s = np.concatenate([R[i]["nas"].reshape(2, 16, 2, 256) for i in range(8)], axis=1)
    ng_p = np.stack([R[i]["ngp"] for i in range(8)], axis=1).reshape(2, 8, 4, 32, 64)
    ng_s = np.concatenate([R[i]["ngs"].reshape(2, 16, 4, 32, 64) for i in range(8)], axis=1)
    nc_p = np.stack([R[i]["ncp"] for i in range(8)], axis=1).reshape(2, 8, 30, 256)
    nc_s = np.concatenate([R[i]["ncs"].reshape(2, 16, 30, 256) for i in range(8)], axis=1)
    nv_s = np.concatenate([R[i]["nvs"].reshape(2, 16, 8, 256) for i in range(8)], axis=1)
    outs = (y_prompt, y_sample, na_p, na_s, ng_p, ng_s, nc_p, nc_s, nv_s)
    return tuple(np.ascontiguousarray(o, dtype=np.float32) for o in outs)
```

```python
import os
import numpy as np
import ml_dtypes
SKIP = set()
from contextlib import ExitStack
import concourse.bass as bass
import concourse.mybir as mybir
from concourse.bass_utils import run_bass_kernel_spmd

F32 = mybir.dt.float32
BF16 = mybir.dt.bfloat16
ALU = mybir.AluOpType
AF = mybir.ActivationFunctionType
AX = mybir.AxisListType

DEPTH = 2
ALPHA = float((2 * DEPTH) ** 0.25)
EPS = 1e-5
NCV = 109
GR = 64
ENGS = ("pe", "act", "dve", "pool", "sp")
STRICT = True


def I(name, *a, **k):
    return lambda e: getattr(e, name)(*a, **k)


class Op:
    __slots__ = ("eng", "fn", "reads", "writes", "deps", "signal", "cnt", "dkey", "didx", "raw")

    def __init__(self, eng, fn, reads, writes, dkey):
        self.eng = eng
        self.fn = fn
        self.reads = reads
        self.writes = writes
        self.deps = []
        self.signal = False
        self.cnt = 0
        self.dkey = dkey
        self.didx = 0
        self.raw = set()


class Prog:
    def __init__(self, nc):
        self.nc = nc
        self.ops = {e: [] for e in ENGS}
        self.lw = {}
        self.rd = {}
        self.dkeys = {}
        self.group_keys = set()

    def add(self, eng, fn, reads=(), writes=(), dkey=None, group=False):
        ps_r = [r for r in reads if r[0] == "ps"]
        if ps_r:
            reads = [r for r in reads if r[0] != "ps"]
            writes = list(writes) + [r for r in ps_r if r not in writes]
        op = Op(eng, fn, tuple(reads), tuple(writes), dkey)
        deps = []
        for r in ps_r:
            w = self.lw.get(r)
            if w is not None:
                op.raw.add(id(w))
        for r in op.reads:
            w = self.lw.get(r)
            if w is not None:
                deps.append(w)
                op.raw.add(id(w))
        for w in op.writes:
            p = self.lw.get(w)
            if p is not None:
                deps.append(p)
            q = self.rd.get(w)
            if q:
                deps.extend(q)
        for r in op.reads:
            self.rd.setdefault(r, []).append(op)
        for w in op.writes:
            self.lw[w] = op
            self.rd[w] = []
        seen = set()
        for d in deps:
            if id(d) in seen or d is op:
                continue
            seen.add(id(d))
            op.deps.append(d)
        if dkey is not None:
            lst = self.dkeys.setdefault(dkey, [])
            lst.append(op)
            op.didx = len(lst)
            if group:
                self.group_keys.add(dkey)
        self.ops[eng].append(op)
        return op

    def pe(self, fn, reads=(), writes=()):
        return self.add("pe", fn, reads, writes)

    def act(self, fn, reads=(), writes=()):
        return self.add("act", fn, reads, writes)

    def dve(self, fn, reads=(), writes=()):
        return self.add("dve", fn, reads, writes)

    def pool(self, fn, reads=(), writes=()):
        return self.add("pool", fn, reads, writes)

    def dma(self, q, out, in_, reads=(), writes=(), dkey=None, group=False, **kw):
        return self.add(q, I("dma_start", out=out, in_=in_, **kw), reads, writes, dkey=dkey, group=group)

    def _needs_edge(self, op, d):
        if d.dkey is not None or op.dkey is not None:
            return True
        if d.eng != op.eng:
            return True
        if op.eng == "pe":
            return False
        return STRICT or id(d) in op.raw

    def finalize(self, stack):
        nc = self.nc
        for e in ENGS:
            for op in self.ops[e]:
                op.deps = [d for d in op.deps if self._needs_edge(op, d)]
                for d in op.deps:
                    if d.dkey is None:
                        d.signal = True
        self.esem = {}
        for e in ENGS:
            c = 0
            for op in self.ops[e]:
                if op.dkey is None and op.signal:
                    c += 1
                    op.cnt = c
            self.esem[e] = stack.enter_context(nc.semaphore("sem_" + e))
        self.dsem = {}
        for i, k in enumerate(self.dkeys):
            self.dsem[k] = stack.enter_context(nc.semaphore("dsem%d" % i))

    def emit(self, block):
        emap = {"pe": "tensor", "act": "scalar", "dve": "vector", "pool": "gpsimd", "sp": "sync"}
        prog = self

        def mk(ename):
            def body(eng):
                waited = {}
                for op in prog.ops[ename]:
                    need = {}
                    for d in op.deps:
                        if d.dkey is not None:
                            s = prog.dsem[d.dkey]
                            if d.dkey in prog.group_keys:
                                v = 16 * len(prog.dkeys[d.dkey])
                            else:
                                v = 16 * d.didx
                            k = ("d", d.dkey)
                        else:
                            s = prog.esem[d.eng]
                            v = d.cnt
                            k = ("e", d.eng)
                        if need.get(k, (None, 0))[1] < v:
                            need[k] = (s, v)
                    for k, (s, v) in need.items():
                        if waited.get(k, 0) >= v:
                            continue
                        eng.wait_ge(s, v)
                        waited[k] = v
                    ins = op.fn(eng)
                    if op.dkey is not None:
                        ins.then_inc(prog.dsem[op.dkey], 16)
                    elif op.signal:
                        ins.then_inc(prog.esem[ename], 1)
                if ename == "sp":
                    for k, lst in prog.dkeys.items():
                        v = 16 * len(lst)
                        if waited.get(("d", k), 0) < v:
                            eng.wait_ge(prog.dsem[k], v)
            return body

        for ename in ENGS:
            if not prog.ops[ename] and ename != "sp":
                continue
            getattr(block, emap[ename])(mk(ename))


class Buf:
    def __init__(self, t, pstride, off, n, esz, ns, boff):
        self.t = t
        self.pstride = pstride
        self.off = off
        self.n = n
        self.esz = esz
        self.ns = ns
        self.boff = boff

    def k(self, lo=0, hi=None):
        if hi is None:
            hi = self.n
        b0 = self.boff + lo * self.esz
        b1 = self.boff + hi * self.esz
        return [(self.ns, g) for g in range(b0 // GR, (b1 - 1) // GR + 1)]

    def ap(self, off, dims, parts=128, p0=0):
        return bass.AP(self.t, p0 * self.pstride + self.off + off, [[self.pstride, parts]] + [list(d) for d in dims])

    def c(self, c0, n, parts=128, p0=0):
        return self.ap(c0, [[1, n]], parts, p0)

    def sub(self, lo, n):
        return Buf(self.t, self.pstride, self.off + lo, n, self.esz, self.ns, self.boff + lo * self.esz)


class _Stop(Exception):
    pass


STAGE_LIMIT = [99]


def stage(n):
    if n > STAGE_LIMIT[0]:
        raise _Stop()


def build_program():
    nc = bass.Bass("TRN2", target_bir_lowering=False)

    def din(name, shape, dt=F32):
        return nc.dram_tensor(name, shape, dt, kind="ExternalInput")

    def dout(name, shape):
        return nc.dram_tensor(name, shape, F32, kind="ExternalOutput")

    xp_d = din("xp", [2048, 1024])
    xs_d = din("xs", [128, 1024])
    sta_d = din("sta", [2, 32, 256])
    stg_d = din("stg", [2, 16, 8192])
    stc_d = din("stc", [2, 480, 256])
    win_d = din("w_in", [2, 1024, 2576])
    wout_d = din("w_out", [2, 1024, 1024])
    wff1_d = din("w_ff1", [2, 1024, 4096])
    wff2_d = din("w_ff2", [2, 4096, 1024])
    cvec_d = din("cvec", [2, 128, NCV])
    gw2_d = din("gw2", [2, 16, 128])
    dln_d = din("dln", [2, 2, 256])
    dws_d = din("dws", [2, 4, 128, 128])
    dbs_d = din("dbs", [2, 4, 128])
    ident_d = din("ident", [128, 128])
    tril_d = din("tril", [128, 128])
    hm4_d = din("hm4", [128, 4])
    bm_d = din("bm", [128, 256])
    rmask_d = din("rmask", [128, 1152], BF16)
    seqm_d = din("seqm", [128, 2048], BF16)
    tokm_d = din("tokm", [128, 16])
    dbt_d = din("dbt", [8, 128, 128])
    dwst_d = din("dwst", [16, 128, 128])

    yp_d = dout("yp", [2048, 1024])
    ys_d = dout("ys", [128, 1024])
    nap_d = dout("nap", [2, 2, 256])
    nas_d = dout("nas", [2, 32, 256])
    ngp_d = dout("ngp", [2, 8192])
    ngs_d = dout("ngs", [2, 16, 8192])
    ncp_d = dout("ncp", [2, 30, 256])
    ncs_d = dout("ncs", [2, 480, 256])
    nvs_d = dout("nvs", [2, 128, 256])

    def DAP(t, off, dims):
        return bass.AP(t, off, [list(d) for d in dims])

    NTMAX = 1152
    st = ExitStack()
    with st:
        P = Prog(nc)

        def mk(name, n, dt):
            t = st.enter_context(nc.sbuf_tensor("s_" + name, [128, n], dt))
            esz = 4 if dt == F32 else 2
            return Buf(t, n, 0, n, esz, name, 0)

        x32 = mk("x32", 8 * NTMAX, F32)
        xb = mk("xb", 8 * NTMAX, BF16)
        WSLOT = 4224
        wring = [mk("wr%d" % i, WSLOT, BF16) for i in range(3)]
        ident32 = mk("ident32", 128, F32)
        identb = mk("identb", 128, BF16)
        onesb = mk("onesb", 128, BF16)
        tril = mk("tril", 128, F32)
        hm4 = mk("hm4", 4, F32)
        bmk = mk("bmk", 256, F32)
        rmask = mk("rmask", NTMAX, BF16)
        seqm = mk("seqm", 2048, BF16)
        tokm = mk("tokm", 16, F32)
        trilb = mk("trilb", 128, BF16)
        sblkb = mk("sblkb", 128, BF16)
        cvec = mk("cvec", 2 * NCV, F32)
        ngb = mk("ngb", 2, F32)
        gw2 = mk("gw2", 2 * 128, F32)
        dlng = mk("dlng", 2 * 2 * 256, F32)
        dbias = mk("dbias", 2 * 2 * 2 * 128, F32)
        wsT = mk("wsT", 2 * 2 * 4 * 128, BF16)
        stA = mk("stA", 2 * 2 * 2, BF16)
        stC = mk("stC", 2 * 2 * 30, BF16)
        Sbd32 = mk("Sbd32", 2 * 256, F32)
        SCRB = 100 * 1024
        scr_t = st.enter_context(nc.sbuf_tensor("scr", [128, SCRB // 2], BF16))
        scr_h = {BF16: scr_t, F32: scr_t.bitcast(F32)}
        scr_top = [0]

        def carve(n, dt):
            esz = 4 if dt == F32 else 2
            b0 = (scr_top[0] + 63) // 64 * 64
            scr_top[0] = b0 + n * esz
            assert scr_top[0] <= SCRB, ("scratch overflow", scr_top[0])
            return Buf(scr_h[dt], SCRB // esz, b0 // esz, n, esz, "scr", b0)

        def alias(buf, n, dt, byte_off=0):
            esz = 4 if dt == F32 else 2
            b0 = buf.boff + byte_off
            assert b0 % esz == 0 and byte_off + n * esz <= buf.n * buf.esz
            return Buf(scr_h[dt], SCRB // esz, b0 // esz, n, esz, "scr", b0)

        pst = st.enter_context(nc.psum_tensor("pst", [128, 4096], F32))
        pst_bf = pst.bitcast(BF16)
        ps_ctr = [0]

        def psbank(i):
            return Buf(pst, 4096, 512 * i, 512, 4, "ps", 2048 * i)

        def psbank_bf(i):
            return Buf(pst_bf, 8192, 1024 * i, 1024, 2, "ps", 2048 * i)

        ps_reserved = set()

        def ps_next(bf=False):
            while True:
                i = ps_ctr[0] % 8
                ps_ctr[0] += 1
                if i not in ps_reserved:
                    break
            return psbank_bf(i) if bf else psbank(i)

        def ps_reserve():
            while True:
                i = ps_ctr[0] % 8
                ps_ctr[0] += 1
                if i not in ps_reserved:
                    break
            ps_reserved.add(i)
            return psbank(i), i

        def ps_next2():
            if ps_ctr[0] % 2:
                ps_ctr[0] += 1
            i = ps_ctr[0] % 8
            ps_ctr[0] += 2
            return Buf(pst, 4096, 512 * i, 1024, 4, "ps", 2048 * i)

        def kx32(t0, fc=None):
            if fc is None:
                return [("x32", f, t0 // 512) for f in range(8)]
            return [("x32", fc, t0 // 512)]

        def kxb(t0, fc=None):
            if fc is None:
                return [("xb", f, t0 // 512) for f in range(8)]
            return [("xb", fc, t0 // 512)]

        ev_ctr = [0]

        def evac_copy(out_ap, in_ap, reads, writes, eng=None):
            if eng is None:
                eng = "act" if ev_ctr[0] % 2 == 0 else "dve"
                ev_ctr[0] += 1
            if eng == "act":
                P.act(I("activation", out_ap, in_ap, AF.Copy), reads, writes)
            else:
                P.dve(I("tensor_copy", out_ap, in_ap), reads, writes)

        def transpose(out_ps_ap, in_ap, n_in_parts, reads, writes, bf=False):
            idn = identb if bf else ident32
            ida = idn.ap(0, [[1, n_in_parts]], parts=n_in_parts)
            P.pe(I("transpose", out_ps_ap, in_ap, ida), list(reads) + idn.k(), writes)

        wtiles = []
        for S in range(2):
            for l in range(2):
                for (c0, c1) in [(0, 512), (512, 1024), (1536, 2064), (1024, 1536), (2064, 2576)]:
                    wtiles.append(("in", l, c0, c1))
                for c0 in (0, 512):
                    wtiles.append(("out", l, c0, c0 + 512))
                for g in range(8):
                    wtiles.append(("ff1", l, 512 * g, 512 * g + 512))
                for oc in range(8):
                    wtiles.append(("ff2", l, 128 * oc, 128 * oc + 128))
        w_issued = [0]
        w_next = [0]

        def w_issue(gi):
            kind, l, c0, c1 = wtiles[gi]
            nco = c1 - c0
            slot = wring[gi % 3]
            if kind == "ff2":
                src = DAP(wff2_d, l * 4096 * 1024 + c0, [[1024, 128], [128 * 1024, 32], [1, nco]])
                dst = slot.ap(0, [[nco, 32], [1, nco]])
            else:
                dt_, ncols = {"in": (win_d, 2576), "out": (wout_d, 1024), "ff1": (wff1_d, 4096)}[kind]
                src = DAP(dt_, l * 1024 * ncols + c0, [[ncols, 128], [128 * ncols, 8], [1, nco]])
                dst = slot.ap(0, [[nco, 8], [1, nco]])
            P.dma("pool", dst, src, writes=slot.k(), dkey=("w", gi % 3))

        def w_acquire(kind, l, ahead=2):
            gi = w_next[0]
            while not (wtiles[gi][0] == kind and wtiles[gi][1] == l):
                gi += 1
            w_next[0] = gi + 1
            while w_issued[0] < min(len(wtiles), gi + 1 + ahead):
                w_issue(w_issued[0])
                w_issued[0] += 1
            nco = wtiles[gi][3] - wtiles[gi][2]
            return wring[gi % 3], nco

        def cload(buf, src, parts=128, dims=None):
            P.dma("sp", buf.ap(0, dims if dims else [[1, buf.n]], parts=parts), src, writes=buf.k(), dkey="c", group=True)

        cload(ident32, ident_d.ap())
        cload(tril, tril_d.ap())
        cload(hm4, hm4_d.ap())
        cload(bmk, bm_d.ap())
        cload(rmask, rmask_d.ap())
        cload(seqm, seqm_d.ap())
        cload(tokm, tokm_d.ap())
        if 'v' not in SKIP:
          cload(cvec, DAP(cvec_d, 0, [[NCV, 128], [128 * NCV, 2], [1, NCV]]), dims=[[NCV, 2], [1, NCV]])
        if 'g' not in SKIP:
          cload(gw2, DAP(gw2_d, 0, [[128, 16], [16 * 128, 2], [1, 128]]), parts=16, dims=[[128, 2], [1, 128]])
        if 'b' not in SKIP:
            cload(dlng, DAP(dln_d, 0, [[0, 128], [1, 1024]]))
        cload(dbias, DAP(dbt_d, 0, [[128, 128], [128 * 128, 8], [1, 128]]), dims=[[128, 8], [1, 128]])
        P.dve(I("tensor_copy", identb.c(0, 128), ident32.c(0, 128)), ident32.k(), identb.k())
        P.dve(I("memset", onesb.c(0, 128), 1.0), (), onesb.k())
        P.dve(I("tensor_copy", trilb.c(0, 128), tril.c(0, 128)), tril.k(), trilb.k())
        for l in range(2):
            P.dve(I("tensor_scalar", ngb.c(l, 1), cvec.c(l * NCV + 6, 1), -1.0, None, ALU.mult),
                  cvec.k(), ngb.k())
        P.dve(I("memset", Sbd32.c(0, 512), 0.0), (), Sbd32.k())
        P.dve(I("memset", stA.c(0, 8), 0.0), (), stA.k())
        P.dve(I("memset", stC.c(0, 120), 0.0), (), stC.k())

        mark = scr_top[0]
        wst = carve(16 * 128, F32)
        P.dma("sp", wst.ap(0, [[128, 16], [1, 128]]), DAP(dwst_d, 0, [[128, 128], [128 * 128, 16], [1, 128]]),
              writes=wst.k(), dkey="c", group=True)
        P.dve(I("tensor_tensor", wsT.ap(0, [[128, 16], [1, 128]]), wst.ap(0, [[128, 16], [1, 128]]),
                                        tril.ap(0, [[0, 16], [1, 128]]), ALU.mult), wst.k() + tril.k(), wsT.k())
        scr_top[0] = mark

        def layer(l, Sidx):
            nP = 1024
            has_s = Sidx == 1
            NT = nP + (128 if has_s else 0)
            TT = [(0, 512, "p"), (512, 512, "p")] + ([(1024, 128, "s")] if has_s else [])
            NCH = NT // 128
            cv = lambda col: cvec.c(l * NCV + col, 1)
            mark0 = scr_top[0]
            ymix = carve(8 * NT, BF16)

            def xb_t(kc, t0, n):
                return xb.c(kc * NTMAX + t0, n)

            def mm_fm(wb, nco, cc0, ccn, t0, n, pb):
                for kc in range(8):
                    P.pe(I("matmul", pb.c(0, n, parts=ccn), wb.ap(kc * nco + cc0, [[1, ccn]]), xb_t(kc, t0, n),
                                                    start=(kc == 0), stop=(kc == 7)),
                         wb.k() + kxb(t0, kc), pb.k())

            def cview(buf, j, L, H, t0, n, kind, shift=0):
                if kind == "p":
                    return buf.ap(j * L + t0 + shift, [[1, n]])
                return buf.ap(j * L + H + nP + shift, [[H + 8, 16], [1, 8]])

            def kcv(buf, j, L, H, t0, n, kind, lo, hi):
                if kind == "p":
                    return buf.k(j * L + t0 + lo, j * L + t0 + n + hi)
                return buf.k(j * L + H + nP, j * L + H + nP + 16 * (H + 8))

            def tview(buf, base, t0, n, kind):
                if kind == "p":
                    return buf.ap(base + t0, [[1, n]])
                return buf.ap(base + t0, [[8, 16], [1, 8]])

            stage(2)
            LA = 2 + nP + (160 if has_s else 0)
            mA = scr_top[0]
            ab32 = carve(2 * NT, F32)
            ac32 = carve(2 * NT, F32)
            gAb = carve(2 * LA, BF16)
            diagA = carve(6 * 128, BF16)
            gst = carve(2 * 34, F32)
            for j in range(2):
                P.dve(I("tensor_tensor", diagA.ap(j * 3 * 128, [[128, 3], [1, 128]]), ident32.ap(0, [[0, 3], [1, 128]]),
                        cvec.ap(l * NCV + j * 3, [[1, 3], [0, 128]]), ALU.mult), ident32.k() + cvec.k(), diagA.k())
                P.dve(I("tensor_copy", gAb.c(j * LA, 2), stA.c((l * 2 + j) * 2, 2)), stA.k(), gAb.k(j * LA, j * LA + 2))
            if has_s:
                sta_st = carve(256, F32)
                P.dma("sp", sta_st.ap(0, [[1, 256]], parts=32), DAP(sta_d, l * 32 * 256, [[256, 32], [1, 256]]),
                      writes=sta_st.k(), dkey="ld", group=False)
                for j in range(2):
                    pb = ps_next()
                    transpose(pb.c(0, 32), sta_st.ap(j * 128, [[1, 128]], parts=32), 32, sta_st.k(), pb.k())
                    P.act(I("activation", gAb.ap(j * LA + 2 + nP, [[10, 16], [1, 2]]),
                                                             pb.ap(0, [[2, 16], [1, 2]]), AF.Copy), pb.k(), gAb.k())
            wb, nco = w_acquire("in", l)
            for ci, dst in ((0, ab32), (1, ab32), (2, ac32), (3, ac32)):
                j = ci % 2
                for (t0, n, kind) in TT:
                    pb = ps_next()
                    mm_fm(wb, nco, ci * 128, 128, t0, n, pb)
                    evac_copy(dst.c(j * NT + t0, n), pb.c(0, n), pb.k(), dst.k(j * NT + t0, j * NT + t0 + n))
            wb2, nco2 = w_acquire("in", l)
            for j in range(2):
                for (t0, n, kind) in TT:
                    pb = ps_next()
                    mm_fm(wb2, nco2, j * 128, 128, t0, n, pb)
                    P.dve(I("tensor_tensor",
                        cview(gAb, j, LA, 2, t0, n, kind, shift=2 if kind == "p" else 2), tview(pb, 0, 0, n, kind),
                        tview(ac32, j * NT, t0, n, kind), ALU.mult), pb.k() + ac32.k(j * NT + t0, j * NT + t0 + n), kcv(gAb, j, LA, 2, t0, n, kind, 2, 2))
                    if kind == "p" and Sidx == 1 and t0 == 512:
                        P.dve(I("tensor_tensor", gst.c(j * 34, 2), pb.c(510, 2),
                                                                    ac32.c(j * NT + 1022, 2), ALU.mult),
                              pb.k() + ac32.k(), gst.k())
                    if kind == "s":
                        P.dve(I("tensor_tensor",
                            gst.ap(j * 34 + 2, [[2, 16], [1, 2]]), pb.ap(6, [[8, 16], [1, 2]]),
                            ac32.ap(j * NT + 1024 + 6, [[8, 16], [1, 2]]), ALU.mult), pb.k() + ac32.k(), gst.k())
            if Sidx == 0:
                for j in range(2):
                    P.dve(I("tensor_copy", stA.c((l * 2 + j) * 2, 2), gAb.c(j * LA + nP, 2)), gAb.k(j * LA + nP, j * LA + nP + 2), stA.k())
            for j in range(2):
                for (t0, n, kind) in TT:
                    pb = ps_next()
                    for k in range(3):
                        P.pe(I("matmul",
                            tview(pb, 0, 0, n, kind), diagA.c((j * 3 + k) * 128, 128), cview(gAb, j, LA, 2, t0, n, kind, shift=k),
                            start=(k == 0), stop=(k == 2)), diagA.k() + kcv(gAb, j, LA, 2, t0, n, kind, 0, 2), pb.k())
                    P.dve(I("tensor_tensor",
                        ymix.c((0 + j) * NT + t0, n), pb.c(0, n), ab32.c(j * NT + t0, n), ALU.mult),
                        pb.k() + ab32.k(j * NT + t0, j * NT + t0 + n), ymix.k(j * NT + t0, j * NT + t0 + n))
            if has_s:
                stg_a = carve(256, F32)
                for j in range(2):
                    pb = ps_next()
                    transpose(pb.c(0, 128, parts=34), gst.c(j * 34, 34), 128, gst.k(), pb.k())
                    P.act(I("activation", stg_a.c(j * 128, 128, parts=34), pb.c(0, 128, parts=34), AF.Copy),
                          pb.k(), stg_a.k())
                P.dma("sp", DAP(nap_d, l * 512, [[256, 2], [1, 256]]), stg_a.c(0, 256, parts=2), reads=stg_a.k(), dkey=("o", 1))
                P.dma("sp", DAP(nas_d, l * 32 * 256, [[256, 32], [1, 256]]), stg_a.c(0, 256, parts=32, p0=2),
                      reads=stg_a.k(), dkey=("o", 2))
            scr_top[0] = mA

            stage(3)
            mB = scr_top[0]
            qeT = carve(NT, BF16)
            keT = carve(NT, BF16)
            gsb = carve(2 * NT, BF16)
            vtm = carve(NCH * 256, BF16)
            ketm = carve(NCH * 128, BF16)
            elast = carve(32, F32)
            if has_s:
                S0c = carve(16 * 64, F32)
                S0bdb = carve(16 * 256, BF16)
                qeTs = carve(16 * 128, BF16)
                ketms = carve(16 * 128, BF16)
            mB2 = scr_top[0]
            B1 = carve(NT, F32)
            B2 = carve(NT, F32)
            B3 = carve(NT, F32)
            B4 = carve(NT, F32)
            zlr = carve(NT, F32)
            wb4, nco4 = w_acquire("in", l, ahead=1)
            for (t0, n, kind) in TT:
                pb = ps_next()
                mm_fm(wb4, nco4, 0, 16, t0, n, pb)
                evac_copy(zlr.c(t0, n, parts=16), pb.c(0, n, parts=16), pb.k(), zlr.k(t0, t0 + n))
            for (t0, n, kind) in TT:
                pb = ps_next()
                P.pe(I("matmul", pb.c(0, n), gw2.c(l * 128, 128, parts=16), zlr.c(t0, n, parts=16),
                       start=True, stop=True), gw2.k() + zlr.k(t0, t0 + n), pb.k())
                P.act(I("activation", B3.c(t0, n), pb.c(0, n), AF.Exp, bias=ngb.c(l, 1), scale=-1.0),
                      pb.k() + ngb.k(), B3.k(t0, t0 + n))
            P.act(I("activation", B3.c(0, NT), B3.c(0, NT), AF.Ln, bias=1.0, scale=1.0), B3.k(), B3.k())
            P.dve(I("tensor_tensor_scan", B4.c(0, NT), rmask.c(0, NT), B3.c(0, NT), 0.0, ALU.mult, ALU.add),
                  rmask.k() + B3.k(), B4.k())
            P.act(I("activation", B3.c(0, NT), B4.c(0, NT), AF.Exp, scale=-1.0 / 16.0), B4.k(), B3.k())
            P.act(I("activation", B4.c(0, NT), B4.c(0, NT), AF.Exp, scale=1.0 / 16.0), B4.k(), B4.k())
            P.dve(I("tensor_copy", elast.ap(0, [[1, 8]]), B3.ap(127, [[128, 8]])), B3.k(), elast.k())
            if has_s:
                P.dve(I("tensor_copy", elast.ap(8, [[1, 16]]), B3.ap(1024 + 7, [[8, 16]])), B3.k(), elast.k())
            for ci, dst in ((2, B1), (3, B2)):
                for (t0, n, kind) in TT:
                    pb = ps_next()
                    mm_fm(wb2, nco2, ci * 128, 128, t0, n, pb)
                    evac_copy(dst.c(t0, n), pb.c(0, n), pb.k(), dst.k(t0, t0 + n))
            P.dve(I("scalar_tensor_tensor", qeT.c(0, NT), B1.c(0, NT), float(32 ** -0.5), B3.c(0, NT), ALU.mult, ALU.mult),
                  B1.k() + B3.k(), qeT.k())
            P.dve(I("tensor_tensor", keT.c(0, NT), B2.c(0, NT), B4.c(0, NT), ALU.mult), B2.k() + B4.k(), keT.k())
            wb3, nco3 = w_acquire("in", l, ahead=1)
            for c in range(NCH):
                pb = ps_next()
                for kc in range(8):
                    P.pe(I("matmul", pb.c(0, 256), xb_t(kc, c * 128, 128), wb3.ap(kc * nco3, [[1, 256]]),
                           start=(kc == 0), stop=(kc == 7)), wb3.k() + kxb(c * 128, kc), pb.k())
                evac_copy(vtm.c(c * 256, 256), pb.c(0, 256), pb.k(), vtm.k(c * 256, c * 256 + 256))
            for j in range(2):
                for (t0, n, kind) in TT:
                    pb = ps_next()
                    mm_fm(wb3, nco3, 256 + j * 128, 128, t0, n, pb)
                    P.act(I("activation", gsb.c(j * NT + t0, n), pb.c(0, n), AF.Silu),
                          pb.k(), gsb.k(j * NT + t0, j * NT + t0 + n))
            scr_top[0] = mB2
            NPC = 8
            keTm_all = carve(4 * NT, BF16)
            attm_all = carve(NCH * 512, BF16)
            t1_all = carve(NCH * 256, F32)
            Sbdb_all = carve((NPC + 1) * 256, BF16)
            osb_all = carve(NCH * 256, F32)
            on_all = alias(attm_all, NCH * 256, BF16)
            ssa = carve(NCH * 8, F32)
            atmp = [carve(512, BF16) for _ in range(2)]
            ytmp = [carve(128, BF16) for _ in range(3)]
            if has_s:
                t1s = alias(keTm_all, 4 * 256, F32)
                reds = alias(keTm_all, 4 * 64, F32, byte_off=4096)
            for h in range(4):
                P.dve(I("tensor_scalar", keTm_all.c(h * NT, NT), keT.c(0, NT), hm4.c(h, 1), None, ALU.mult),
                      keT.k() + hm4.k(), keTm_all.k(h * NT, h * NT + NT))
            if has_s:
                P.dma("sp", S0c.ap(0, [[64, 16], [1, 64]]), DAP(stg_d, l * 16 * 8192, [[64, 128], [8192, 16], [1, 64]]),
                      writes=S0c.k(), dkey="ld2", group=False)
                P.dve(I("tensor_tensor", S0bdb.ap(0, [[256, 16], [64, 4], [1, 64]]), S0c.ap(0, [[64, 16], [0, 4], [1, 64]]),
                        hm4.ap(0, [[0, 16], [1, 4], [0, 64]]), ALU.mult), S0c.k() + hm4.k(), S0bdb.k())
            for c in range(NCH):
                c0 = c * 128
                pbt = ps_next(bf=True)
                transpose(pbt.c(0, 128), keT.c(c0, 128), 128, keT.k(c0, c0 + 128), pbt.k(), bf=True)
                evac_copy(ketm.c(c * 128, 128), pbt.c(0, 128), pbt.k(), ketm.k(c * 128, c * 128 + 128), eng="act")
            for c in range(NCH):
                kind = "p" if c < 8 else "s"
                c0 = c * 128
                pa = ps_next()
                for h in range(4):
                    P.pe(I("matmul", pa.c(h * 128, 128), keTm_all.c(h * NT + c0, 128), qeT.c(c0, 128), start=True, stop=True),
                         keTm_all.k(h * NT, h * NT + NT) + qeT.k(c0, c0 + 128), pa.k())
                mk_ = trilb if kind == "p" else sblkb
                ta = atmp[c % 2]
                P.act(I("activation", ta.c(0, 512), pa.c(0, 512), AF.Copy), pa.k(), ta.k())
                P.dve(I("tensor_tensor", attm_all.ap(c * 512, [[128, 4], [1, 128]]), ta.ap(0, [[128, 4], [1, 128]]),
                        mk_.ap(0, [[0, 4], [1, 128]]), ALU.mult), ta.k() + mk_.k(), attm_all.k(c * 512, c * 512 + 512))
                if kind == "p":
                    pd = ps_next()
                    P.pe(I("matmul", pd.c(0, 256), ketm.c(c * 128, 128), vtm.c(c * 256, 256), start=True, stop=True),
                         ketm.k(c * 128, c * 128 + 128) + vtm.k(c * 256, c * 256 + 256), pd.k())
                    P.dve(I("scalar_tensor_tensor", t1_all.c(c * 256, 256), pd.c(0, 256), elast.c(c, 1), bmk.c(0, 256),
                            ALU.mult, ALU.mult), pd.k() + elast.k() + bmk.k(), t1_all.k(c * 256, c * 256 + 256))
            P.dve(I("tensor_copy", Sbdb_all.c(0, 256), Sbd32.c(l * 256, 256)), Sbd32.k(), Sbdb_all.k(0, 256))
            for c in range(NPC):
                P.dve(I("scalar_tensor_tensor", Sbd32.c(l * 256, 256), Sbd32.c(l * 256, 256), elast.c(c, 1), t1_all.c(c * 256, 256),
                        ALU.mult, ALU.add), Sbd32.k() + elast.k() + t1_all.k(c * 256, c * 256 + 256), Sbd32.k())
                P.dve(I("tensor_copy", Sbdb_all.c((c + 1) * 256, 256), Sbd32.c(l * 256, 256)), Sbd32.k(),
                      Sbdb_all.k((c + 1) * 256, (c + 2) * 256))
            if Sidx == 1:
                for h in range(4):
                    P.dma("sp", DAP(ngp_d, l * 8192 + h * 2048, [[64, 32], [1, 64]]),
                          Sbd32.c(l * 256 + h * 64, 64, parts=32, p0=32 * h), reads=Sbd32.k(), dkey=("o", 3))
            for c in range(NCH):
                kind = "p" if c < 8 else "s"
                c0 = c * 128
                po = ps_next()
                if kind == "p":
                    P.pe(I("matmul", po.c(0, 256), qeT.c(c0, 128), Sbdb_all.c(c * 256, 256), start=True, stop=False, skip_group_check=True),
                         qeT.k(c0, c0 + 128) + Sbdb_all.k(c * 256, c * 256 + 256), po.k())
                else:
                    P.dve(I("tensor_tensor", qeTs.ap(0, [[128, 16], [1, 128]]), qeT.ap(c0, [[0, 16], [1, 128]]),
                            seqm.ap(0, [[128, 16], [1, 128]]), ALU.mult), qeT.k(c0, c0 + 128) + seqm.k(), qeTs.k())
                    for q in range(16):
                        P.pe(I("matmul", po.c(0, 256), qeTs.c(q * 128, 128), S0bdb.c(q * 256, 256),
                               start=(q == 0), stop=False, skip_group_check=True), qeTs.k() + S0bdb.k(), po.k())
                for h in range(4):
                    P.pe(I("matmul", po.c(h * 64, 64), attm_all.c(c * 512 + h * 128, 128), vtm.c(c * 256 + h * 64, 64),
                           start=False, stop=(h == 3), skip_group_check=True),
                         attm_all.k(c * 512, c * 512 + 512) + vtm.k(c * 256, c * 256 + 256), po.k())
                evac_copy(osb_all.c(c * 256, 256), po.c(0, 256), po.k(), osb_all.k(c * 256, c * 256 + 256), eng="act")
            NO = NCH * 256
            P.act(I("activation", t1_all.c(0, NO), osb_all.c(0, NO), AF.Square), osb_all.k(), t1_all.k())
            P.dve(I("tensor_reduce", ssa.c(0, NCH * 4), t1_all.ap(0, [[64, NCH * 4], [1, 64]]), AX.X, ALU.add), t1_all.k(), ssa.k())
            P.act(I("activation", ssa.c(NCH * 4, NCH * 4), ssa.c(0, NCH * 4), AF.Ln, bias=cEPS.c(0, 1), scale=1.0 / 64.0),
                  ssa.k() + cEPS.k(), ssa.k())
            P.act(I("activation", ssa.c(0, NCH * 4), ssa.c(NCH * 4, NCH * 4), AF.Exp, scale=-0.5), ssa.k(), ssa.k())
            P.dve(I("tensor_tensor", on_all.ap(0, [[64, NCH * 4], [1, 64]]), osb_all.ap(0, [[64, NCH * 4], [1, 64]]),
                    ssa.ap(0, [[1, NCH * 4], [0, 64]]), ALU.mult), osb_all.k() + ssa.k(), on_all.k())
            for c in range(NCH):
                c0 = c * 128
                for j in range(2):
                    pt = ps_next(bf=True)
                    transpose(pt.c(0, 128), on_all.c(c * 256 + j * 128, 128), 128, on_all.k(), pt.k(), bf=True)
                    yt = ytmp[(2 * c + j) % 3]
                    P.act(I("activation", yt.c(0, 128), pt.c(0, 128), AF.Copy, scale=cv(7 + j)), pt.k() + cvec.k(), yt.k())
                    P.dve(I("tensor_tensor", ymix.c((2 + j) * NT + c0, 128), yt.c(0, 128), gsb.c(j * NT + c0, 128), ALU.mult),
                          yt.k() + gsb.k(j * NT + c0, j * NT + c0 + 128), ymix.k((2 + j) * NT + c0, (2 + j) * NT + c0 + 128))
            if has_s:
                c = 8
                P.dve(I("tensor_tensor", ketms.ap(0, [[128, 16], [1, 128]]), ketm.ap(c * 128, [[0, 16], [1, 128]]),
                        tokm.ap(0, [[1, 16], [0, 128]]), ALU.mult), ketm.k(c * 128, c * 128 + 128) + tokm.k(), ketms.k())
                for rd in range(4):
                    pd2 = ps_next2()
                    for qq in range(4):
                        q = rd * 4 + qq
                        P.pe(I("matmul", pd2.c(qq * 256, 256), ketms.c(q * 128, 128), vtm.c(c * 256, 256), start=True, stop=True),
                             ketms.k() + vtm.k(c * 256, c * 256 + 256), pd2.k())
                    for h in range(4):
                        P.dve(I("tensor_tensor", reds.ap(0, [[64, 4], [1, 64]], parts=32, p0=32 * h),
                                pd2.ap(64 * h, [[256, 4], [1, 64]], parts=32, p0=32 * h),
                                S0c.ap(rd * 256, [[64, 4], [1, 64]], parts=32, p0=32 * h), ALU.add),
                              pd2.k() + S0c.k(), reds.k())
                    P.dve(I("tensor_tensor", S0c.ap(rd * 256, [[64, 4], [1, 64]]), reds.ap(0, [[64, 4], [1, 64]]),
                            elast.ap(8 + rd * 4, [[1, 4], [0, 64]]), ALU.mult), reds.k() + elast.k(), S0c.k())
                P.dma("sp", DAP(ngs_d, l * 16 * 8192, [[64, 128], [8192, 16], [1, 64]]), S0c.ap(0, [[64, 16], [1, 64]]),
                      reads=S0c.k(), dkey=("o", 4))
            scr_top[0] = mB

            stage(4)
            LC = 30 + nP + (16 * 38 if has_s else 0)
            mC = scr_top[0]
            ca32 = carve(2 * NT, F32)
            cv32 = alias(ca32, 2 * NT, F32)
            gCb = carve(2 * LC, BF16)
            diagC = carve(62 * 128, BF16)
            cst = carve(2 * 158, F32)
            sgt2 = [carve(512, F32) for _ in range(2)]
            sg_i = [0]
            cset = (carve(2 * 512, BF16), carve(2 * 512, BF16), carve(512, F32), carve(512, F32), carve(512, F32))
            ctmp = sgt2
            u16 = carve(2 * NT, BF16)
            vv32 = carve(NCH * 256, F32)
            vvm = carve(NCH * 512, BF16)
            stt = carve(NCH * 8, F32)
            mv = carve(NCH * 2, F32)
            rsd = carve(NCH, F32)
            nmr = carve(NCH, F32)
            ftm = [carve(128, F32) for _ in range(2)]
            for j in range(2):
                P.dve(I("tensor_tensor", diagC.ap(j * 31 * 128, [[128, 31], [1, 128]]), ident32.ap(0, [[0, 31], [1, 128]]),
                        cvec.ap(l * NCV + 9 + j * 31, [[1, 31], [0, 128]]), ALU.mult),
                      ident32.k() + cvec.k(), diagC.k(j * 31 * 128, (j + 1) * 31 * 128))
                P.dve(I("tensor_copy", gCb.c(j * LC, 30), stC.c((l * 2 + j) * 30, 30)), stC.k(), gCb.k(j * LC, j * LC + 30))
            P.dve(I("memset", vvm.c(0, NCH * 512), 0.0), (), vvm.k())
            if has_s:
                stc_st = carve(4 * 256, F32)
                P.dma("sp", stc_st.ap(0, [[256, 4], [1, 256]], parts=120), DAP(stc_d, l * 480 * 256, [[256, 120], [120 * 256, 4], [1, 256]]),
                      writes=stc_st.k(), dkey="ld3", group=False)
                for grp in range(4):
                    for j in range(2):
                        pb = ps_next()
                        transpose(pb.c(0, 120), stc_st.ap(grp * 256 + j * 128, [[1, 128]], parts=120), 120, stc_st.k(), pb.k())
                        evac_copy(gCb.ap(j * LC + 30 + nP + grp * 4 * 38, [[38, 4], [1, 30]]), pb.ap(0, [[30, 4], [1, 30]]), pb.k(), gCb.k(j * LC + 30 + nP, j * LC + 30 + nP + 16 * 38))
                P.dma("sp", DAP(ncs_d, l * 480 * 256, [[30 * 256, 16], [1, 22 * 256]]),
                      DAP(stc_d, l * 480 * 256 + 8 * 256, [[30 * 256, 16], [1, 22 * 256]]), dkey=("o", 5))
                stg_c = alias(stc_st, 256, F32)
                stg_c2 = alias(stc_st, 256, F32, byte_off=1024)

            def C1():
                for j in range(2):
                    for (t0, n, kind) in TT:
                        pb = ps_next()
                        mm_fm(wb4, nco4, 16 + j * 128, 128, t0, n, pb)
                        evac_copy(ca32.c(j * NT + t0, n), pb.c(0, n), pb.k(), ca32.k(j * NT + t0, j * NT + t0 + n), eng="dve")
                for j in range(2):
                    for (t0, n, kind) in TT:
                        pb = ps_next()
                        mm_fm(wb4, nco4, 272 + j * 128, 128, t0, n, pb)
                        sgt = sgt2[sg_i[0] % 2]
                        sg_i[0] += 1
                        P.act(I("activation", sgt.c(0, n), pb.c(0, n), AF.Sigmoid), pb.k(), sgt.k())
                        P.dve(I("tensor_tensor", cview(gCb, j, LC, 30, t0, n, kind, shift=30), tview(sgt, 0, 0, n, kind),
                                tview(ca32, j * NT, t0, n, kind), ALU.mult), sgt.k() + ca32.k(j * NT + t0, j * NT + t0 + n), kcv(gCb, j, LC, 30, t0, n, kind, 30, 30))
                        if kind == "p" and Sidx == 1 and t0 == 512:
                            P.dve(I("tensor_tensor", cst.c(j * 158, 30), sgt.c(482, 30), ca32.c(j * NT + 994, 30), ALU.mult),
                                  sgt.k() + ca32.k(j * NT + t0, j * NT + t0 + n), cst.k())
                        if kind == "s":
                            P.dve(I("tensor_tensor", cst.c(j * 158 + 30, 128), sgt.c(0, 128), ca32.c(j * NT + 1024, 128), ALU.mult),
                                  sgt.k() + ca32.k(j * NT + t0, j * NT + t0 + n), cst.k())
                if Sidx == 0:
                    for j in range(2):
                        P.dve(I("tensor_copy", stC.c((l * 2 + j) * 30, 30), gCb.c(j * LC + nP, 30)), gCb.k(j * LC + nP, j * LC + nP + 30), stC.k())

            def C2():
                for j in range(2):
                    for (t0, n, kind) in TT:
                        pb = ps_next()
                        for k in range(31):
                            P.pe(I("matmul", tview(pb, 0, 0, n, kind), diagC.c((j * 31 + k) * 128, 128),
                                   cview(gCb, j, LC, 30, t0, n, kind, shift=k), start=(k == 0), stop=(k == 30)),
                                 diagC.k((j * 31 + k) * 128, (j * 31 + k + 1) * 128) + kcv(gCb, j, LC, 30, t0, n, kind, 0, 30), pb.k())
                        P.dve(I("tensor_scalar", cv32.c(j * NT + t0, n), pb.c(0, n), cv(71 + j), None, ALU.add),
                              pb.k() + cvec.k(), cv32.k(j * NT + t0, j * NT + t0 + n))

            def C3():
                cb16, csq, mean, msq, rstd = cset
                for ti, (t0, n, kind) in enumerate(TT):
                    for j in range(2):
                        P.dve(I("tensor_copy", cb16.c(j * 512, n), cv32.c(j * NT + t0, n)), cv32.k(j * NT + t0, j * NT + t0 + n),
                              cb16.k(j * 512, j * 512 + n))
                        P.act(I("activation", csq.c(j * 512, n), cv32.c(j * NT + t0, n), AF.Square), cv32.k(j * NT + t0, j * NT + t0 + n),
                              csq.k(j * 512, j * 512 + n))
                    p1 = ps_next()
                    p2 = ps_next()
                    for j in range(2):
                        P.pe(I("matmul", p1.c(0, n), onesb.c(0, 128), cb16.c(j * 512, n), start=(j == 0), stop=(j == 1)),
                             onesb.k() + cb16.k(j * 512, j * 512 + n), p1.k())
                    for j in range(2):
                        P.pe(I("matmul", p2.c(0, n), onesb.c(0, 128), csq.c(j * 512, n), start=(j == 0), stop=(j == 1)),
                             onesb.k() + csq.k(j * 512, j * 512 + n), p2.k())
                    ln_stats(p1, p2, n, 1.0 / 256.0, mean, msq, rstd)
                    for j in range(2):
                        tmpn = ctmp[j]
                        P.dve(I("tensor_tensor", tmpn.c(0, n), cv32.c(j * NT + t0, n), mean.c(0, n), ALU.subtract),
                              cv32.k(j * NT + t0, j * NT + t0 + n) + mean.k(), tmpn.k())
                        P.dve(I("tensor_tensor", tmpn.c(0, n), tmpn.c(0, n), rstd.c(0, n), ALU.mult), tmpn.k() + rstd.k(), tmpn.k())
                        P.act(I("activation", ymix.c((4 + j) * NT + t0, n), tmpn.c(0, n), AF.Silu, bias=cv(75 + j), scale=cv(73 + j)),
                              tmpn.k() + cvec.k(), ymix.k((4 + j) * NT + t0, (4 + j) * NT + t0 + n))
                if has_s:
                    for j in range(2):
                        pb = ps_next()
                        transpose(pb.c(0, 128, parts=30), cst.c(j * 158, 30), 128, cst.k(), pb.k())
                        evac_copy(stg_c.c(j * 128, 128, parts=30), pb.c(0, 128, parts=30), pb.k(), stg_c.k())
                        pb = ps_next()
                        transpose(pb.c(0, 128), cst.c(j * 158 + 30, 128), 128, cst.k(), pb.k())
                        evac_copy(stg_c2.c(j * 128, 128), pb.c(0, 128), pb.k(), stg_c2.k())
                    P.dma("sp", DAP(ncp_d, l * 30 * 256, [[256, 30], [1, 256]]), stg_c.c(0, 256, parts=30), reads=stg_c.k(), dkey=("o", 6))
                    for q in range(16):
                        P.dma("sp", DAP(ncs_d, l * 480 * 256 + q * 30 * 256 + 22 * 256, [[256, 8], [1, 256]]),
                              stg_c2.c(0, 256, parts=8, p0=8 * q), reads=stg_c2.k(), dkey=("o", 7))

            def D1():
                for j in range(2):
                    for (t0, n, kind) in TT:
                        pb = ps_next()
                        mm_fm(wb5, nco5, j * 128, 128, t0, n, pb)
                        P.act(I("activation", u16.c(j * NT + t0, n), pb.c(0, n), AF.Gelu_apprx_tanh),
                              pb.k(), u16.k(j * NT + t0, j * NT + t0 + n))

            def D2():
                for c in range(NCH):
                    pb = ps_next()
                    for kc in range(8):
                        P.pe(I("matmul", pb.c(0, 256), xb_t(kc, c * 128, 128), wb5.ap(kc * nco5 + 256, [[1, 256]]),
                               start=(kc == 0), stop=(kc == 7)), wb5.k() + kxb(c * 128, kc), pb.k())
                    P.act(I("activation", vv32.c(c * 256, 256), pb.c(0, 256), AF.Gelu_apprx_tanh), pb.k(), vv32.k(c * 256, c * 256 + 256))
                    P.dve(I("bn_stats", stt.c(c * 8, 6), vv32.c(c * 256, 256)), vv32.k(c * 256, c * 256 + 256), stt.k())
                    P.dve(I("bn_aggr", mv.c(c * 2, 2), stt.c(c * 8, 6)), stt.k(), mv.k())

            def D3():
                P.act(I("activation", rsd.c(0, NCH), mv.ap(1, [[2, NCH]]), AF.Ln, bias=cEPS.c(0, 1), scale=1.0), mv.k() + cEPS.k(), rsd.k())
                P.act(I("activation", rsd.c(0, NCH), rsd.c(0, NCH), AF.Exp, scale=-0.5), rsd.k(), rsd.k())
                vall = vv32.ap(0, [[256, NCH], [1, 256]])
                P.dve(I("scalar_tensor_tensor", nmr.c(0, NCH), mv.ap(0, [[2, NCH]]), -1.0, rsd.c(0, NCH), ALU.mult, ALU.mult),
                      mv.k() + rsd.k(), nmr.k())
                for c in range(NCH):
                    P.act(I("activation", vv32.c(c * 256, 256), vv32.c(c * 256, 256), AF.Identity, bias=nmr.c(c, 1), scale=rsd.c(c, 1)),
                          vv32.k(c * 256, c * 256 + 256) + nmr.k() + rsd.k(), vv32.k(c * 256, c * 256 + 256))
                P.dve(I("tensor_tensor", vall, vall, dlng.ap((l * 2 + 0) * 256, [[0, NCH], [1, 256]]), ALU.mult), vv32.k() + dlng.k(), vv32.k())
                P.dve(I("tensor_tensor", vall, vall, dlng.ap((l * 2 + 1) * 256, [[0, NCH], [1, 256]]), ALU.add), vv32.k() + dlng.k(), vv32.k())
                for j in range(2):
                    evac_copy(vvm.ap(j * 256, [[512, NCH], [192, 2], [1, 64]]), vv32.ap(j * 128, [[256, NCH], [64, 2], [1, 64]]),
                              vv32.k(), vvm.k())
                if has_s:
                    P.dma("sp", DAP(nvs_d, l * 128 * 256, [[256, 128], [1, 256]]), vv32.c(8 * 256, 256), reads=vv32.k(), dkey=("o", 8))

            def D4():
                fi = 0
                for c in range(NCH):
                    kd = 0 if c < 8 else 1
                    for j in range(2):
                        pb = ps_next()
                        for gg in range(2):
                            g = 2 * j + gg
                            P.pe(I("matmul", pb.c(0, 128), vvm.c(c * 512 + g * 128, 128), wsT.c(((l * 2 + kd) * 4 + g) * 128, 128),
                                   start=(gg == 0), stop=(gg == 1)), vvm.k(c * 512 + g * 128, c * 512 + g * 128 + 128) + wsT.k(), pb.k())
                        ft = ftm[fi % 2]
                        fi += 1
                        P.dve(I("tensor_tensor", ft.c(0, 128), pb.c(0, 128), dbias.c(((l * 2 + kd) * 2 + j) * 128, 128), ALU.add),
                              pb.k() + dbias.k(), ft.k())
                        P.dve(I("tensor_tensor", ymix.c((6 + j) * NT + c * 128, 128), ft.c(0, 128),
                                u16.c(j * NT + c * 128, 128), ALU.mult), ft.k() + u16.k(j * NT + c * 128, j * NT + c * 128 + 128),
                              ymix.k((6 + j) * NT + c * 128, (6 + j) * NT + c * 128 + 128))

            C1()
            stage(5)
            wb5, nco5 = w_acquire("in", l)
            D2()
            D3()
            C2()
            C3()
            D1()
            D4()
            scr_top[0] = mC

            stage(6)
            class StatAcc:
                def __init__(self, lag, nfc=8, copy_eng="act"):
                    self.copy_eng = copy_eng
                    self.lag = lag
                    self.nfc = nfc
                    self.banks = [(ps_reserve(), ps_reserve()) for _ in TT]
                    self.ring = [(carve(512, BF16), carve(512, BF16)) for _ in range(lag + 2)]
                    self.i = 0
                    self.pending = []

                def add(self, fc, ti, t0, n):
                    xbt, sqt_ = self.ring[self.i % len(self.ring)]
                    self.i += 1
                    if self.copy_eng == "act":
                        P.act(I("activation", xbt.c(0, n), x32.c(fc * NTMAX + t0, n), AF.Copy), kx32(t0, fc), xbt.k())
                    else:
                        P.dve(I("tensor_copy", xbt.c(0, n), x32.c(fc * NTMAX + t0, n)), kx32(t0, fc), xbt.k())
                    P.act(I("activation", sqt_.c(0, n), x32.c(fc * NTMAX + t0, n), AF.Square), kx32(t0, fc), sqt_.k())
                    self.pending.append((fc, ti, n, xbt, sqt_))
                    while len(self.pending) > self.lag:
                        self._flush()

                def _flush(self):
                    fc, ti, n, xbt, sqt_ = self.pending.pop(0)
                    (p1, _), (p2, _) = self.banks[ti]
                    P.pe(I("matmul", p1.c(0, n), onesb.c(0, 128), xbt.c(0, n), start=(fc == 0), stop=(fc == self.nfc - 1)),
                         onesb.k() + xbt.k(), p1.k())
                    P.pe(I("matmul", p2.c(0, n), onesb.c(0, 128), sqt_.c(0, n), start=(fc == 0), stop=(fc == self.nfc - 1)),
                         onesb.k() + sqt_.k(), p2.k())

                def finish(self, mean_, msq_, rstd_):
                    while self.pending:
                        self._flush()
                    for ti, (t0, n, kind) in enumerate(TT):
                        (p1, i1_), (p2, i2_) = self.banks[ti]
                        ln_stats(p1, p2, n, 1.0 / 1024.0, mean_.sub(t0, n), msq_.sub(t0, n), rstd_.sub(t0, n))
                        ps_reserved.discard(i1_)
                        ps_reserved.discard(i2_)

            def resid_ln(acc, gcol, bcol):
                mean_, msq_, rstd_ = lnset
                acc.finish(mean_, msq_, rstd_)
                for fc in range(8):
                    kall32 = [("x32", fc, ti) for ti in range(len(TT))]
                    kallb = [("xb", fc, ti) for ti in range(len(TT))]
                    tn = tmpn3[fc % len(tmpn3)]
                    P.dve(I("tensor_tensor", tn.c(0, NT), x32.c(fc * NTMAX, NT), mean_.c(0, NT), ALU.subtract),
                          kall32 + mean_.k(), tn.k())
                    P.dve(I("tensor_tensor", tn.c(0, NT), tn.c(0, NT), rstd_.c(0, NT), ALU.mult), tn.k() + rstd_.k(), tn.k())
                    if fc != 7:
                        P.act(I("activation", x32.c(fc * NTMAX, NT), tn.c(0, NT), AF.Identity,
                                bias=cv(bcol + fc), scale=cv(gcol + fc)), tn.k() + cvec.k(), kall32)
                    else:
                        P.dve(I("tensor_scalar", x32.c(fc * NTMAX, NT), tn.c(0, NT), cv(gcol + fc), cv(bcol + fc), ALU.mult, ALU.add),
                              tn.k() + cvec.k(), kall32)
                    P.act(I("activation", xb.c(fc * NTMAX, NT), tn.c(0, NT), AF.Identity,
                            bias=cv(bcol + fc), scale=cv(gcol + fc)), tn.k() + cvec.k(), kallb)

            def carve_ln():
                return ((carve(NT, F32), carve(NT, F32), carve(NT, F32)), [carve(NT, F32) for _ in range(3)])

            lnscr = scr_top[0]
            lnset, tmpn3 = carve_ln()
            acc1 = StatAcc(lag=6, copy_eng="dve")
            for half in range(2):
                wbo, ncoo = w_acquire("out", l)
                for cc in range(4):
                    fc = half * 4 + cc
                    for (t0, n, kind) in TT:
                        pb = ps_next()
                        for kc in range(8):
                            P.pe(I("matmul",
                                pb.c(0, n), wbo.ap(kc * ncoo + cc * 128, [[1, 128]]), ymix.c(kc * NT + t0, n),
                                start=(kc == 0), stop=(kc == 7)), wbo.k() + ymix.k(kc * NT + t0, kc * NT + t0 + n), pb.k())
                        if 'dbgmix' in SKIP:
                            P.dve(I("tensor_copy", x32.c(fc * NTMAX + t0, n), pb.c(0, n)), pb.k(), kx32(t0, fc))
                        else:
                            P.dve(I("scalar_tensor_tensor",
                                x32.c(fc * NTMAX + t0, n), x32.c(fc * NTMAX + t0, n), ALPHA, pb.c(0, n), ALU.mult, ALU.add),
                                kx32(t0, fc) + pb.k(), kx32(t0, fc))
                            acc1.add(fc, TT.index((t0, n, kind)), t0, n)
            if 'dbgmix' in SKIP:
                scr_top[0] = mark0
                return
            resid_ln(acc1, 77, 85)
            scr_top[0] = lnscr

            stage(7)
            if 'ffn' in SKIP:
                scr_top[0] = mark0
                return
            scr_top[0] = mark0
            hbuf = carve(32 * NT, BF16)
            sqt = [carve(512, F32) for _ in range(2)]
            si = 0
            for g in range(8):
                wbf, ncof = w_acquire("ff1", l)
                for cc in range(4):
                    hc = g * 4 + cc
                    for (t0, n, kind) in TT:
                        pb = ps_next()
                        mm_fm(wbf, ncof, cc * 128, 128, t0, n, pb)
                        s_ = sqt[si]
                        si = 1 - si
                        P.act(I("activation", s_.c(0, n), pb.c(0, n), AF.Square), pb.k(), s_.k())
                        P.dve(I("scalar_tensor_tensor",
                            hbuf.c(hc * NT + t0, n), pb.c(0, n), 0.0, s_.c(0, n), ALU.is_gt, ALU.mult),
                            pb.k() + s_.k(), hbuf.k(hc * NT + t0, hc * NT + t0 + n))
            acc2 = StatAcc(lag=2)
            for oc in range(8):
                wbf, ncof = w_acquire("ff2", l)
                for (t0, n, kind) in TT:
                    pb = ps_next()
                    for kc in range(32):
                        P.pe(I("matmul", pb.c(0, n), wbf.c(kc * 128, 128), hbuf.c(kc * NT + t0, n),
                                                                           start=(kc == 0), stop=(kc == 31)), wbf.k() + hbuf.k(kc * NT + t0, kc * NT + t0 + n), pb.k())
                    P.dve(I("scalar_tensor_tensor",
                        x32.c(oc * NTMAX + t0, n), x32.c(oc * NTMAX + t0, n), ALPHA, pb.c(0, n), ALU.mult, ALU.add),
                        kx32(t0, oc) + pb.k(), kx32(t0, oc))
                    acc2.add(oc, TT.index((t0, n, kind)), t0, n)

            scr_top[0] = mark0
            lnset, tmpn3 = carve_ln()
            resid_ln(acc2, 93, 101)
            scr_top[0] = mark0

        cEPS = mk("cEPS", 1, F32)
        P.dve(I("memset", cEPS.c(0, 1), EPS), (), cEPS.k())
        sblk = mk("sblk", 128, F32)
        sblk_d = din("sblk", [128, 128])
        P.dma("sp", sblk.c(0, 128), sblk_d.ap(), writes=sblk.k(), dkey="c", group=True)
        P.dve(I("tensor_copy", sblkb.c(0, 128), sblk.c(0, 128)), sblk.k(), sblkb.k())

        def ln_stats(p1, p2, n, inv, mean, msq, rstd):
            P.act(I("activation", mean.c(0, n), p1.c(0, n), AF.Copy, scale=inv), p1.k(), mean.k())
            P.act(I("activation", msq.c(0, n), p1.c(0, n), AF.Square, scale=inv), p1.k(), msq.k())
            P.dve(I("scalar_tensor_tensor", rstd.c(0, n), p2.c(0, n), inv, msq.c(0, n), ALU.mult, ALU.subtract),
                  p2.k() + msq.k(), rstd.k())
            P.act(I("activation", rstd.c(0, n), rstd.c(0, n), AF.Ln, bias=cEPS.c(0, 1), scale=1.0), rstd.k() + cEPS.k(), rstd.k())
            P.act(I("activation", rstd.c(0, n), rstd.c(0, n), AF.Exp, scale=-0.5), rstd.k(), rstd.k())

        def main_body():
            stage(1)
            for Sidx in range(2):
                nP = 1024
                NT = nP + (128 if Sidx == 1 else 0)
                NCH = NT // 128
                mark = scr_top[0]
                xst = [carve(1024, F32) for _ in range(2)]
                for c in range(2 if 'x2' in SKIP else NCH):
                    s_ = xst[c % 2]
                    if c < 8:
                        src = DAP(xp_d, (Sidx * 1024 + c * 128) * 1024, [[1024, 128], [1, 1024]])
                    else:
                        src = xs_d.ap()
                    P.dma("sp", s_.c(0, 1024), src, writes=s_.k(), dkey=("xin", c % 2))
                    for hf in range(2):
                        pb = ps_next()
                        for f4 in range(4):
                            fc = hf * 4 + f4
                            transpose(pb.c(f4 * 128, 128), s_.c(fc * 128, 128), 128, s_.k(), pb.k())
                        P.act(I("activation", x32.ap(hf * 4 * NTMAX + c * 128, [[NTMAX, 4], [1, 128]]),
                                                                        pb.ap(0, [[128, 4], [1, 128]]), AF.Copy), pb.k(), [k_ for f_ in range(hf * 4, hf * 4 + 4) for k_ in kx32(c * 128, f_)])
                        if 'x1' not in SKIP:
                            P.dve(I("tensor_copy", xb.ap(hf * 4 * NTMAX + c * 128, [[NTMAX, 4], [1, 128]]),
                                                                             pb.ap(0, [[128, 4], [1, 128]])), pb.k(), [k_ for f_ in range(hf * 4, hf * 4 + 4) for k_ in kxb(c * 128, f_)])
                scr_top[0] = mark
                for l in range(DEPTH):
                    layer(l, Sidx)
                mark = scr_top[0]
                yst = [carve(1024, F32) for _ in range(2)]
                for c in range(NCH):
                    s_ = yst[c % 2]
                    for hf in range(2):
                        pb = ps_next()
                        for f4 in range(4):
                            fc = hf * 4 + f4
                            transpose(pb.c(f4 * 128, 128), x32.c(fc * NTMAX + c * 128, 128), 128, kx32(c * 128, fc), pb.k())
                        evac_copy(s_.c(hf * 512, 512), pb.c(0, 512), pb.k(), s_.k())
                    if c < 8:
                        dst = DAP(yp_d, (Sidx * 1024 + c * 128) * 1024, [[1024, 128], [1, 1024]])
                    else:
                        dst = ys_d.ap()
                    P.dma("sp", dst, s_.c(0, 1024), reads=s_.k(), dkey=("yout", c % 2))
                scr_top[0] = mark

        try:
            main_body()
        except _Stop:
            pass

        P.finalize(st)
        with nc.Block() as block:
            P.emit(block)
    return nc


def _consts():
    p = np.arange(128)
    c = {}
    c["ident"] = np.eye(128, dtype=np.float32)
    c["tril"] = (p[:, None] <= p[None, :]).astype(np.float32)
    c["sblk"] = ((p[:, None] <= p[None, :]) & (p[:, None] // 8 == p[None, :] // 8)).astype(np.float32)
    c["hm4"] = (p[:, None] // 32 == np.arange(4)[None, :]).astype(np.float32)
    c["bm"] = (p[:, None] // 32 == (np.arange(256)[None, :] // 64)).astype(np.float32)
    rm = np.ones((128, 1152), np.float32)
    rm[:, 0:1024:128] = 0.0
    rm[:, 1024:1152:8] = 0.0
    c["rmask"] = rm.astype(ml_dtypes.bfloat16)
    sq = (np.arange(128)[None, :] // 8 == np.arange(16)[:, None]).astype(np.float32)
    c["seqm"] = np.broadcast_to(sq.reshape(1, 2048), (128, 2048)).astype(ml_dtypes.bfloat16)
    c["tokm"] = (p[:, None] // 8 == np.arange(16)[None, :]).astype(np.float32)
    return c


def _cvec(inp, l):
    def fm(v, nchunk):
        return np.asarray(v, np.float32).reshape(nchunk, 128).T
    cols = []
    aw = np.asarray(inp["a_conv_w"][l], np.float32)
    for j in range(2):
        for k in range(3):
            cols.append(aw[k, j * 128:(j + 1) * 128][:, None])
    cols.append(np.asarray(inp["b_gate_b"][l], np.float32)[:, None])
    cols.append(fm(inp["b_norm_g"][l], 2))
    cw = np.asarray(inp["c_conv_w"][l], np.float32)
    for j in range(2):
        for k in range(31):
            cols.append(cw[k, j * 128:(j + 1) * 128][:, None])
    cols.append(fm(inp["c_conv_b"][l], 2))
    cols.append(fm(inp["c_ln_g"][l], 2))
    cols.append(fm(inp["c_ln_b"][l], 2))
    cols.append(fm(inp["ln1_g"][l], 8))
    cols.append(fm(inp["ln1_b"][l], 8))
    cols.append(fm(inp["ln2_g"][l], 8))
    cols.append(fm(inp["ln2_b"][l], 8))
    out = np.concatenate(cols, axis=1)
    assert out.shape == (128, NCV), out.shape
    return out


def _dbt(inp):
    bs = np.asarray(inp["d_bs"], np.float32)
    out = np.zeros((2, 2, 2, 128, 128), np.float32)
    for l in range(2):
        for j in range(2):
            for gg in range(2):
                out[l, 0, j, 64 * gg:64 * gg + 64, :] = bs[l, 2 * j + gg][None, :]
                out[l, 1, j, 64 * gg:64 * gg + 64, :] = np.tile(bs[l, 2 * j + gg, :8], 16)[None, :]
    return np.ascontiguousarray(out.reshape(8, 128, 128))


def _dwst(inp):
    ws = np.asarray(inp["d_ws"], np.float32)
    out = np.zeros((2, 2, 4, 128, 128), np.float32)
    out[:, 0] = ws.transpose(0, 1, 3, 2)
    blk = ws[:, :, :8, :8].transpose(0, 1, 3, 2)
    for q in range(16):
        out[:, 1, :, 8 * q:8 * q + 8, 8 * q:8 * q + 8] = blk
    return np.ascontiguousarray(out.reshape(16, 128, 128))


_NC_CACHE = {}


def make_in_maps(inp):
    consts = _consts()
    f32 = lambda a: np.ascontiguousarray(a, dtype=np.float32)
    shared = {
        "w_in": f32(inp["w_in"]), "w_out": f32(inp["w_out"]), "w_ff1": f32(inp["w_ff1"]), "w_ff2": f32(inp["w_ff2"]),
        "cvec": f32(np.stack([_cvec(inp, l) for l in range(2)])),
        "gw2": f32(inp["b_gate_w2"]),
        "dln": f32(np.stack([inp["d_ln_g"], inp["d_ln_b"]], axis=1)),
        "dws": f32(inp["d_ws"]), "dbs": f32(inp["d_bs"]),
        "dbt": _dbt(inp), "dwst": _dwst(inp),
    }
    shared.update(consts)
    in_maps = []
    for i in range(8):
        m = dict(shared)
        m["xp"] = f32(inp["x_prompt"][i])
        m["xs"] = f32(inp["x_sample"][16 * i:16 * i + 16].reshape(128, 1024))
        m["sta"] = f32(inp["state_conv_a"][:, 16 * i:16 * i + 16].reshape(2, 32, 256))
        m["stg"] = f32(inp["state_gla"][:, 16 * i:16 * i + 16].reshape(2, 16, 8192))
        m["stc"] = f32(inp["state_conv_c"][:, 16 * i:16 * i + 16].reshape(2, 480, 256))
        in_maps.append(m)
    return in_maps


def kernel(**inp):
    inp = {k: np.asarray(v) for k, v in inp.items()}
    if "nc" not in _NC_CACHE:
        _NC_CACHE["nc"] = build_program()
    nc = _NC_CACHE["nc"]
    in_maps = make_in_maps(inp)
    res = run_bass_kernel_spmd(nc, in_maps, core_ids=list(range(8)))
    R = res.results
    y_prompt = np.stack([R[i]["yp"] for i in range(8)]).reshape(8, 2048, 1024)
    y_sample = np.concatenate([R[i]["ys"].reshape(16, 8, 1024) for i in range(8)], axis=0)
    na_p = np.stack([R[i]["nap"] for i in range(8)], axis=1).reshape(2, 8, 2, 256)
    na_s = np.concatenate([R[i]["nas"].reshape(2, 16, 2, 256) for i in range(8)], axis=1)
    ng_p = np.stack([R[i]["ngp"] for i in range(8)], axis=1).reshape(2, 8, 4, 32, 64)
    ng_s = np.concatenate([R[i]["ngs"].reshape(2, 16, 4, 32, 64) for i in range(8)], axis=1)
    nc_p = np.stack([R[i]["ncp"] for i in range(8)], axis=1).reshape(2, 8, 30, 256)
    nc_s = np.concatenate([R[i]["ncs"].reshape(2, 16, 30, 256) for i in range(8)], axis=1)
    nv_s = np.concatenate([R[i]["nvs"].reshape(2, 16, 8, 256) for i in range(8)], axis=1)
    outs = (y_prompt, y_sample, na_p, na_s, ng_p, ng_s, nc_p, nc_s, nv_s)
    return tuple(np.ascontiguousarray(o, dtype=np.float32) for o in outs)
```

```python
import os
import numpy as np
import ml_dtypes
SKIP = set()
from contextlib import ExitStack
import concourse.bass as bass
import concourse.mybir as mybir
from concourse.bass_utils import run_bass_kernel_spmd

F32 = mybir.dt.float32
BF16 = mybir.dt.bfloat16
ALU = mybir.AluOpType
AF = mybir.ActivationFunctionType
AX = mybir.AxisListType

DEPTH = 2
ALPHA = float((2 * DEPTH) ** 0.25)
EPS = 1e-5
NCV = 109
GR = 64
ENGS = ("pe", "act", "dve", "pool", "sp")
STRICT = True


def I(name, *a, **k):
    return lambda e: getattr(e, name)(*a, **k)


class Op:
    __slots__ = ("eng", "fn", "reads", "writes", "deps", "signal", "cnt", "dkey", "didx", "raw")

    def __init__(self, eng, fn, reads, writes, dkey):
        self.eng = eng
        self.fn = fn
        self.reads = reads
        self.writes = writes
        self.deps = []
        self.signal = False
        self.cnt = 0
        self.dkey = dkey
        self.didx = 0
        self.raw = set()


class Prog:
    def __init__(self, nc):
        self.nc = nc
        self.ops = {e: [] for e in ENGS}
        self.lw = {}
        self.rd = {}
        self.dkeys = {}
        self.group_keys = set()

    def add(self, eng, fn, reads=(), writes=(), dkey=None, group=False):
        ps_r = [r for r in reads if r[0] == "ps"]
        if ps_r:
            reads = [r for r in reads if r[0] != "ps"]
            writes = list(writes) + [r for r in ps_r if r not in writes]
        op = Op(eng, fn, tuple(reads), tuple(writes), dkey)
        deps = []
        for r in ps_r:
            w = self.lw.get(r)
            if w is not None:
                op.raw.add(id(w))
        for r in op.reads:
            w = self.lw.get(r)
            if w is not None:
                deps.append(w)
                op.raw.add(id(w))
        for w in op.writes:
            p = self.lw.get(w)
            if p is not None:
                deps.append(p)
            q = self.rd.get(w)
            if q:
                deps.extend(q)
        for r in op.reads:
            self.rd.setdefault(r, []).append(op)
        for w in op.writes:
            self.lw[w] = op
            self.rd[w] = []
        seen = set()
        for d in deps:
            if id(d) in seen or d is op:
                continue
            seen.add(id(d))
            op.deps.append(d)
        if dkey is not None:
            lst = self.dkeys.setdefault(dkey, [])
            lst.append(op)
            op.didx = len(lst)
            if group:
                self.group_keys.add(dkey)
        self.ops[eng].append(op)
        return op

    def pe(self, fn, reads=(), writes=()):
        return self.add("pe", fn, reads, writes)

    def act(self, fn, reads=(), writes=()):
        return self.add("act", fn, reads, writes)

    def dve(self, fn, reads=(), writes=()):
        return self.add("dve", fn, reads, writes)

    def pool(self, fn, reads=(), writes=()):
        return self.add("pool", fn, reads, writes)

    def dma(self, q, out, in_, reads=(), writes=(), dkey=None, group=False, **kw):
        return self.add(q, I("dma_start", out=out, in_=in_, **kw), reads, writes, dkey=dkey, group=group)

    def _needs_edge(self, op, d):
        if d.dkey is not None or op.dkey is not None:
            return True
        if d.eng != op.eng:
            return True
        if op.eng == "pe":
            return False
        return STRICT or id(d) in op.raw

    def finalize(self, stack):
        nc = self.nc
        for e in ENGS:
            for op in self.ops[e]:
                op.deps = [d for d in op.deps if self._needs_edge(op, d)]
                for d in op.deps:
                    if d.dkey is None:
                        d.signal = True
        self.esem = {}
        for e in ENGS:
            c = 0
            for op in self.ops[e]:
                if op.dkey is None and op.signal:
                    c += 1
                    op.cnt = c
            self.esem[e] = stack.enter_context(nc.semaphore("sem_" + e))
        self.dsem = {}
        for i, k in enumerate(self.dkeys):
            self.dsem[k] = stack.enter_context(nc.semaphore("dsem%d" % i))

    def emit(self, block):
        emap = {"pe": "tensor", "act": "scalar", "dve": "vector", "pool": "gpsimd", "sp": "sync"}
        prog = self

        def mk(ename):
            def body(eng):
                waited = {}
                for op in prog.ops[ename]:
                    need = {}
                    for d in op.deps:
                        if d.dkey is not None:
                            s = prog.dsem[d.dkey]
                            if d.dkey in prog.group_keys:
                                v = 16 * len(prog.dkeys[d.dkey])
                            else:
                                v = 16 * d.didx
                            k = ("d", d.dkey)
                        else:
                            s = prog.esem[d.eng]
                            v = d.cnt
                            k = ("e", d.eng)
                        if need.get(k, (None, 0))[1] < v:
                            need[k] = (s, v)
                    for k, (s, v) in need.items():
                        if waited.get(k, 0) >= v:
                            continue
                        eng.wait_ge(s, v)
                        waited[k] = v
                    ins = op.fn(eng)
                    if op.dkey is not None:
                        ins.then_inc(prog.dsem[op.dkey], 16)
                    elif op.signal:
                        ins.then_inc(prog.esem[ename], 1)
                if ename == "sp":
                    for k, lst in prog.dkeys.items():
                        v = 16 * len(lst)
                        if waited.get(("d", k), 0) < v:
                            eng.wait_ge(prog.dsem[k], v)
            return body

        for ename in ENGS:
            if not prog.ops[ename] and ename != "sp":
                continue
            getattr(block, emap[ename])(mk(ename))


class Buf:
    def __init__(self, t, pstride, off, n, esz, ns, boff):
        self.t = t
        self.pstride = pstride
        self.off = off
        self.n = n
        self.esz = esz
        self.ns = ns
        self.boff = boff

    def k(self, lo=0, hi=None):
        if hi is None:
            hi = self.n
        b0 = self.boff + lo * self.esz
        b1 = self.boff + hi * self.esz
        return [(self.ns, g) for g in range(b0 // GR, (b1 - 1) // GR + 1)]

    def ap(self, off, dims, parts=128, p0=0):
        return bass.AP(self.t, p0 * self.pstride + self.off + off, [[self.pstride, parts]] + [list(d) for d in dims])

    def c(self, c0, n, parts=128, p0=0):
        return self.ap(c0, [[1, n]], parts, p0)

    def sub(self, lo, n):
        return Buf(self.t, self.pstride, self.off + lo, n, self.esz, self.ns, self.boff + lo * self.esz)


class _Stop(Exception):
    pass


STAGE_LIMIT = [99]


def stage(n):
    if n > STAGE_LIMIT[0]:
        raise _Stop()


def build_program():
    nc = bass.Bass("TRN2", target_bir_lowering=False)

    def din(name, shape, dt=F32):
        return nc.dram_tensor(name, shape, dt, kind="ExternalInput")

    def dout(name, shape):
        return nc.dram_tensor(name, shape, F32, kind="ExternalOutput")

    xp_d = din("xp", [2048, 1024])
    xs_d = din("xs", [128, 1024])
    sta_d = din("sta", [2, 32, 256])
    stg_d = din("stg", [2, 16, 8192])
    stc_d = din("stc", [2, 480, 256])
    win_d = din("w_in", [2, 1024, 2576])
    wout_d = din("w_out", [2, 1024, 1024])
    wff1_d = din("w_ff1", [2, 1024, 4096])
    wff2_d = din("w_ff2", [2, 4096, 1024])
    cvec_d = din("cvec", [2, 128, NCV])
    gw2_d = din("gw2", [2, 16, 128])
    dln_d = din("dln", [2, 2, 256])
    dws_d = din("dws", [2, 4, 128, 128])
    dbs_d = din("dbs", [2, 4, 128])
    ident_d = din("ident", [128, 128])
    tril_d = din("tril", [128, 128])
    hm4_d = din("hm4", [128, 4])
    bm_d = din("bm", [128, 256])
    rmask_d = din("rmask", [128, 1152], BF16)
    seqm_d = din("seqm", [128, 2048], BF16)
    tokm_d = din("tokm", [128, 16])
    dbt_d = din("dbt", [8, 128, 128])
    dwst_d = din("dwst", [16, 128, 128])

    yp_d = dout("yp", [2048, 1024])
    ys_d = dout("ys", [128, 1024])
    nap_d = dout("nap", [2, 2, 256])
    nas_d = dout("nas", [2, 32, 256])
    ngp_d = dout("ngp", [2, 8192])
    ngs_d = dout("ngs", [2, 16, 8192])
    ncp_d = dout("ncp", [2, 30, 256])
    ncs_d = dout("ncs", [2, 480, 256])
    nvs_d = dout("nvs", [2, 128, 256])

    def DAP(t, off, dims):
        return bass.AP(t, off, [list(d) for d in dims])

    NTMAX = 1152
    st = ExitStack()
    with st:
        P = Prog(nc)

        def mk(name, n, dt):
            t = st.enter_context(nc.sbuf_tensor("s_" + name, [128, n], dt))
            esz = 4 if dt == F32 else 2
            return Buf(t, n, 0, n, esz, name, 0)

        x32 = mk("x32", 8 * NTMAX, F32)
        xb = mk("xb", 8 * NTMAX, BF16)
        WSLOT = 4224
        wring = [mk("wr%d" % i, WSLOT, BF16) for i in range(3)]
        ident32 = mk("ident32", 128, F32)
        identb = mk("identb", 128, BF16)
        onesb = mk("onesb", 128, BF16)
        tril = mk("tril", 128, F32)
        hm4 = mk("hm4", 4, F32)
        bmk = mk("bmk", 256, F32)
        rmask = mk("rmask", NTMAX, BF16)
        seqm = mk("seqm", 2048, BF16)
        tokm = mk("tokm", 16, F32)
        trilb = mk("trilb", 128, BF16)
        sblkb = mk("sblkb", 128, BF16)
        cvec = mk("cvec", 2 * NCV, F32)
        ngb = mk("ngb", 2, F32)
        gw2 = mk("gw2", 2 * 128, F32)
        dlng = mk("dlng", 2 * 2 * 256, F32)
        dbias = mk("dbias", 2 * 2 * 2 * 128, F32)
        wsT = mk("wsT", 2 * 2 * 4 * 128, BF16)
        stA = mk("stA", 2 * 2 * 2, BF16)
        stC = mk("stC", 2 * 2 * 30, BF16)
        Sbd32 = mk("Sbd32", 2 * 256, F32)
        SCRB = 100 * 1024
        scr_t = st.enter_context(nc.sbuf_tensor("scr", [128, SCRB // 2], BF16))
        scr_h = {BF16: scr_t, F32: scr_t.bitcast(F32)}
        scr_top = [0]

        def carve(n, dt):
            esz = 4 if dt == F32 else 2
            b0 = (scr_top[0] + 63) // 64 * 64
            scr_top[0] = b0 + n * esz
            assert scr_top[0] <= SCRB, ("scratch overflow", scr_top[0])
            return Buf(scr_h[dt], SCRB // esz, b0 // esz, n, esz, "scr", b0)

        def alias(buf, n, dt, byte_off=0):
            esz = 4 if dt == F32 else 2
            b0 = buf.boff + byte_off
            assert b0 % esz == 0 and byte_off + n * esz <= buf.n * buf.esz
            return Buf(scr_h[dt], SCRB // esz, b0 // esz, n, esz, "scr", b0)

        pst = st.enter_context(nc.psum_tensor("pst", [128, 4096], F32))
        pst_bf = pst.bitcast(BF16)
        ps_ctr = [0]

        def psbank(i):
            return Buf(pst, 4096, 512 * i, 512, 4, "ps", 2048 * i)

        def psbank_bf(i):
            return Buf(pst_bf, 8192, 1024 * i, 1024, 2, "ps", 2048 * i)

        ps_reserved = set()

        def ps_next(bf=False):
            while True:
                i = ps_ctr[0] % 8
                ps_ctr[0] += 1
                if i not in ps_reserved:
                    break
            return psbank_bf(i) if bf else psbank(i)

        def ps_reserve():
            while True:
                i = ps_ctr[0] % 8
                ps_ctr[0] += 1
                if i not in ps_reserved:
                    break
            ps_reserved.add(i)
            return psbank(i), i

        def ps_next2():
            if ps_ctr[0] % 2:
                ps_ctr[0] += 1
            i = ps_ctr[0] % 8
            ps_ctr[0] += 2
            return Buf(pst, 4096, 512 * i, 1024, 4, "ps", 2048 * i)

        def kx32(t0, fc=None):
            if fc is None:
                return [("x32", f, t0 // 512) for f in range(8)]
            return [("x32", fc, t0 // 512)]

        def kxb(t0, fc=None):
            if fc is None:
                return [("xb", f, t0 // 512) for f in range(8)]
            return [("xb", fc, t0 // 512)]

        ev_ctr = [0]

        def evac_copy(out_ap, in_ap, reads, writes, eng=None):
            if eng is None:
                eng = "act" if ev_ctr[0] % 2 == 0 else "dve"
                ev_ctr[0] += 1
            if eng == "act":
                P.act(I("activation", out_ap, in_ap, AF.Copy), reads, writes)
            else:
                P.dve(I("tensor_copy", out_ap, in_ap), reads, writes)

        def transpose(out_ps_ap, in_ap, n_in_parts, reads, writes, bf=False):
            idn = identb if bf else ident32
            ida = idn.ap(0, [[1, n_in_parts]], parts=n_in_parts)
            P.pe(I("transpose", out_ps_ap, in_ap, ida), list(reads) + idn.k(), writes)

        wtiles = []
        for S in range(2):
            for l in range(2):
                for (c0, c1) in [(0, 512), (512, 1024), (1536, 2064), (1024, 1536), (2064, 2576)]:
                    wtiles.append(("in", l, c0, c1))
                for c0 in (0, 512):
                    wtiles.append(("out", l, c0, c0 + 512))
                for g in range(8):
                    wtiles.append(("ff1", l, 512 * g, 512 * g + 512))
                for oc in range(8):
                    wtiles.append(("ff2", l, 128 * oc, 128 * oc + 128))
        w_issued = [0]
        w_next = [0]

        def w_issue(gi):
            kind, l, c0, c1 = wtiles[gi]
            nco = c1 - c0
            slot = wring[gi % 3]
            if kind == "ff2":
                src = DAP(wff2_d, l * 4096 * 1024 + c0, [[1024, 128], [128 * 1024, 32], [1, nco]])
                dst = slot.ap(0, [[nco, 32], [1, nco]])
            else:
                dt_, ncols = {"in": (win_d, 2576), "out": (wout_d, 1024), "ff1": (wff1_d, 4096)}[kind]
                src = DAP(dt_, l * 1024 * ncols + c0, [[ncols, 128], [128 * ncols, 8], [1, nco]])
                dst = slot.ap(0, [[nco, 8], [1, nco]])
            P.dma("pool", dst, src, writes=slot.k(), dkey=("w", gi % 3))

        def w_acquire(kind, l, ahead=2):
            gi = w_next[0]
            while not (wtiles[gi][0] == kind and wtiles[gi][1] == l):
                gi += 1
            w_next[0] = gi + 1
            while w_issued[0] < min(len(wtiles), gi + 1 + ahead):
                w_issue(w_issued[0])
                w_issued[0] += 1
            nco = wtiles[gi][3] - wtiles[gi][2]
            return wring[gi % 3], nco

        def cload(buf, src, parts=128, dims=None):
            P.dma("sp", buf.ap(0, dims if dims else [[1, buf.n]], parts=parts), src, writes=buf.k(), dkey="c", group=True)

        cload(ident32, ident_d.ap())
        cload(tril, tril_d.ap())
        cload(hm4, hm4_d.ap())
        cload(bmk, bm_d.ap())
        cload(rmask, rmask_d.ap())
        cload(seqm, seqm_d.ap())
        cload(tokm, tokm_d.ap())
        if 'v' not in SKIP:
          cload(cvec, DAP(cvec_d, 0, [[NCV, 128], [128 * NCV, 2], [1, NCV]]), dims=[[NCV, 2], [1, NCV]])
        if 'g' not in SKIP:
          cload(gw2, DAP(gw2_d, 0, [[128, 16], [16 * 128, 2], [1, 128]]), parts=16, dims=[[128, 2], [1, 128]])
        if 'b' not in SKIP:
            cload(dlng, DAP(dln_d, 0, [[0, 128], [1, 1024]]))
        cload(dbias, DAP(dbt_d, 0, [[128, 128], [128 * 128, 8], [1, 128]]), dims=[[128, 8], [1, 128]])
        P.dve(I("tensor_copy", identb.c(0, 128), ident32.c(0, 128)), ident32.k(), identb.k())
        P.dve(I("memset", onesb.c(0, 128), 1.0), (), onesb.k())
        P.dve(I("tensor_copy", trilb.c(0, 128), tril.c(0, 128)), tril.k(), trilb.k())
        for l in range(2):
            P.dve(I("tensor_scalar", ngb.c(l, 1), cvec.c(l * NCV + 6, 1), -1.0, None, ALU.mult),
                  cvec.k(), ngb.k())
        P.dve(I("memset", Sbd32.c(0, 512), 0.0), (), Sbd32.k())
        P.dve(I("memset", stA.c(0, 8), 0.0), (), stA.k())
        P.dve(I("memset", stC.c(0, 120), 0.0), (), stC.k())

        mark = scr_top[0]
        wst = carve(16 * 128, F32)
        P.dma("sp", wst.ap(0, [[128, 16], [1, 128]]), DAP(dwst_d, 0, [[128, 128], [128 * 128, 16], [1, 128]]),
              writes=wst.k(), dkey="c", group=True)
        P.dve(I("tensor_tensor", wsT.ap(0, [[128, 16], [1, 128]]), wst.ap(0, [[128, 16], [1, 128]]),
                                        tril.ap(0, [[0, 16], [1, 128]]), ALU.mult), wst.k() + tril.k(), wsT.k())
        scr_top[0] = mark

        def layer(l, Sidx):
            nP = 1024
            has_s = Sidx == 1
            NT = nP + (128 if has_s else 0)
            TT = [(0, 512, "p"), (512, 512, "p")] + ([(1024, 128, "s")] if has_s else [])
            NCH = NT // 128
            cv = lambda col: cvec.c(l * NCV + col, 1)
            mark0 = scr_top[0]
            ymix = carve(8 * NT, BF16)

            def xb_t(kc, t0, n):
                return xb.c(kc * NTMAX + t0, n)

            def mm_fm(wb, nco, cc0, ccn, t0, n, pb):
                for kc in range(8):
                    P.pe(I("matmul", pb.c(0, n, parts=ccn), wb.ap(kc * nco + cc0, [[1, ccn]]), xb_t(kc, t0, n),
                                                    start=(kc == 0), stop=(kc == 7)),
                         wb.k() + kxb(t0, kc), pb.k())

            def cview(buf, j, L, H, t0, n, kind, shift=0):
                if kind == "p":
                    return buf.ap(j * L + t0 + shift, [[1, n]])
                return buf.ap(j * L + H + nP + shift, [[H + 8, 16], [1, 8]])

            def kcv(buf, j, L, H, t0, n, kind, lo, hi):
                if kind == "p":
                    return buf.k(j * L + t0 + lo, j * L + t0 + n + hi)
                return buf.k(j * L + H + nP, j * L + H + nP + 16 * (H + 8))

            def tview(buf, base, t0, n, kind):
                if kind == "p":
                    return buf.ap(base + t0, [[1, n]])
                return buf.ap(base + t0, [[8, 16], [1, 8]])

            stage(2)
            LA = 2 + nP + (160 if has_s else 0)
            mA = scr_top[0]
            ab32 = carve(2 * NT, F32)
            ac32 = carve(2 * NT, F32)
            gAb = carve(2 * LA, BF16)
            diagA = carve(6 * 128, BF16)
            gst = carve(2 * 34, F32)
            for j in range(2):
                P.dve(I("tensor_tensor", diagA.ap(j * 3 * 128, [[128, 3], [1, 128]]), ident32.ap(0, [[0, 3], [1, 128]]),
                        cvec.ap(l * NCV + j * 3, [[1, 3], [0, 128]]), ALU.mult), ident32.k() + cvec.k(), diagA.k())
                P.dve(I("tensor_copy", gAb.c(j * LA, 2), stA.c((l * 2 + j) * 2, 2)), stA.k(), gAb.k(j * LA, j * LA + 2))
            if has_s:
                sta_st = carve(256, F32)
                P.dma("sp", sta_st.ap(0, [[1, 256]], parts=32), DAP(sta_d, l * 32 * 256, [[256, 32], [1, 256]]),
                      writes=sta_st.k(), dkey="ld", group=False)
                for j in range(2):
                    pb = ps_next()
                    transpose(pb.c(0, 32), sta_st.ap(j * 128, [[1, 128]], parts=32), 32, sta_st.k(), pb.k())
                    P.act(I("activation", gAb.ap(j * LA + 2 + nP, [[10, 16], [1, 2]]),
                                                             pb.ap(0, [[2, 16], [1, 2]]), AF.Copy), pb.k(), gAb.k())
            wb, nco = w_acquire("in", l)
            for ci, dst in ((0, ab32), (1, ab32), (2, ac32), (3, ac32)):
                j = ci % 2
                for (t0, n, kind) in TT:
                    pb = ps_next()
                    mm_fm(wb, nco, ci * 128, 128, t0, n, pb)
                    evac_copy(dst.c(j * NT + t0, n), pb.c(0, n), pb.k(), dst.k(j * NT + t0, j * NT + t0 + n))
            wb2, nco2 = w_acquire("in", l)
            for j in range(2):
                for (t0, n, kind) in TT:
                    pb = ps_next()
                    mm_fm(wb2, nco2, j * 128, 128, t0, n, pb)
                    P.dve(I("tensor_tensor",
                        cview(gAb, j, LA, 2, t0, n, kind, shift=2 if kind == "p" else 2), tview(pb, 0, 0, n, kind),
                        tview(ac32, j * NT, t0, n, kind), ALU.mult), pb.k() + ac32.k(j * NT + t0, j * NT + t0 + n), kcv(gAb, j, LA, 2, t0, n, kind, 2, 2))
                    if kind == "p" and Sidx == 1 and t0 == 512:
                        P.dve(I("tensor_tensor", gst.c(j * 34, 2), pb.c(510, 2),
                                                                    ac32.c(j * NT + 1022, 2), ALU.mult),
                              pb.k() + ac32.k(), gst.k())
                    if kind == "s":
                        P.dve(I("tensor_tensor",
                            gst.ap(j * 34 + 2, [[2, 16], [1, 2]]), pb.ap(6, [[8, 16], [1, 2]]),
                            ac32.ap(j * NT + 1024 + 6, [[8, 16], [1, 2]]), ALU.mult), pb.k() + ac32.k(), gst.k())
            if Sidx == 0:
                for j in range(2):
                    P.dve(I("tensor_copy", stA.c((l * 2 + j) * 2, 2), gAb.c(j * LA + nP, 2)), gAb.k(j * LA + nP, j * LA + nP + 2), stA.k())
            for j in range(2):
                for (t0, n, kind) in TT:
                    pb = ps_next()
                    for k in range(3):
                        P.pe(I("matmul",
                            tview(pb, 0, 0, n, kind), diagA.c((j * 3 + k) * 128, 128), cview(gAb, j, LA, 2, t0, n, kind, shift=k),
                            start=(k == 0), stop=(k == 2)), diagA.k() + kcv(gAb, j, LA, 2, t0, n, kind, 0, 2), pb.k())
                    P.dve(I("tensor_tensor",
                        ymix.c((0 + j) * NT + t0, n), pb.c(0, n), ab32.c(j * NT + t0, n), ALU.mult),
                        pb.k() + ab32.k(j * NT + t0, j * NT + t0 + n), ymix.k(j * NT + t0, j * NT + t0 + n))
            if has_s:
                stg_a = carve(256, F32)
                for j in range(2):
                    pb = ps_next()
                    transpose(pb.c(0, 128, parts=34), gst.c(j * 34, 34), 128, gst.k(), pb.k())
                    P.act(I("activation", stg_a.c(j * 128, 128, parts=34), pb.c(0, 128, parts=34), AF.Copy),
                          pb.k(), stg_a.k())
                P.dma("sp", DAP(nap_d, l * 512, [[256, 2], [1, 256]]), stg_a.c(0, 256, parts=2), reads=stg_a.k(), dkey=("o", 1))
                P.dma("sp", DAP(nas_d, l * 32 * 256, [[256, 32], [1, 256]]), stg_a.c(0, 256, parts=32, p0=2),
                      reads=stg_a.k(), dkey=("o", 2))
            scr_top[0] = mA

            stage(3)
            mB = scr_top[0]
            qeT = carve(NT, BF16)
            keT = carve(NT, BF16)
            gsb = carve(2 * NT, BF16)
            vtm = carve(NCH * 256, BF16)
            ketm = carve(NCH * 128, BF16)
            elast = carve(32, F32)
            if has_s:
                S0c = carve(16 * 64, F32)
                S0bdb = carve(16 * 256, BF16)
                qeTs = carve(16 * 128, BF16)
                ketms = carve(16 * 128, BF16)
            mB2 = scr_top[0]
            B1 = carve(NT, F32)
            B2 = carve(NT, F32)
            B3 = carve(NT, F32)
            B4 = carve(NT, F32)
            zlr = carve(NT, F32)
            wb4, nco4 = w_acquire("in", l, ahead=1)
            for (t0, n, kind) in TT:
                pb = ps_next()
                mm_fm(wb4, nco4, 0, 16, t0, n, pb)
                evac_copy(zlr.c(t0, n, parts=16), pb.c(0, n, parts=16), pb.k(), zlr.k(t0, t0 + n))
            for (t0, n, kind) in TT:
                pb = ps_next()
                P.pe(I("matmul", pb.c(0, n), gw2.c(l * 128, 128, parts=16), zlr.c(t0, n, parts=16),
                       start=True, stop=True), gw2.k() + zlr.k(t0, t0 + n), pb.k())
                P.act(I("activation", B3.c(t0, n), pb.c(0, n), AF.Exp, bias=ngb.c(l, 1), scale=-1.0),
                      pb.k() + ngb.k(), B3.k(t0, t0 + n))
            P.act(I("activation", B3.c(0, NT), B3.c(0, NT), AF.Ln, bias=1.0, scale=1.0), B3.k(), B3.k())
            P.dve(I("tensor_tensor_scan", B4.c(0, NT), rmask.c(0, NT), B3.c(0, NT), 0.0, ALU.mult, ALU.add),
                  rmask.k() + B3.k(), B4.k())
            P.act(I("activation", B3.c(0, NT), B4.c(0, NT), AF.Exp, scale=-1.0 / 16.0), B4.k(), B3.k())
            P.act(I("activation", B4.c(0, NT), B4.c(0, NT), AF.Exp, scale=1.0 / 16.0), B4.k(), B4.k())
            P.dve(I("tensor_copy", elast.ap(0, [[1, 8]]), B3.ap(127, [[128, 8]])), B3.k(), elast.k())
            if has_s:
                P.dve(I("tensor_copy", elast.ap(8, [[1, 16]]), B3.ap(1024 + 7, [[8, 16]])), B3.k(), elast.k())
            for ci, dst in ((2, B1), (3, B2)):
                for (t0, n, kind) in TT:
                    pb = ps_next()
                    mm_fm(wb2, nco2, ci * 128, 128, t0, n, pb)
                    evac_copy(dst.c(t0, n), pb.c(0, n), pb.k(), dst.k(t0, t0 + n))
            P.dve(I("scalar_tensor_tensor", qeT.c(0, NT), B1.c(0, NT), float(32 ** -0.5), B3.c(0, NT), ALU.mult, ALU.mult),
                  B1.k() + B3.k(), qeT.k())
            P.dve(I("tensor_tensor", keT.c(0, NT), B2.c(0, NT), B4.c(0, NT), ALU.mult), B2.k() + B4.k(), keT.k())
            wb3, nco3 = w_acquire("in", l, ahead=1)
            for c in range(NCH):
                pb = ps_next()
                for kc in range(8):
                    P.pe(I("matmul", pb.c(0, 256), xb_t(kc, c * 128, 128), wb3.ap(kc * nco3, [[1, 256]]),
                           start=(kc == 0), stop=(kc == 7)), wb3.k() + kxb(c * 128, kc), pb.k())
                evac_copy(vtm.c(c * 256, 256), pb.c(0, 256), pb.k(), vtm.k(c * 256, c * 256 + 256))
            for j in range(2):
                for (t0, n, kind) in TT:
                    pb = ps_next()
                    mm_fm(wb3, nco3, 256 + j * 128, 128, t0, n, pb)
                    P.act(I("activation", gsb.c(j * NT + t0, n), pb.c(0, n), AF.Silu),
                          pb.k(), gsb.k(j * NT + t0, j * NT + t0 + n))
            scr_top[0] = mB2
            NPC = 8
            keTm_all = carve(4 * NT, BF16)
            attm_all = carve(NCH * 512, BF16)
            t1_all = carve(NCH * 256, F32)
            Sbdb_all = carve((NPC + 1) * 256, BF16)
            osb_all = carve(NCH * 256, F32)
            on_all = alias(attm_all, NCH * 256, BF16)
            ssa = carve(NCH * 8, F32)
            atmp = [carve(512, BF16) for _ in range(2)]
            ytmp = [carve(128, BF16) for _ in range(3)]
            if has_s:
                t1s = alias(keTm_all, 4 * 256, F32)
                reds = alias(keTm_all, 4 * 64, F32, byte_off=4096)
            for h in range(4):
                P.dve(I("tensor_scalar", keTm_all.c(h * NT, NT), keT.c(0, NT), hm4.c(h, 1), None, ALU.mult),
                      keT.k() + hm4.k(), keTm_all.k(h * NT, h * NT + NT))
            if has_s:
                P.dma("sp", S0c.ap(0, [[64, 16], [1, 64]]), DAP(stg_d, l * 16 * 8192, [[64, 128], [8192, 16], [1, 64]]),
                      writes=S0c.k(), dkey="ld2", group=False)
                P.dve(I("tensor_tensor", S0bdb.ap(0, [[256, 16], [64, 4], [1, 64]]), S0c.ap(0, [[64, 16], [0, 4], [1, 64]]),
                        hm4.ap(0, [[0, 16], [1, 4], [0, 64]]), ALU.mult), S0c.k() + hm4.k(), S0bdb.k())
            for c in range(NCH):
                c0 = c * 128
                pbt = ps_next(bf=True)
                transpose(pbt.c(0, 128), keT.c(c0, 128), 128, keT.k(c0, c0 + 128), pbt.k(), bf=True)
                evac_copy(ketm.c(c * 128, 128), pbt.c(0, 128), pbt.k(), ketm.k(c * 128, c * 128 + 128), eng="act")
            for c in range(NCH):
                kind = "p" if c < 8 else "s"
                c0 = c * 128
                pa = ps_next()
                for h in range(4):
                    P.pe(I("matmul", pa.c(h * 128, 128), keTm_all.c(h * NT + c0, 128), qeT.c(c0, 128), start=True, stop=True),
                         keTm_all.k(h * NT, h * NT + NT) + qeT.k(c0, c0 + 128), pa.k())
                mk_ = trilb if kind == "p" else sblkb
                ta = atmp[c % 2]
                P.act(I("activation", ta.c(0, 512), pa.c(0, 512), AF.Copy), pa.k(), ta.k())
                P.dve(I("tensor_tensor", attm_all.ap(c * 512, [[128, 4], [1, 128]]), ta.ap(0, [[128, 4], [1, 128]]),
                        mk_.ap(0, [[0, 4], [1, 128]]), ALU.mult), ta.k() + mk_.k(), attm_all.k(c * 512, c * 512 + 512))
                if kind == "p":
                    pd = ps_next()
                    P.pe(I("matmul", pd.c(0, 256), ketm.c(c * 128, 128), vtm.c(c * 256, 256), start=True, stop=True),
                         ketm.k(c * 128, c * 128 + 128) + vtm.k(c * 256, c * 256 + 256), pd.k())
                    P.dve(I("scalar_tensor_tensor", t1_all.c(c * 256, 256), pd.c(0, 256), elast.c(c, 1), bmk.c(0, 256),
                            ALU.mult, ALU.mult), pd.k() + elast.k() + bmk.k(), t1_all.k(c * 256, c * 256 + 256))
            P.dve(I("tensor_copy", Sbdb_all.c(0, 256), Sbd32.c(l * 256, 256)), Sbd32.k(), Sbdb_all.k(0, 256))
            for c in range(NPC):
                P.dve(I("scalar_tensor_tensor", Sbd32.c(l * 256, 256), Sbd32.c(l * 256, 256), elast.c(c, 1), t1_all.c(c * 256, 256),
                        ALU.mult, ALU.add), Sbd32.k() + elast.k() + t1_all.k(c * 256, c * 256 + 256), Sbd32.k())
                P.dve(I("tensor_copy", Sbdb_all.c((c + 1) * 256, 256), Sbd32.c(l * 256, 256)), Sbd32.k(),
                      Sbdb_all.k((c + 1) * 256, (c + 2) * 256))
            if Sidx == 1:
                for h in range(4):
                    P.dma("sp", DAP(ngp_d, l * 8192 + h * 2048, [[64, 32], [1, 64]]),
                          Sbd32.c(l * 256 + h * 64, 64, parts=32, p0=32 * h), reads=Sbd32.k(), dkey=("o", 3))
            for c in range(NCH):
                kind = "p" if c < 8 else "s"
                c0 = c * 128
                po = ps_next()
                if kind == "p":
                    P.pe(I("matmul", po.c(0, 256), qeT.c(c0, 128), Sbdb_all.c(c * 256, 256), start=True, stop=False, skip_group_check=True),
                         qeT.k(c0, c0 + 128) + Sbdb_all.k(c * 256, c * 256 + 256), po.k())
                else:
                    P.dve(I("tensor_tensor", qeTs.ap(0, [[128, 16], [1, 128]]), qeT.ap(c0, [[0, 16], [1, 128]]),
                            seqm.ap(0, [[128, 16], [1, 128]]), ALU.mult), qeT.k(c0, c0 + 128) + seqm.k(), qeTs.k())
                    for q in range(16):
                        P.pe(I("matmul", po.c(0, 256), qeTs.c(q * 128, 128), S0bdb.c(q * 256, 256),
                               start=(q == 0), stop=False, skip_group_check=True), qeTs.k() + S0bdb.k(), po.k())
                for h in range(4):
                    P.pe(I("matmul", po.c(h * 64, 64), attm_all.c(c * 512 + h * 128, 128), vtm.c(c * 256 + h * 64, 64),
                           start=False, stop=(h == 3), skip_group_check=True),
                         attm_all.k(c * 512, c * 512 + 512) + vtm.k(c * 256, c * 256 + 256), po.k())
                evac_copy(osb_all.c(c * 256, 256), po.c(0, 256), po.k(), osb_all.k(c * 256, c * 256 + 256), eng="act")
            NO = NCH * 256
            P.act(I("activation", t1_all.c(0, NO), osb_all.c(0, NO), AF.Square), osb_all.k(), t1_all.k())
            P.dve(I("tensor_reduce", ssa.c(0, NCH * 4), t1_all.ap(0, [[64, NCH * 4], [1, 64]]), AX.X, ALU.add), t1_all.k(), ssa.k())
            P.act(I("activation", ssa.c(NCH * 4, NCH * 4), ssa.c(0, NCH * 4), AF.Ln, bias=cEPS.c(0, 1), scale=1.0 / 64.0),
                  ssa.k() + cEPS.k(), ssa.k())
            P.act(I("activation", ssa.c(0, NCH * 4), ssa.c(NCH * 4, NCH * 4), AF.Exp, scale=-0.5), ssa.k(), ssa.k())
            P.dve(I("tensor_tensor", on_all.ap(0, [[64, NCH * 4], [1, 64]]), osb_all.ap(0, [[64, NCH * 4], [1, 64]]),
                    ssa.ap(0, [[1, NCH * 4], [0, 64]]), ALU.mult), osb_all.k() + ssa.k(), on_all.k())
            for c in range(NCH):
                c0 = c * 128
                for j in range(2):
                    pt = ps_next(bf=True)
                    transpose(pt.c(0, 128), on_all.c(c * 256 + j * 128, 128), 128, on_all.k(), pt.k(), bf=True)
                    yt = ytmp[(2 * c + j) % 3]
                    P.act(I("activation", yt.c(0, 128), pt.c(0, 128), AF.Copy, scale=cv(7 + j)), pt.k() + cvec.k(), yt.k())
                    P.dve(I("tensor_tensor", ymix.c((2 + j) * NT + c0, 128), yt.c(0, 128), gsb.c(j * NT + c0, 128), ALU.mult),
                          yt.k() + gsb.k(j * NT + c0, j * NT + c0 + 128), ymix.k((2 + j) * NT + c0, (2 + j) * NT + c0 + 128))
            if has_s:
                c = 8
                P.dve(I("tensor_tensor", ketms.ap(0, [[128, 16], [1, 128]]), ketm.ap(c * 128, [[0, 16], [1, 128]]),
                        tokm.ap(0, [[1, 16], [0, 128]]), ALU.mult), ketm.k(c * 128, c * 128 + 128) + tokm.k(), ketms.k())
                for rd in range(4):
                    pd2 = ps_next2()
                    for qq in range(4):
                        q = rd * 4 + qq
                        P.pe(I("matmul", pd2.c(qq * 256, 256), ketms.c(q * 128, 128), vtm.c(c * 256, 256), start=True, stop=True),
                             ketms.k() + vtm.k(c * 256, c * 256 + 256), pd2.k())
                    for h in range(4):
                        P.dve(I("tensor_tensor", reds.ap(0, [[64, 4], [1, 64]], parts=32, p0=32 * h),
                                pd2.ap(64 * h, [[256, 4], [1, 64]], parts=32, p0=32 * h),
                                S0c.ap(rd * 256, [[64, 4], [1, 64]], parts=32, p0=32 * h), ALU.add),
                              pd2.k() + S0c.k(), reds.k())
                    P.dve(I("tensor_tensor", S0c.ap(rd * 256, [[64, 4], [1, 64]]), reds.ap(0, [[64, 4], [1, 64]]),
                            elast.ap(8 + rd * 4, [[1, 4], [0, 64]]), ALU.mult), reds.k() + elast.k(), S0c.k())
                P.dma("sp", DAP(ngs_d, l * 16 * 8192, [[64, 128], [8192, 16], [1, 64]]), S0c.ap(0, [[64, 16], [1, 64]]),
                      reads=S0c.k(), dkey=("o", 4))
            scr_top[0] = mB

            stage(4)
            LC = 30 + nP + (16 * 38 if has_s else 0)
            mC = scr_top[0]
            ca32 = carve(2 * NT, F32)
            cv32 = alias(ca32, 2 * NT, F32)
            gCb = carve(2 * LC, BF16)
            diagC = carve(62 * 128, BF16)
            cst = carve(2 * 158, F32)
            sgt2 = [carve(512, F32) for _ in range(2)]
            sg_i = [0]
            cset = (carve(2 * 512, BF16), carve(2 * 512, BF16), carve(512, F32), carve(512, F32), carve(512, F32))
            ctmp = sgt2
            u16 = carve(2 * NT, BF16)
            vv32 = carve(NCH * 256, F32)
            vvm = carve(NCH * 512, BF16)
            stt = carve(NCH * 8, F32)
            mv = carve(NCH * 2, F32)
            rsd = carve(NCH, F32)
            nmr = carve(NCH, F32)
            ftm = [carve(128, F32) for _ in range(2)]
            for j in range(2):
                P.dve(I("tensor_tensor", diagC.ap(j * 31 * 128, [[128, 31], [1, 128]]), ident32.ap(0, [[0, 31], [1, 128]]),
                        cvec.ap(l * NCV + 9 + j * 31, [[1, 31], [0, 128]]), ALU.mult),
                      ident32.k() + cvec.k(), diagC.k(j * 31 * 128, (j + 1) * 31 * 128))
                P.dve(I("tensor_copy", gCb.c(j * LC, 30), stC.c((l * 2 + j) * 30, 30)), stC.k(), gCb.k(j * LC, j * LC + 30))
            P.dve(I("memset", vvm.c(0, NCH * 512), 0.0), (), vvm.k())
            if has_s:
                stc_st = carve(4 * 256, F32)
                P.dma("sp", stc_st.ap(0, [[256, 4], [1, 256]], parts=120), DAP(stc_d, l * 480 * 256, [[256, 120], [120 * 256, 4], [1, 256]]),
                      writes=stc_st.k(), dkey="ld3", group=False)
                for grp in range(4):
                    for j in range(2):
                        pb = ps_next()
                        transpose(pb.c(0, 120), stc_st.ap(grp * 256 + j * 128, [[1, 128]], parts=120), 120, stc_st.k(), pb.k())
                        evac_copy(gCb.ap(j * LC + 30 + nP + grp * 4 * 38, [[38, 4], [1, 30]]), pb.ap(0, [[30, 4], [1, 30]]), pb.k(), gCb.k(j * LC + 30 + nP, j * LC + 30 + nP + 16 * 38))
                P.dma("sp", DAP(ncs_d, l * 480 * 256, [[30 * 256, 16], [1, 22 * 256]]),
                      DAP(stc_d, l * 480 * 256 + 8 * 256, [[30 * 256, 16], [1, 22 * 256]]), dkey=("o", 5))
                stg_c = alias(stc_st, 256, F32)
                stg_c2 = alias(stc_st, 256, F32, byte_off=1024)

            def C1():
                for j in range(2):
                    for (t0, n, kind) in TT:
                        pb = ps_next()
                        mm_fm(wb4, nco4, 16 + j * 128, 128, t0, n, pb)
                        evac_copy(ca32.c(j * NT + t0, n), pb.c(0, n), pb.k(), ca32.k(j * NT + t0, j * NT + t0 + n), eng="dve")
                for j in range(2):
                    for (t0, n, kind) in TT:
                        pb = ps_next()
                        mm_fm(wb4, nco4, 272 + j * 128, 128, t0, n, pb)
                        sgt = sgt2[sg_i[0] % 2]
                        sg_i[0] += 1
                        P.act(I("activation", sgt.c(0, n), pb.c(0, n), AF.Sigmoid), pb.k(), sgt.k())
                        P.dve(I("tensor_tensor", cview(gCb, j, LC, 30, t0, n, kind, shift=30), tview(sgt, 0, 0, n, kind),
                                tview(ca32, j * NT, t0, n, kind), ALU.mult), sgt.k() + ca32.k(j * NT + t0, j * NT + t0 + n), kcv(gCb, j, LC, 30, t0, n, kind, 30, 30))
                        if kind == "p" and Sidx == 1 and t0 == 512:
                            P.dve(I("tensor_tensor", cst.c(j * 158, 30), sgt.c(482, 30), ca32.c(j * NT + 994, 30), ALU.mult),
                                  sgt.k() + ca32.k(j * NT + t0, j * NT + t0 + n), cst.k())
                        if kind == "s":
                            P.dve(I("tensor_tensor", cst.c(j * 158 + 30, 128), sgt.c(0, 128), ca32.c(j * NT + 1024, 128), ALU.mult),
                                  sgt.k() + ca32.k(j * NT + t0, j * NT + t0 + n), cst.k())
                if Sidx == 0:
                    for j in range(2):
                        P.dve(I("tensor_copy", stC.c((l * 2 + j) * 30, 30), gCb.c(j * LC + nP, 30)), gCb.k(j * LC + nP, j * LC + nP + 30), stC.k())

            def C2():
                for j in range(2):
                    for (t0, n, kind) in TT:
                        pb = ps_next()
                        for k in range(31):
                            P.pe(I("matmul", tview(pb, 0, 0, n, kind), diagC.c((j * 31 + k) * 128, 128),
                                   cview(gCb, j, LC, 30, t0, n, kind, shift=k), start=(k == 0), stop=(k == 30)),
                                 diagC.k((j * 31 + k) * 128, (j * 31 + k + 1) * 128) + kcv(gCb, j, LC, 30, t0, n, kind, 0, 30), pb.k())
                        P.dve(I("tensor_scalar", cv32.c(j * NT + t0, n), pb.c(0, n), cv(71 + j), None, ALU.add),
                              pb.k() + cvec.k(), cv32.k(j * NT + t0, j * NT + t0 + n))

            def C3():
                cb16, csq, mean, msq, rstd = cset
                for ti, (t0, n, kind) in enumerate(TT):
                    for j in range(2):
                        P.dve(I("tensor_copy", cb16.c(j * 512, n), cv32.c(j * NT + t0, n)), cv32.k(j * NT + t0, j * NT + t0 + n),
                              cb16.k(j * 512, j * 512 + n))
                        P.act(I("activation", csq.c(j * 512, n), cv32.c(j * NT + t0, n), AF.Square), cv32.k(j * NT + t0, j * NT + t0 + n),
                              csq.k(j * 512, j * 512 + n))
                    p1 = ps_next()
                    p2 = ps_next()
                    for j in range(2):
                        P.pe(I("matmul", p1.c(0, n), onesb.c(0, 128), cb16.c(j * 512, n), start=(j == 0), stop=(j == 1)),
                             onesb.k() + cb16.k(j * 512, j * 512 + n), p1.k())
                    for j in range(2):
                        P.pe(I("matmul", p2.c(0, n), onesb.c(0, 128), csq.c(j * 512, n), start=(j == 0), stop=(j == 1)),
                             onesb.k() + csq.k(j * 512, j * 512 + n), p2.k())
                    ln_stats(p1, p2, n, 1.0 / 256.0, mean, msq, rstd)
                    for j in range(2):
                        tmpn = ctmp[j]
                        P.dve(I("tensor_tensor", tmpn.c(0, n), cv32.c(j * NT + t0, n), mean.c(0, n), ALU.subtract),
                              cv32.k(j * NT + t0, j * NT + t0 + n) + mean.k(), tmpn.k())
                        P.dve(I("tensor_tensor", tmpn.c(0, n), tmpn.c(0, n), rstd.c(0, n), ALU.mult), tmpn.k() + rstd.k(), tmpn.k())
                        P.act(I("activation", ymix.c((4 + j) * NT + t0, n), tmpn.c(0, n), AF.Silu, bias=cv(75 + j), scale=cv(73 + j)),
                              tmpn.k() + cvec.k(), ymix.k((4 + j) * NT + t0, (4 + j) * NT + t0 + n))
                if has_s:
                    for j in range(2):
                        pb = ps_next()
                        transpose(pb.c(0, 128, parts=30), cst.c(j * 158, 30), 128, cst.k(), pb.k())
                        evac_copy(stg_c.c(j * 128, 128, parts=30), pb.c(0, 128, parts=30), pb.k(), stg_c.k())
                        pb = ps_next()
                        transpose(pb.c(0, 128), cst.c(j * 158 + 30, 128), 128, cst.k(), pb.k())
                        evac_copy(stg_c2.c(j * 128, 128), pb.c(0, 128), pb.k(), stg_c2.k())
                    P.dma("sp", DAP(ncp_d, l * 30 * 256, [[256, 30], [1, 256]]), stg_c.c(0, 256, parts=30), reads=stg_c.k(), dkey=("o", 6))
                    for q in range(16):
                        P.dma("sp", DAP(ncs_d, l * 480 * 256 + q * 30 * 256 + 22 * 256, [[256, 8], [1, 256]]),
                              stg_c2.c(0, 256, parts=8, p0=8 * q), reads=stg_c2.k(), dkey=("o", 7))

            def D1():
                for j in range(2):
                    for (t0, n, kind) in TT:
                        pb = ps_next()
                        mm_fm(wb5, nco5, j * 128, 128, t0, n, pb)
                        P.act(I("activation", u16.c(j * NT + t0, n), pb.c(0, n), AF.Gelu_apprx_tanh),
                              pb.k(), u16.k(j * NT + t0, j * NT + t0 + n))

            def D2():
                for c in range(NCH):
                    pb = ps_next()
                    for kc in range(8):
                        P.pe(I("matmul", pb.c(0, 256), xb_t(kc, c * 128, 128), wb5.ap(kc * nco5 + 256, [[1, 256]]),
                               start=(kc == 0), stop=(kc == 7)), wb5.k() + kxb(c * 128, kc), pb.k())
                    P.act(I("activation", vv32.c(c * 256, 256), pb.c(0, 256), AF.Gelu_apprx_tanh), pb.k(), vv32.k(c * 256, c * 256 + 256))
                    P.dve(I("bn_stats", stt.c(c * 8, 6), vv32.c(c * 256, 256)), vv32.k(c * 256, c * 256 + 256), stt.k())
                    P.dve(I("bn_aggr", mv.c(c * 2, 2), stt.c(c * 8, 6)), stt.k(), mv.k())

            def D3():
                P.act(I("activation", rsd.c(0, NCH), mv.ap(1, [[2, NCH]]), AF.Ln, bias=cEPS.c(0, 1), scale=1.0), mv.k() + cEPS.k(), rsd.k())
                P.act(I("activation", rsd.c(0, NCH), rsd.c(0, NCH), AF.Exp, scale=-0.5), rsd.k(), rsd.k())
                vall = vv32.ap(0, [[256, NCH], [1, 256]])
                P.dve(I("scalar_tensor_tensor", nmr.c(0, NCH), mv.ap(0, [[2, NCH]]), -1.0, rsd.c(0, NCH), ALU.mult, ALU.mult),
                      mv.k() + rsd.k(), nmr.k())
                for c in range(NCH):
                    P.act(I("activation", vv32.c(c * 256, 256), vv32.c(c * 256, 256), AF.Identity, bias=nmr.c(c, 1), scale=rsd.c(c, 1)),
                          vv32.k(c * 256, c * 256 + 256) + nmr.k() + rsd.k(), vv32.k(c * 256, c * 256 + 256))
                P.dve(I("tensor_tensor", vall, vall, dlng.ap((l * 2 + 0) * 256, [[0, NCH], [1, 256]]), ALU.mult), vv32.k() + dlng.k(), vv32.k())
                P.dve(I("tensor_tensor", vall, vall, dlng.ap((l * 2 + 1) * 256, [[0, NCH], [1, 256]]), ALU.add), vv32.k() + dlng.k(), vv32.k())
                for j in range(2):
                    evac_copy(vvm.ap(j * 256, [[512, NCH], [192, 2], [1, 64]]), vv32.ap(j * 128, [[256, NCH], [64, 2], [1, 64]]),
                              vv32.k(), vvm.k())
                if has_s:
                    P.dma("sp", DAP(nvs_d, l * 128 * 256, [[256, 128], [1, 256]]), vv32.c(8 * 256, 256), reads=vv32.k(), dkey=("o", 8))

            def D4():
                fi = 0
                for c in range(NCH):
                    kd = 0 if c < 8 else 1
                    for j in range(2):
                        pb = ps_next()
                        for gg in range(2):
                            g = 2 * j + gg
                            P.pe(I("matmul", pb.c(0, 128), vvm.c(c * 512 + g * 128, 128), wsT.c(((l * 2 + kd) * 4 + g) * 128, 128),
                                   start=(gg == 0), stop=(gg == 1)), vvm.k(c * 512 + g * 128, c * 512 + g * 128 + 128) + wsT.k(), pb.k())
                        ft = ftm[fi % 2]
                        fi += 1
                        P.dve(I("tensor_tensor", ft.c(0, 128), pb.c(0, 128), dbias.c(((l * 2 + kd) * 2 + j) * 128, 128), ALU.add),
                              pb.k() + dbias.k(), ft.k())
                        P.dve(I("tensor_tensor", ymix.c((6 + j) * NT + c * 128, 128), ft.c(0, 128),
                                u16.c(j * NT + c * 128, 128), ALU.mult), ft.k() + u16.k(j * NT + c * 128, j * NT + c * 128 + 128),
                              ymix.k((6 + j) * NT + c * 128, (6 + j) * NT + c * 128 + 128))

            C1()
            stage(5)
            wb5, nco5 = w_acquire("in", l)
            D2()
            D3()
            C2()
            C3()
            D1()
            D4()
            scr_top[0] = mC

            stage(6)
            class StatAcc:
                def __init__(self, lag, nfc=8):
                    self.lag = lag
                    self.nfc = nfc
                    self.banks = [(ps_reserve(), ps_reserve()) for _ in TT]
                    self.ring = [(carve(512, BF16), carve(512, BF16)) for _ in range(lag + 2)]
                    self.i = 0
                    self.pending = []

                def add(self, fc, ti, t0, n):
                    xbt, sqt_ = self.ring[self.i % len(self.ring)]
                    self.i += 1
                    P.act(I("activation", xbt.c(0, n), x32.c(fc * NTMAX + t0, n), AF.Copy), kx32(t0, fc), xbt.k())
                    P.act(I("activation", sqt_.c(0, n), x32.c(fc * NTMAX + t0, n), AF.Square), kx32(t0, fc), sqt_.k())
                    self.pending.append((fc, ti, n, xbt, sqt_))
                    while len(self.pending) > self.lag:
                        self._flush()

                def _flush(self):
                    fc, ti, n, xbt, sqt_ = self.pending.pop(0)
                    (p1, _), (p2, _) = self.banks[ti]
                    P.pe(I("matmul", p1.c(0, n), onesb.c(0, 128), xbt.c(0, n), start=(fc == 0), stop=(fc == self.nfc - 1)),
                         onesb.k() + xbt.k(), p1.k())
                    P.pe(I("matmul", p2.c(0, n), onesb.c(0, 128), sqt_.c(0, n), start=(fc == 0), stop=(fc == self.nfc - 1)),
                         onesb.k() + sqt_.k(), p2.k())

                def finish(self, mean_, msq_, rstd_):
                    while self.pending:
                        self._flush()
                    for ti, (t0, n, kind) in enumerate(TT):
                        (p1, i1_), (p2, i2_) = self.banks[ti]
                        ln_stats(p1, p2, n, 1.0 / 1024.0, mean_.sub(t0, n), msq_.sub(t0, n), rstd_.sub(t0, n))
                        ps_reserved.discard(i1_)
                        ps_reserved.discard(i2_)

            def resid_ln(acc, gcol, bcol):
                mean_, msq_, rstd_ = lnset
                acc.finish(mean_, msq_, rstd_)
                for fc in range(8):
                    kall32 = [("x32", fc, ti) for ti in range(len(TT))]
                    kallb = [("xb", fc, ti) for ti in range(len(TT))]
                    tn = tmpn3[fc % len(tmpn3)]
                    P.dve(I("tensor_tensor", tn.c(0, NT), x32.c(fc * NTMAX, NT), mean_.c(0, NT), ALU.subtract),
                          kall32 + mean_.k(), tn.k())
                    P.dve(I("tensor_tensor", tn.c(0, NT), tn.c(0, NT), rstd_.c(0, NT), ALU.mult), tn.k() + rstd_.k(), tn.k())
                    if fc != 7:
                        P.act(I("activation", x32.c(fc * NTMAX, NT), tn.c(0, NT), AF.Identity,
                                bias=cv(bcol + fc), scale=cv(gcol + fc)), tn.k() + cvec.k(), kall32)
                    else:
                        P.dve(I("tensor_scalar", x32.c(fc * NTMAX, NT), tn.c(0, NT), cv(gcol + fc), cv(bcol + fc), ALU.mult, ALU.add),
                              tn.k() + cvec.k(), kall32)
                    P.act(I("activation", xb.c(fc * NTMAX, NT), tn.c(0, NT), AF.Identity,
                            bias=cv(bcol + fc), scale=cv(gcol + fc)), tn.k() + cvec.k(), kallb)

            def carve_ln():
                return ((carve(NT, F32), carve(NT, F32), carve(NT, F32)), [carve(NT, F32) for _ in range(4)])

            lnscr = scr_top[0]
            lnset, tmpn3 = carve_ln()
            acc1 = StatAcc(lag=6)
            for half in range(2):
                wbo, ncoo = w_acquire("out", l)
                for cc in range(4):
                    fc = half * 4 + cc
                    for (t0, n, kind) in TT:
                        pb = ps_next()
                        for kc in range(8):
                            P.pe(I("matmul",
                                pb.c(0, n), wbo.ap(kc * ncoo + cc * 128, [[1, 128]]), ymix.c(kc * NT + t0, n),
                                start=(kc == 0), stop=(kc == 7)), wbo.k() + ymix.k(kc * NT + t0, kc * NT + t0 + n), pb.k())
                        if 'dbgmix' in SKIP:
                            P.dve(I("tensor_copy", x32.c(fc * NTMAX + t0, n), pb.c(0, n)), pb.k(), kx32(t0, fc))
                        else:
                            P.dve(I("scalar_tensor_tensor",
                                x32.c(fc * NTMAX + t0, n), x32.c(fc * NTMAX + t0, n), ALPHA, pb.c(0, n), ALU.mult, ALU.add),
                                kx32(t0, fc) + pb.k(), kx32(t0, fc))
                            acc1.add(fc, TT.index((t0, n, kind)), t0, n)
            if 'dbgmix' in SKIP:
                scr_top[0] = mark0
                return
            resid_ln(acc1, 77, 85)
            scr_top[0] = lnscr

            stage(7)
            if 'ffn' in SKIP:
                scr_top[0] = mark0
                return
            scr_top[0] = mark0
            hbuf = carve(32 * NT, BF16)
            sqt = [carve(512, F32) for _ in range(2)]
            si = 0
            for g in range(8):
                wbf, ncof = w_acquire("ff1", l)
                for cc in range(4):
                    hc = g * 4 + cc
                    for (t0, n, kind) in TT:
                        pb = ps_next()
                        mm_fm(wbf, ncof, cc * 128, 128, t0, n, pb)
                        s_ = sqt[si]
                        si = 1 - si
                        P.act(I("activation", s_.c(0, n), pb.c(0, n), AF.Square), pb.k(), s_.k())
                        P.dve(I("scalar_tensor_tensor",
                            hbuf.c(hc * NT + t0, n), pb.c(0, n), 0.0, s_.c(0, n), ALU.is_gt, ALU.mult),
                            pb.k() + s_.k(), hbuf.k(hc * NT + t0, hc * NT + t0 + n))
            acc2 = StatAcc(lag=2)
            for oc in range(8):
                wbf, ncof = w_acquire("ff2", l)
                for (t0, n, kind) in TT:
                    pb = ps_next()
                    for kc in range(32):
                        P.pe(I("matmul", pb.c(0, n), wbf.c(kc * 128, 128), hbuf.c(kc * NT + t0, n),
                                                                           start=(kc == 0), stop=(kc == 31)), wbf.k() + hbuf.k(kc * NT + t0, kc * NT + t0 + n), pb.k())
                    P.dve(I("scalar_tensor_tensor",
                        x32.c(oc * NTMAX + t0, n), x32.c(oc * NTMAX + t0, n), ALPHA, pb.c(0, n), ALU.mult, ALU.add),
                        kx32(t0, oc) + pb.k(), kx32(t0, oc))
                    acc2.add(oc, TT.index((t0, n, kind)), t0, n)

            scr_top[0] = mark0
            lnset, tmpn3 = carve_ln()
            resid_ln(acc2, 93, 101)
            scr_top[0] = mark0

        cEPS = mk("cEPS", 1, F32)
        P.dve(I("memset", cEPS.c(0, 1), EPS), (), cEPS.k())
        sblk = mk("sblk", 128, F32)
        sblk_d = din("sblk", [128, 128])
        P.dma("sp", sblk.c(0, 128), sblk_d.ap(), writes=sblk.k(), dkey="c", group=True)
        P.dve(I("tensor_copy", sblkb.c(0, 128), sblk.c(0, 128)), sblk.k(), sblkb.k())

        def ln_stats(p1, p2, n, inv, mean, msq, rstd):
            P.act(I("activation", mean.c(0, n), p1.c(0, n), AF.Copy, scale=inv), p1.k(), mean.k())
            P.act(I("activation", msq.c(0, n), p1.c(0, n), AF.Square, scale=inv), p1.k(), msq.k())
            P.dve(I("scalar_tensor_tensor", rstd.c(0, n), p2.c(0, n), inv, msq.c(0, n), ALU.mult, ALU.subtract),
                  p2.k() + msq.k(), rstd.k())
            P.act(I("activation", rstd.c(0, n), rstd.c(0, n), AF.Ln, bias=cEPS.c(0, 1), scale=1.0), rstd.k() + cEPS.k(), rstd.k())
            P.act(I("activation", rstd.c(0, n), rstd.c(0, n), AF.Exp, scale=-0.5), rstd.k(), rstd.k())

        def main_body():
            stage(1)
            for Sidx in range(2):
                nP = 1024
                NT = nP + (128 if Sidx == 1 else 0)
                NCH = NT // 128
                mark = scr_top[0]
                xst = [carve(1024, F32) for _ in range(2)]
                for c in range(2 if 'x2' in SKIP else NCH):
                    s_ = xst[c % 2]
                    if c < 8:
                        src = DAP(xp_d, (Sidx * 1024 + c * 128) * 1024, [[1024, 128], [1, 1024]])
                    else:
                        src = xs_d.ap()
                    P.dma("sp", s_.c(0, 1024), src, writes=s_.k(), dkey=("xin", c % 2))
                    for hf in range(2):
                        pb = ps_next()
                        for f4 in range(4):
                            fc = hf * 4 + f4
                            transpose(pb.c(f4 * 128, 128), s_.c(fc * 128, 128), 128, s_.k(), pb.k())
                        P.act(I("activation", x32.ap(hf * 4 * NTMAX + c * 128, [[NTMAX, 4], [1, 128]]),
                                                                        pb.ap(0, [[128, 4], [1, 128]]), AF.Copy), pb.k(), [k_ for f_ in range(hf * 4, hf * 4 + 4) for k_ in kx32(c * 128, f_)])
                        if 'x1' not in SKIP:
                            P.dve(I("tensor_copy", xb.ap(hf * 4 * NTMAX + c * 128, [[NTMAX, 4], [1, 128]]),
                                                                             pb.ap(0, [[128, 4], [1, 128]])), pb.k(), [k_ for f_ in range(hf * 4, hf * 4 + 4) for k_ in kxb(c * 128, f_)])
                scr_top[0] = mark
                for l in range(DEPTH):
                    layer(l, Sidx)
                mark = scr_top[0]
                yst = [carve(1024, F32) for _ in range(2)]
                for c in range(NCH):
                    s_ = yst[c % 2]
                    for hf in range(2):
                        pb = ps_next()
                        for f4 in range(4):
                            fc = hf * 4 + f4
                            transpose(pb.c(f4 * 128, 128), x32.c(fc * NTMAX + c * 128, 128), 128, kx32(c * 128, fc), pb.k())
                        evac_copy(s_.c(hf * 512, 512), pb.c(0, 512), pb.k(), s_.k())
                    if c < 8:
                        dst = DAP(yp_d, (Sidx * 1024 + c * 128) * 1024, [[1024, 128], [1, 1024]])
                    else:
                        dst = ys_d.ap()
                    P.dma("sp", dst, s_.c(0, 1024), reads=s_.k(), dkey=("yout", c % 2))
                scr_top[0] = mark

        try:
            main_body()
        except _Stop:
            pass

        P.finalize(st)
        with nc.Block() as block:
            P.emit(block)
    return nc


def _consts():
    p = np.arange(128)
    c = {}
    c["ident"] = np.eye(128, dtype=np.float32)
    c["tril"] = (p[:, None] <= p[None, :]).astype(np.float32)
    c["sblk"] = ((p[:, None] <= p[None, :]) & (p[:, None] // 8 == p[None, :] // 8)).astype(np.float32)
    c["hm4"] = (p[:, None] // 32 == np.arange(4)[None, :]).astype(np.float32)
    c["bm"] = (p[:, None] // 32 == (np.arange(256)[None, :] // 64)).astype(np.float32)
    rm = np.ones((128, 1152), np.float32)
    rm[:, 0:1024:128] = 0.0
    rm[:, 1024:1152:8] = 0.0
    c["rmask"] = rm.astype(ml_dtypes.bfloat16)
    sq = (np.arange(128)[None, :] // 8 == np.arange(16)[:, None]).astype(np.float32)
    c["seqm"] = np.broadcast_to(sq.reshape(1, 2048), (128, 2048)).astype(ml_dtypes.bfloat16)
    c["tokm"] = (p[:, None] // 8 == np.arange(16)[None, :]).astype(np.float32)
    return c


def _cvec(inp, l):
    def fm(v, nchunk):
        return np.asarray(v, np.float32).reshape(nchunk, 128).T
    cols = []
    aw = np.asarray(inp["a_conv_w"][l], np.float32)
    for j in range(2):
        for k in range(3):
            cols.append(aw[k, j * 128:(j + 1) * 128][:, None])
    cols.append(np.asarray(inp["b_gate_b"][l], np.float32)[:, None])
    cols.append(fm(inp["b_norm_g"][l], 2))
    cw = np.asarray(inp["c_conv_w"][l], np.float32)
    for j in range(2):
        for k in range(31):
            cols.append(cw[k, j * 128:(j + 1) * 128][:, None])
    cols.append(fm(inp["c_conv_b"][l], 2))
    cols.append(fm(inp["c_ln_g"][l], 2))
    cols.append(fm(inp["c_ln_b"][l], 2))
    cols.append(fm(inp["ln1_g"][l], 8))
    cols.append(fm(inp["ln1_b"][l], 8))
    cols.append(fm(inp["ln2_g"][l], 8))
    cols.append(fm(inp["ln2_b"][l], 8))
    out = np.concatenate(cols, axis=1)
    assert out.shape == (128, NCV), out.shape
    return out


def _dbt(inp):
    bs = np.asarray(inp["d_bs"], np.float32)
    out = np.zeros((2, 2, 2, 128, 128), np.float32)
    for l in range(2):
        for j in range(2):
            for gg in range(2):
                out[l, 0, j, 64 * gg:64 * gg + 64, :] = bs[l, 2 * j + gg][None, :]
                out[l, 1, j, 64 * gg:64 * gg + 64, :] = np.tile(bs[l, 2 * j + gg, :8], 16)[None, :]
    return np.ascontiguousarray(out.reshape(8, 128, 128))


def _dwst(inp):
    ws = np.asarray(inp["d_ws"], np.float32)
    out = np.zeros((2, 2, 4, 128, 128), np.float32)
    out[:, 0] = ws.transpose(0, 1, 3, 2)
    blk = ws[:, :, :8, :8].transpose(0, 1, 3, 2)
    for q in range(16):
        out[:, 1, :, 8 * q:8 * q + 8, 8 * q:8 * q + 8] = blk
    return np.ascontiguousarray(out.reshape(16, 128, 128))


_NC_CACHE = {}


def make_in_maps(inp):
    consts = _consts()
    f32 = lambda a: np.ascontiguousarray(a, dtype=np.float32)
    shared = {
        "w_in": f32(inp["w_in"]), "w_out": f32(inp["w_out"]), "w_ff1": f32(inp["w_ff1"]), "w_ff2": f32(inp["w_ff2"]),
        "cvec": f32(np.stack([_cvec(inp, l) for l in range(2)])),
        "gw2": f32(inp["b_gate_w2"]),
        "dln": f32(np.stack([inp["d_ln_g"], inp["d_ln_b"]], axis=1)),
        "dws": f32(inp["d_ws"]), "dbs": f32(inp["d_bs"]),
        "dbt": _dbt(inp), "dwst": _dwst(inp),
    }
    shared.update(consts)
    in_maps = []
    for i in range(8):
        m = dict(shared)
        m["xp"] = f32(inp["x_prompt"][i])
        m["xs"] = f32(inp["x_sample"][16 * i:16 * i + 16].reshape(128, 1024))
        m["sta"] = f32(inp["state_conv_a"][:, 16 * i:16 * i + 16].reshape(2, 32, 256))
        m["stg"] = f32(inp["state_gla"][:, 16 * i:16 * i + 16].reshape(2, 16, 8192))
        m["stc"] = f32(inp["state_conv_c"][:, 16 * i:16 * i + 16].reshape(2, 480, 256))
        in_maps.append(m)
    return in_maps


def kernel(**inp):
    inp = {k: np.asarray(v) for k, v in inp.items()}
    if "nc" not in _NC_CACHE:
        _NC_CACHE["nc"] = build_program()
    nc = _NC_CACHE["nc"]
    in_maps = make_in_maps(inp)
    res = run_bass_kernel_spmd(nc, in_maps, core_ids=list(range(8)))
    R = res.results
    y_prompt = np.stack([R[i]["yp"] for i in range(8)]).reshape(8, 2048, 1024)
    y_sample = np.concatenate([R[i]["ys"].reshape(16, 8, 1024) for i in range(8)], axis=0)
    na_p = np.stack([R[i]["nap"] for i in range(8)], axis=1).reshape(2, 8, 2, 256)
    na_s = np.concatenate([R[i]["nas"].reshape(2, 16, 2, 256) for i in range(8)], axis=1)
    ng_p = np.stack([R[i]["ngp"] for i in range(8)], axis=1).reshape(2, 8, 4, 32, 64)
    ng_s = np.concatenate([R[i]["ngs"].reshape(2, 16, 4, 32, 64) for i in range(8)], axis=1)
    nc_p = np.stack([R[i]["ncp"] for i in range(8)], axis=1).reshape(2, 8, 30, 256)
    nc_s = np.concatenate([R[i]["ncs"].reshape(2, 16, 30, 256) for i in range(8)], axis=1)
    nv_s = np.concatenate([R[i]["nvs"].reshape(2, 16, 8, 256) for i in range(8)], axis=1)
    outs = (y_prompt, y_sample, na_p, na_s, ng_p, ng_s, nc_p, nc_s, nv_s)
    return tuple(np.ascontiguousarray(o, dtype=np.float32) for o in outs)
```
